# Optimizing a Trainium2 kernel written in Bass

```python
import math
import jax, jax.numpy as jnp
from jax import lax
import numpy as np

D_MODEL = 1024
BATCH = 1
SEQ = 16384
DEPTH = 2
DEC_BATCH = 32
DEC_SEQ = 16
PAST_LEN = 2048

CHUNK = 64
QBLK = 128
EPS = 1e-5
NEG_BIG = -1e30

S5_WIDTH = D_MODEL // 4
S5_GROUP = 16
S5_GROUPS = S5_WIDTH // S5_GROUP
S5_STATE = 64
S5_DT_MIN = 1e-3
S5_DT_MAX = 1e-1

HG_HEADS = 4
HG_DK = 64
HG_DV = 64
HG_WIDTH = HG_HEADS * HG_DV

MLA_HEADS = 8
MLA_Q_RANK = 256
MLA_KV_RANK = 128
MLA_NOPE = 64
MLA_ROPE = 32
MLA_V = 64
MLA_WIDTH = MLA_HEADS * MLA_V
ROPE_THETA = 10000.0

MIX_WIDTH = S5_WIDTH + HG_WIDTH + MLA_WIDTH
D_FF = 4 * D_MODEL

IN_SPLITS = (S5_WIDTH, HG_HEADS * HG_DK, HG_HEADS * HG_DK, HG_WIDTH, HG_WIDTH, MLA_Q_RANK, MLA_KV_RANK, MLA_ROPE)
IN_WIDTH = S5_WIDTH + 2 * HG_HEADS * HG_DK + 2 * HG_WIDTH + MLA_Q_RANK + MLA_KV_RANK + MLA_ROPE

kernel_name = 'hybrid_s5_hgrn2_mla_stream_step'


def rms_norm(x, g):
    xf = x.astype(jnp.float32)
    y = xf * lax.rsqrt(jnp.mean(xf * xf, axis=-1, keepdims=True) + EPS)
    return (y * g.astype(jnp.float32)).astype(x.dtype)


def cmul(ar, ai, br, bi):
    return ar * br - ai * bi, ar * bi + ai * br


def split_columns(proj):
    outs, start = [], 0
    for width in IN_SPLITS:
        outs.append(proj[..., start:start + width])
        start += width
    return outs


def s5_mixer(u, h0_re, h0_im, lam_re, lam_im, log_dt, b_re, b_im, c_re, c_im, d, w_glu, b_glu):
    f32 = jnp.float32
    bsz, L, _ = u.shape
    uf = u.astype(f32).reshape(bsz, L, S5_GROUPS, S5_GROUP)
    lam_re = lam_re.astype(f32)
    lam_im = lam_im.astype(f32)
    dt = jnp.exp(log_dt.astype(f32))[:, None]
    mag = jnp.exp(lam_re * dt)
    ab_re, ab_im = mag * jnp.cos(lam_im * dt), mag * jnp.sin(lam_im * dt)
    den = lam_re * lam_re + lam_im * lam_im
    z_re, z_im = cmul(ab_re - 1.0, ab_im, lam_re / den, -lam_im / den)
    bb_re, bb_im = cmul(z_re[..., None], z_im[..., None], b_re.astype(f32), b_im.astype(f32))
    bu_re = jnp.einsum('blgh,gph->blgp', uf, bb_re)
    bu_im = jnp.einsum('blgh,gph->blgp', uf, bb_im)
    s_re, s_im = cmul(ab_re, ab_im, h0_re.astype(f32), h0_im.astype(f32))
    bu_re = bu_re.at[:, 0].add(s_re)
    bu_im = bu_im.at[:, 0].add(s_im)
    a_re = jnp.broadcast_to(ab_re, bu_re.shape)
    a_im = jnp.broadcast_to(ab_im, bu_im.shape)

    def combine(e1, e2):
        a1r, a1i, b1r, b1i = e1
        a2r, a2i, b2r, b2i = e2
        ar, ai = cmul(a2r, a2i, a1r, a1i)
        br, bi = cmul(a2r, a2i, b1r, b1i)
        return ar, ai, br + b2r, bi + b2i

    _, _, h_re, h_im = lax.associative_scan(combine, (a_re, a_im, bu_re, bu_im), axis=1)
    y = (jnp.einsum('blgp,ghp->blgh', h_re, c_re.astype(f32))
         - jnp.einsum('blgp,ghp->blgh', h_im, c_im.astype(f32)))
    y = y.reshape(bsz, L, S5_WIDTH) + d.astype(f32) * u.astype(f32)
    z = jax.nn.gelu(y)
    out = z * jax.nn.sigmoid(z @ w_glu.astype(f32) + b_glu.astype(f32))
    return out.astype(u.dtype), h_re[:, -1], h_im[:, -1]


def hgrn2_mixer(q, f_pre, i, s0, lb):
    f32 = jnp.float32
    bsz, L, _ = q.shape
    C = min(CHUNK, L)
    N = L // C
    lbf = lb.astype(f32)
    fp = f_pre.astype(f32)
    f = lbf + (1.0 - lbf) * jax.nn.sigmoid(fp)
    logf = jnp.log(f)
    k = (1.0 - lbf) * jax.nn.sigmoid(-fp)
    qf = jax.nn.silu(q.astype(f32))
    v = i.astype(f32)

    def blocks(t, dim):
        return t.reshape(bsz, N, C, HG_HEADS, dim).transpose(1, 0, 3, 2, 4)

    mask = jnp.tril(jnp.ones((C, C), dtype=bool))[:, :, None]

    def step(S, blk):
        qb, kb, vb, gb = blk
        b = jnp.cumsum(gb, axis=2)
        inter = jnp.einsum('bhtd,bhdv->bhtv', qb * jnp.exp(b), S)
        diff = b[:, :, :, None, :] - b[:, :, None, :, :]
        decay = jnp.where(mask, jnp.exp(jnp.where(mask, diff, 0.0)), 0.0)
        att = jnp.einsum('bhtd,bhsd,bhtsd->bhts', qb, kb, decay)
        intra = jnp.einsum('bhts,bhsv->bhtv', att, vb)
        b_last = b[:, :, -1:, :]
        S_new = (jnp.exp(b_last[:, :, 0, :])[..., None] * S
                 + jnp.einsum('bhsd,bhsv->bhdv', kb * jnp.exp(b_last - b), vb))
        return S_new, inter + intra

    S, o = lax.scan(step, s0.astype(f32),
                    (blocks(qf, HG_DK), blocks(k, HG_DK), blocks(v, HG_DV), blocks(logf, HG_DK)))
    o = o.transpose(1, 0, 3, 2, 4).reshape(bsz, L, HG_WIDTH)
    return o, S


def rope(x, pos):
    half = MLA_ROPE // 2
    inv = ROPE_THETA ** (-jnp.arange(half, dtype=jnp.float32) / half)
    ang = pos.astype(jnp.float32)[:, None] * inv[None, :]
    ang = ang.reshape((pos.shape[0],) + (1,) * (x.ndim - 3) + (half,))
    cos, sin = jnp.cos(ang), jnp.sin(ang)
    xf = x.astype(jnp.float32)
    x1, x2 = xf[..., :half], xf[..., half:]
    return jnp.concatenate([x1 * cos - x2 * sin, x2 * cos + x1 * sin], axis=-1).astype(x.dtype)


def mla_project(cq, ckv, kpe, pos, q_norm_g, w_uq, kv_norm_g, w_uk):
    bsz, L, _ = cq.shape
    q = (rms_norm(cq, q_norm_g) @ w_uq).reshape(bsz, L, MLA_HEADS, MLA_NOPE + MLA_ROPE)
    q_nope = q[..., :MLA_NOPE]
    q_pe = rope(q[..., MLA_NOPE:], pos)
    c_kv = rms_norm(ckv, kv_norm_g)
    k_pe = rope(kpe, pos)
    q_lat = jnp.einsum('blhd,chd->blhc', q_nope, w_uk)
    return q_lat, q_pe, c_kv, k_pe


def mla_attend(q_lat, q_pe, q_pos, c_kv, k_pe, k_pos, w_uv):
    bsz, Lq = q_lat.shape[:2]
    qb = min(QBLK, Lq)
    nb = Lq // qb
    scale = (MLA_NOPE + MLA_ROPE) ** -0.5
    k_chunk = k_pos // CHUNK

    def block(args):
        ql, qp, qpos = args
        s = (jnp.einsum('bqhc,bkc->bhqk', ql, c_kv)
             + jnp.einsum('bqhr,bkr->bhqk', qp, k_pe)).astype(jnp.float32) * scale
        mask = k_chunk[None, :] <= (qpos // CHUNK)[:, None]
        p = jax.nn.softmax(jnp.where(mask, s, NEG_BIG), axis=-1)
        return jnp.einsum('bhqk,bkc->bqhc', p.astype(c_kv.dtype), c_kv)

    def split(t):
        return t.reshape((bsz, nb, qb) + t.shape[2:]).swapaxes(0, 1)

    o_lat = lax.map(block, (split(q_lat), split(q_pe), q_pos.reshape(nb, qb)))
    o_lat = o_lat.swapaxes(0, 1).reshape(bsz, Lq, MLA_HEADS, MLA_KV_RANK)
    o = jnp.einsum('blhc,chv->blhv', o_lat, w_uv)
    return o.reshape(bsz, Lq, MLA_WIDTH)


def trunk_layer(x, q_pos, k_pos, s5_h0_re, s5_h0_im, hg_s0, kv_past, pe_past, lb, w):
    h = rms_norm(x, w['norm1_g'])
    u, hq, hf, hi, hg, cq, ckv, kpe = split_columns(h @ w['w_in'])
    s5_y, s5_re, s5_im = s5_mixer(u, s5_h0_re, s5_h0_im, w['s5_lambda_re'], w['s5_lambda_im'], w['s5_log_dt'],
                                  w['s5_b_re'], w['s5_b_im'], w['s5_c_re'], w['s5_c_im'], w['s5_d'],
                                  w['s5_w_glu'], w['s5_b_glu'])
    hg_o, hg_s = hgrn2_mixer(hq, hf, hi, hg_s0, lb)
    q_lat, q_pe, c_kv, k_pe = mla_project(cq, ckv, kpe, q_pos, w['mla_q_norm_g'], w['mla_w_uq'],
                                          w['mla_kv_norm_g'], w['mla_w_uk'])
    if kv_past is None:
        kv_all, pe_all = c_kv, k_pe
    else:
        kv_all = jnp.concatenate([kv_past.astype(c_kv.dtype), c_kv], axis=1)
        pe_all = jnp.concatenate([pe_past.astype(k_pe.dtype), k_pe], axis=1)
    mla_o = mla_attend(q_lat, q_pe, q_pos, kv_all, pe_all, k_pos, w['mla_w_uv'])
    g = w['out_norm_g']
    a0, a1 = S5_WIDTH, S5_WIDTH + HG_WIDTH
    mixed = jnp.concatenate([
        rms_norm(s5_y, g[:a0]),
        rms_norm(hg_o, g[a0:a1]) * jax.nn.silu(hg.astype(jnp.float32)),
        rms_norm(mla_o, g[a1:]),
    ], axis=-1).astype(x.dtype)
    x = x + mixed @ w['w_out']
    h2 = rms_norm(x, w['norm2_g'])
    x = x + jnp.square(jax.nn.relu(h2 @ w['w_up'])) @ w['w_down']
    return x, c_kv, k_pe, hg_s, s5_re, s5_im


def setup_inputs(seed: int = 0) -> dict:
    key = jax.random.key(seed)
    ks = iter(jax.random.split(key, 40))

    def nrm(shape, scale):
        return scale * jax.random.normal(next(ks), shape, jnp.float32)

    def gain(shape):
        return 1.0 + 0.01 * jax.random.normal(next(ks), shape, jnp.float32)

    L = DEPTH
    n_idx = jnp.arange(S5_STATE, dtype=jnp.float32)
    return {
        'x_prompt': nrm((BATCH, SEQ, D_MODEL), 1.0),
        'x_sample': nrm((DEC_BATCH, DEC_SEQ, D_MODEL), 1.0),
        'cache_mla_kv': nrm((L, DEC_BATCH, PAST_LEN, MLA_KV_RANK), 1.0),
        'cache_mla_pe': nrm((L, DEC_BATCH, PAST_LEN, MLA_ROPE), 1.0),
        'state_hgrn': nrm((L, DEC_BATCH, HG_HEADS, HG_DK, HG_DV), 0.5),
        'state_s5_re': nrm((L, DEC_BATCH, S5_GROUPS, S5_STATE), 0.05),
        'state_s5_im': nrm((L, DEC_BATCH, S5_GROUPS, S5_STATE), 0.05),
        'norm1_g': gain((L, D_MODEL)),
        'w_in': nrm((L, D_MODEL, IN_WIDTH), D_MODEL ** -0.5),
        's5_lambda_re': -0.5 + nrm((L, S5_GROUPS, S5_STATE), 0.01),
        's5_lambda_im': math.pi * n_idx + nrm((L, S5_GROUPS, S5_STATE), 0.01),
        's5_log_dt': jax.random.uniform(next(ks), (L, S5_GROUPS), jnp.float32,
                                        math.log(S5_DT_MIN), math.log(S5_DT_MAX)),
        's5_b_re': nrm((L, S5_GROUPS, S5_STATE, S5_GROUP), S5_GROUP ** -0.5),
        's5_b_im': nrm((L, S5_GROUPS, S5_STATE, S5_GROUP), S5_GROUP ** -0.5),
        's5_c_re': nrm((L, S5_GROUPS, S5_GROUP, S5_STATE), S5_STATE ** -0.5),
        's5_c_im': nrm((L, S5_GROUPS, S5_GROUP, S5_STATE), S5_STATE ** -0.5),
        's5_d': nrm((L, S5_WIDTH), 1.0),
        's5_w_glu': nrm((L, S5_WIDTH, S5_WIDTH), S5_WIDTH ** -0.5),
        's5_b_glu': nrm((L, S5_WIDTH), 0.01),
        'hgrn_lb_logits': nrm((L, HG_HEADS * HG_DK), 0.5),
        'mla_q_norm_g': gain((L, MLA_Q_RANK)),
        'mla_w_uq': nrm((L, MLA_Q_RANK, MLA_HEADS * (MLA_NOPE + MLA_ROPE)), MLA_Q_RANK ** -0.5),
        'mla_kv_norm_g': gain((L, MLA_KV_RANK)),
        'mla_w_uk': nrm((L, MLA_KV_RANK, MLA_HEADS, MLA_NOPE), MLA_KV_RANK ** -0.5),
        'mla_w_uv': nrm((L, MLA_KV_RANK, MLA_HEADS, MLA_V), MLA_KV_RANK ** -0.5),
        'out_norm_g': gain((L, MIX_WIDTH)),
        'w_out': nrm((L, MIX_WIDTH, D_MODEL), MIX_WIDTH ** -0.5),
        'norm2_g': gain((L, D_MODEL)),
        'w_up': nrm((L, D_MODEL, D_FF), D_MODEL ** -0.5),
        'w_down': nrm((L, D_FF, D_MODEL), D_FF ** -0.5),
        'final_norm_g': gain((D_MODEL,)),
    }


def reference(x_prompt, x_sample, cache_mla_kv, cache_mla_pe, state_hgrn, state_s5_re, state_s5_im,
              norm1_g, w_in, s5_lambda_re, s5_lambda_im, s5_log_dt, s5_b_re, s5_b_im, s5_c_re, s5_c_im,
              s5_d, s5_w_glu, s5_b_glu, hgrn_lb_logits, mla_q_norm_g, mla_w_uq, mla_kv_norm_g, mla_w_uk,
              mla_w_uv, out_norm_g, w_out, norm2_g, w_up, w_down, final_norm_g):
    bp, lp = x_prompt.shape[0], x_prompt.shape[1]
    ls = x_sample.shape[1]
    past = cache_mla_kv.shape[2]
    pos_p = jnp.arange(lp, dtype=jnp.int32)
    pos_s = past + jnp.arange(ls, dtype=jnp.int32)
    kpos_s = jnp.arange(past + ls, dtype=jnp.int32)
    lb_p = jax.nn.softmax(hgrn_lb_logits.astype(jnp.float32), axis=0)
    lb_all = jnp.cumsum(lb_p, axis=0) - lb_p[0]
    zero_s5 = jnp.zeros((bp, S5_GROUPS, S5_STATE), jnp.float32)
    zero_hg = jnp.zeros((bp, HG_HEADS, HG_DK, HG_DV), jnp.float32)

    xp, xs = x_prompt, x_sample
    p_kv, p_pe, p_hg, p_re, p_im = [], [], [], [], []
    s_kv, s_pe, s_hg, s_re, s_im = [], [], [], [], []
    for l in range(DEPTH):
        w = {
            'norm1_g': norm1_g[l], 'w_in': w_in[l],
            's5_lambda_re': s5_lambda_re[l], 's5_lambda_im': s5_lambda_im[l], 's5_log_dt': s5_log_dt[l],
            's5_b_re': s5_b_re[l], 's5_b_im': s5_b_im[l], 's5_c_re': s5_c_re[l], 's5_c_im': s5_c_im[l],
            's5_d': s5_d[l], 's5_w_glu': s5_w_glu[l], 's5_b_glu': s5_b_glu[l],
            'mla_q_norm_g': mla_q_norm_g[l], 'mla_w_uq': mla_w_uq[l], 'mla_kv_norm_g': mla_kv_norm_g[l],
            'mla_w_uk': mla_w_uk[l], 'mla_w_uv': mla_w_uv[l],
            'out_norm_g': out_norm_g[l], 'w_out': w_out[l], 'norm2_g': norm2_g[l],
            'w_up': w_up[l], 'w_down': w_down[l],
        }
        xp, a, b, c, d, e = trunk_layer(xp, pos_p, pos_p, zero_s5, zero_s5, zero_hg, None, None, lb_all[l], w)
        p_kv.append(a); p_pe.append(b); p_hg.append(c); p_re.append(d); p_im.append(e)
        xs, a, b, c, d, e = trunk_layer(xs, pos_s, kpos_s, state_s5_re[l], state_s5_im[l], state_hgrn[l],
                                        cache_mla_kv[l], cache_mla_pe[l], lb_all[l], w)
        s_kv.append(a); s_pe.append(b); s_hg.append(c); s_re.append(d); s_im.append(e)

    y_prompt = rms_norm(xp, final_norm_g)
    y_sample = rms_norm(xs, final_norm_g)
    return (y_prompt, y_sample,
            jnp.stack(p_kv), jnp.stack(p_pe), jnp.stack(p_hg), jnp.stack(p_re), jnp.stack(p_im),
            jnp.stack(s_kv), jnp.stack(s_pe), jnp.stack(s_hg), jnp.stack(s_re), jnp.stack(s_im))
```

```python
import os
from contextlib import ExitStack
import numpy as np
import concourse.bass as bass
import concourse.mybir as mybir
from concourse.bass_utils import run_bass_kernel_spmd

F32 = mybir.dt.float32
BF = mybir.dt.bfloat16
AF = mybir.ActivationFunctionType
OP = mybir.AluOpType
AX = mybir.AxisListType

NCORES = 8
D = 1024
SEQ = 16384
DEPTH = 2
NSEQ = 4
DSEQ = 16
PAST = 2048
INW = 1696
EPS = 1e-5
NDMASEM = 12


class Pg:
    def __init__(self, nc, es):
        self.nc = nc
        self.names = ['pe', 'act', 'dve', 'pool', 'sp']
        self.q = {k: [] for k in self.names}
        self.sem = {k: es.enter_context(nc.semaphore("sem_" + k)) for k in self.names}
        self.cnt = {k: 0 for k in self.names}
        self.dsem = [es.enter_context(nc.semaphore("dsem%d" % i)) for i in range(NDMASEM)]
        self.dcnt = [0] * NDMASEM
        self.dnext = 0
        self.known = {k: {} for k in self.names}
        self.lastw = {}
        self.readers = {}
        self.ninstr = 0

    def _semh(self, key):
        return self.sem[key] if isinstance(key, str) else self.dsem[key[1]]

    def _need(self, e, waits, ev):
        if ev is None:
            return
        k, v = ev
        if k == e and e == 'pe':
            return
        if self.known[e].get(k, 0) >= v:
            return
        if waits.get(k, 0) < v:
            waits[k] = v

    def _emit_waits(self, e, waits):
        for k, v in waits.items():
            h = self._semh(k)
            self.q[e].append(lambda eng, h=h, v=v: eng.wait_ge(h, v))
            self.known[e][k] = v
            self.ninstr += 1

    def _deps(self, e, reads, writes):
        waits = {}
        for k in reads:
            self._need(e, waits, self.lastw.get(k))
        for k in writes:
            self._need(e, waits, self.lastw.get(k))
            for ev in self.readers.get(k, {}).values():
                self._need(e, waits, ev)
        self._emit_waits(e, waits)

    def _record(self, e, ev, reads, writes):
        for k in reads:
            self.readers.setdefault(k, {})[e] = ev
        for k in writes:
            self.lastw[k] = ev
            self.readers[k] = {}

    def op(self, e, meth, reads=(), writes=(), signal=True, **kw):
        self._deps(e, reads, writes)
        if signal:
            self.cnt[e] += 1
            ev = (e, self.cnt[e])
            h = self.sem[e]
            self.q[e].append(lambda eng, meth=meth, kw=kw, h=h: getattr(eng, meth)(**kw).then_inc(h, 1))
        else:
            ev = (e, self.cnt[e] + 1)
            self.q[e].append(lambda eng, meth=meth, kw=kw: getattr(eng, meth)(**kw))
        self.ninstr += 1
        self._record(e, ev, reads, writes)

    def dma(self, e, out, in_, reads=(), writes=()):
        i = self.dnext
        self.dnext = (self.dnext + 1) % NDMASEM
        waits = {}
        if self.dcnt[i] > 0:
            self._need(e, waits, (('d', i), self.dcnt[i]))
        self._emit_waits(e, waits)
        self._deps(e, reads, writes)
        self.dcnt[i] += 16
        ev = (('d', i), self.dcnt[i])
        h = self.dsem[i]
        self.q[e].append(lambda eng, out=out, in_=in_, h=h: eng.dma_start(out=out, in_=in_).then_inc(h, 16))
        self.ninstr += 1
        self._record(e, ev, reads, writes)

    def barrier(self):
        for e in self.names:
            waits = {}
            for o in self.names:
                if o != e and self.cnt[o] > 0:
                    self._need(e, waits, (o, self.cnt[o]))
            for i in range(NDMASEM):
                if self.dcnt[i] > 0:
                    self._need(e, waits, (('d', i), self.dcnt[i]))
            self._emit_waits(e, waits)

    def finish(self):
        waits = {}
        for i in range(NDMASEM):
            if self.dcnt[i] > 0:
                self._need('sp', waits, (('d', i), self.dcnt[i]))
        self._emit_waits('sp', waits)

    def replay(self):
        nc = self.nc
        with nc.Block() as block:
            @block.tensor
            def _(eng):
                for c in self.q['pe']:
                    c(eng)

            @block.scalar
            def _(eng):
                for c in self.q['act']:
                    c(eng)

            @block.vector
            def _(eng):
                for c in self.q['dve']:
                    c(eng)

            @block.gpsimd
            def _(eng):
                for c in self.q['pool']:
                    c(eng)

            @block.sync
            def _(eng):
                for c in self.q['sp']:
                    c(eng)


def build(nblk_limit=None, stop_after=None):
    SKIP = os.environ.get('K_SKIP', '')
    nc = bass.Bass("TRN2", target_bir_lowering=False)
    es = ExitStack()
    pg = Pg(nc, es)

    def din(name, shape, dt=F32):
        return nc.dram_tensor(name, list(shape), dt, kind="ExternalInput")

    def dout(name, shape, dt=F32):
        return nc.dram_tensor(name, list(shape), dt, kind="ExternalOutput")

    PERSIST = {"ident_f", "ident_b", "ones_bf", "ones_f", "sqb", "rbc", "ong_sb", "ss", "rstd"}
    ARW = 51800
    arenaP = es.enter_context(nc.sbuf_tensor("arenaP", [128, 1300], F32))
    arenaW = es.enter_context(nc.sbuf_tensor("arenaW", [128, ARW], F32))
    cur = {'P': 0, 'W': 0}

    def sb(name, shape, dt=F32):
        which = 'P' if name in PERSIST else 'W'
        ar = arenaP if which == 'P' else arenaW
        esz = 2 if dt == BF else 4
        nel = int(np.prod(shape[1:]))
        nw = (nel * esz + 3) // 4
        off = cur[which]
        cur[which] = off + nw
        assert cur[which] <= (1300 if which == 'P' else ARW), (name, cur)
        ap = ar[0:shape[0], off:off + nw]
        if dt != F32:
            ap = ap.bitcast(dt)
            if esz == 2 and nel % 2 == 1:
                ap = ap[:, 0:nel]
        if len(shape) == 3:
            ap = ap.rearrange("p (a b) -> p a b", a=shape[1])
        elif len(shape) == 4:
            ap = ap.rearrange("p (a b c) -> p a b c", a=shape[1], b=shape[2])
        elif len(shape) == 5:
            ap = ap.rearrange("p (a b c d) -> p a b c d", a=shape[1], b=shape[2], c=shape[3])
        return ap

    def region():
        cur['W'] = 0

    def ps(name, shape, dt=F32):
        return es.enter_context(nc.psum_tensor(name, list(shape), dt))

    def V(meth, R, W, **kw):
        pg.op('dve', meth, R, W, **kw)

    def A(meth, R, W, **kw):
        pg.op('act', meth, R, W, **kw)

    def G(meth, R, W, **kw):
        pg.op('pool', meth, R, W, **kw)

    def T(meth, R, W, signal=True, **kw):
        pg.op('pe', meth, R, W, signal=signal, **kw)

    TALL = SEQ + NSEQ * DSEQ
    NS1 = 1 + NSEQ
    xin = din("xin", [TALL, D])
    ropec = din("ropec", [TALL, 16])
    ropes = din("ropes", [TALL, 16])
    g1B = din("g1B", [DEPTH, 128, D])
    w_in = din("w_in", [DEPTH, D, INW])
    kvgB = din("kvgB", [DEPTH, 128, 128])
    s5p = din("s5p", [DEPTH, 128, 8, 3])
    s5b = din("s5b", [DEPTH, 128, 8, 2, 16])
    s5c = din("s5c", [DEPTH, 128, 8, 2, 16])
    s5d = din("s5d", [DEPTH, 128, 2])
    st_s5 = din("st_s5", [DEPTH, 128, 8, 2, NSEQ])
    w_glu = din("w_glu", [DEPTH, 256, 256])
    b_glu = din("b_glu", [DEPTH, 128, 2])
    lbl = din("lbl", [128, DEPTH, 2])
    st_hg = din("st_hg", [DEPTH, NSEQ, 128, 2, 64])
    qng = din("qng", [DEPTH, 128, 2])
    w_uq = din("w_uq", [DEPTH, 256, 1024])
    ong = din("ong", [DEPTH, 128, 8])
    w_out = din("w_out", [DEPTH, D, D])
    w_ukT = din("w_ukT", [DEPTH, 128, 8, 128])
    w_uv = din("w_uv", [DEPTH, 128, 8, 64])
    cache_kv = din("cache_kv", [DEPTH, NSEQ, PAST, 128])
    cache_pe = din("cache_pe", [DEPTH, NSEQ, PAST, 32])
    g2B = din("g2B", [DEPTH, 128, D])
    gfB = din("gfB", [128, D])
    w_up = din("w_up", [DEPTH, D, 4 * D])
    w_down = din("w_down", [DEPTH, 4 * D, D])
    o_hgp = dout("o_hgp", [DEPTH, 128, 2, 64])
    o_hgs = dout("o_hgs", [DEPTH, NSEQ, 128, 2, 64])
    qT_d = nc.dram_tensor("qT_d", [8, 128, TALL], BF)
    kT_d = nc.dram_tensor("kT_d", [128, TALL], BF)
    peT_d = nc.dram_tensor("peT_d", [32, TALL], BF)
    vtok_d = nc.dram_tensor("vtok_d", [TALL, 128], BF)
    mixT_d = nc.dram_tensor("mixT_d", [4, 128, TALL], BF)
    xmid_d = nc.dram_tensor("xmid_d", [TALL, D], F32)
    xout_d = nc.dram_tensor("xout_d", [TALL, D], F32)
    o_y = dout("o_y", [TALL, D])
    o_kv = dout("o_kv", [DEPTH, TALL, 128])
    o_pe = dout("o_pe", [DEPTH, TALL, 32])
    o_s5p = dout("o_s5p", [DEPTH, 128, 8, 2, 1])
    o_s5s = dout("o_s5s", [DEPTH, 128, 8, 2, NSEQ])

    ident_f = sb("ident_f", [128, 128], F32)
    ident_b = sb("ident_b", [128, 128], BF)
    G('memset', [], ['ident_f'], ap=ident_f[:], constant=0.0)
    G('affine_select', ['ident_f'], ['ident_f'], out=ident_f[:], in_=ident_f[:], pattern=[[-1, 128]],
      compare_op=OP.not_equal, fill=1.0, base=0, channel_multiplier=1)
    V('tensor_copy', ['ident_f'], ['ident_b'], out=ident_b[:], in_=ident_f[:])

    NPS = 6
    psf = [ps("psf%d" % i, [128, 512], F32) for i in range(NPS)]
    psb = [ps("psb%d" % i, [128, 1024], BF) for i in range(2)]
    rr = {'f': 0, 'b': 0}

    psallow = [list(range(NPS))]

    def getps():
        al = psallow[0]
        rr['f'] = (rr['f'] + 1) % len(al)
        i = al[rr['f']]
        return psf[i], ('psf', i)

    def getpsb():
        i = rr['b']
        rr['b'] = (i + 1) % 2
        return psb[i], ('psb', i)

    blocks = [(b * 512, 128, 4) for b in range(SEQ // 512)] + [(SEQ, 64, 1)]
    if nblk_limit is not None:
        blocks = blocks[:nblk_limit] + blocks[-1:]

    xblk = sb("xblk", [128, 4, D], F32)
    hbf = sb("hbf", [128, 4, D], BF)
    hT = sb("hT", [128, 8, 512], BF)
    junkf = sb("junkf", [128, D], F32)
    ss = sb("ss", [128, 8], F32)
    rstd = sb("rstd", [128, 8], F32)
    g1sb = sb("g1sb", [128, D], F32)
    kvg = sb("kvg", [128, 128], F32)
    w_in_bf = sb("w_in_bf", [128, 8, INW], BF)
    wstage = sb("wstage", [128, 2048], F32)
    kvtm = sb("kvtm", [128, 4, 160], F32)
    kvn = sb("kvn", [128, 4, 128], F32)
    pen = sb("pen", [128, 4, 32], F32)
    rc = sb("rc", [128, 4, 16], F32)
    rs = sb("rs", [128, 4, 16], F32)
    tmp16 = sb("tmp16", [128, 4, 16], F32)
    tmp16b = sb("tmp16b", [128, 4, 16], F32)
    projT = sb("projT", [128, 10, 512], F32)
    uT_bf = sb("uT_bf", [128, 2, 512], BF)
    s5p_sb = sb("s5p_sb", [128, 8, 3], F32)
    s5b_sb = sb("s5b_sb", [128, 8, 2, 16], F32)
    s5c_sb = sb("s5c_sb", [128, 8, 2, 16], F32)
    s5d_sb = sb("s5d_sb", [128, 2], F32)
    sm = sb("sm", [128, 24, 8], F32)
    smi = sb("smi", [128, 8], mybir.dt.int32)
    bb = sb("bb", [128, 8, 2, 16], F32)
    Bm = sb("Bm", [128, 8, 2, 128], BF)
    Cm = sb("Cm", [128, 8, 2, 128], BF)
    Ec = sb("Ec", [128, 8, 128], F32)
    Es = sb("Es", [128, 8, 128], F32)
    rtab = sb("rtab", [128, 8, 128], F32)
    EcS = sb("EcS", [128, 8, 64], F32)
    EsS = sb("EsS", [128, 8, 64], F32)
    rtabS = sb("rtabS", [128, 8, 64], F32)
    abx = sb("abx", [128, 2, 8, NSEQ], F32)
    hcar = sb("hcar", [128, 2, 8, NSEQ], F32)
    hcarS = sb("hcarS", [128, 8, 2, NSEQ], F32)
    hcarP = sb("hcarP", [128, 8, 2, 1], F32)
    ftmp = sb("ftmp", [128, 6, 8, NSEQ], F32)
    bre = sb("bre", [128, 8, 128], F32)
    bim = sb("bim", [128, 8, 128], F32)
    t1 = sb("t1", [128, 8, 128], F32)
    t2 = sb("t2", [128, 8, 128], F32)
    t3 = sb("t3", [128, 8, 128], F32)
    t4 = sb("t4", [128, 8, 128], F32)
    BT = t4
    gre = sb("gre", [128, 8, 128], F32)
    gim = sb("gim", [128, 8, 128], F32)
    hre_bf = sb("hre_bf", [128, 8, 128], BF)
    him_bf = sb("him_bf", [128, 8, 128], BF)
    yT = sb("yT", [128, 2, 512], F32)
    tsm = sb("tsm", [128, 128], F32)

    ones_bf = sb("ones_bf", [128, 128], BF)
    V('memset', [], ['ones_bf'], ap=ones_bf[:], constant=1.0)
    ones_f = sb("ones_f", [128, 128], F32)
    V('memset', [], ['ones_f'], ap=ones_f[:], constant=1.0)
    cm64 = sb("cm64", [128, 512], F32)
    cm16 = sb("cm16", [128, 64], F32)
    cmask4 = sb("cmask4", [128, 4, 64], BF)
    cmaskS = sb("cmaskS", [16, 4, 16], BF)
    w_glu_bf = sb("w_glu_bf", [128, 2, 256], BF)
    b_glu_sb = sb("b_glu_sb", [128, 2], F32)
    lbl_sb = sb("lbl_sb", [128, DEPTH, 2], F32)
    lbw = sb("lbw", [128, 8, 2], F32)
    lb_sb = sb("lb_sb", [128, 2], F32)
    oml_sb = sb("oml_sb", [128, 2], F32)
    qng_sb = sb("qng_sb", [128, 2], F32)
    ong_sb = sb("ong_sb", [128, 8], F32)
    w_uq_bf = sb("w_uq_bf", [128, 2, 1024], BF)
    hgo = sb("hgo", [128, 2, 512], F32)
    vtm_bf = sb("vtm_bf", [128, 4, 256], BF)
    attm = sb("attm", [128, 4, 64], BF)
    kdt = sb("kdt", [128, 256], BF)
    Sst = sb("Sst", [128, 2, 64], F32)
    Sbd = sb("Sbd", [128, 2, 128], BF)
    rbc = sb("rbc", [128, 512], F32)
    sqb = sb("sqb", [128, 512], BF)
    mixA = sb("mixA", [128, 4, 512], BF)
    kvn_bf = sb("kvn_bf", [128, 4, 128], BF)
    pen_bf = sb("pen_bf", [128, 4, 32], BF)
    kT_blk = sb("kT_blk", [128, 512], BF)
    peT_blk = sb("peT_blk", [32, 512], BF)
    tmpo = sb("tmpo", [128, 2, 64], F32)

    def passA_consts():
        V('memset', [], ['cm64'], ap=cm64[:], constant=1.0)
        V('memset', ['cm64'], ['cm64'], ap=cm64[:, 0:512:64], constant=0.0)
        V('memset', [], ['cm16'], ap=cm16[:], constant=1.0)
        V('memset', ['cm16'], ['cm16'], ap=cm16[:, 0:64:16], constant=0.0)
        G('memset', [], ['cmask4'], ap=cmask4[:], constant=1.0)
        for hf in range(2):
            G('affine_select', ['cmask4'], ['cmask4'], out=cmask4[hf * 64:(hf + 1) * 64], in_=cmask4[hf * 64:(hf + 1) * 64],
              pattern=[[0, 4], [1, 64]], compare_op=OP.is_ge, fill=0.0, base=0, channel_multiplier=-1)
        G('memset', [], ['cmaskS'], ap=cmaskS[:], constant=1.0)
        G('affine_select', ['cmaskS'], ['cmaskS'], out=cmaskS[:], in_=cmaskS[:], pattern=[[0, 4], [1, 16]],
          compare_op=OP.is_ge, fill=0.0, base=0, channel_multiplier=-1)

    def load_weight_bf(dst, dstkey, src_ap_fn, ktiles, ncols, stage=None, stagekey='wstage'):
        stage = wstage if stage is None else stage
        n = 0
        for kt in range(ktiles):
            for c0 in range(0, ncols, 1024):
                c1 = min(ncols, c0 + 1024)
                so = (n % 2) * 1024
                n += 1
                pg.dma('sp', stage[:, so:so + c1 - c0], src_ap_fn(kt, c0, c1), writes=[stagekey + str(n % 2)])
                G('tensor_copy', [stagekey + str(n % 2)], [dstkey], out=dst[:, kt, c0:c1], in_=stage[:, so:so + c1 - c0])

    def rms_rstd(ss_ap, rstd_ap, n, keys_r, keys_w):
        V('tensor_scalar', keys_r, keys_w, out=rstd_ap, in0=ss_ap, scalar1=1.0 / n, scalar2=EPS, op0=OP.mult, op1=OP.add)
        A('activation', keys_w, keys_w, out=rstd_ap, in_=rstd_ap, func=AF.Sqrt)
        V('reciprocal', keys_w, keys_w, out=rstd_ap, in_=rstd_ap)

    PI = float(np.pi)
    K5 = ['s5set']

    def s5_setup(l):
        pg.dma('sp', s5p_sb[:], s5p[l], writes=K5)
        pg.dma('sp', s5b_sb[:], s5b[l], writes=K5)
        pg.dma('sp', s5c_sb[:], s5c[l], writes=K5)
        pg.dma('sp', s5d_sb[:], s5d[l], writes=K5)
        lre = s5p_sb[:, :, 0]
        lim = s5p_sb[:, :, 1]
        ldt = s5p_sb[:, :, 2]
        r = lambda i: sm[:, i, :]
        DT, MAG, TH, KF, R0, M1, SIN, COS, ABR, ABI, DEN, LR, LI, ZR, ZI, WC, WS, X1, X2 = range(19)
        A('activation', K5, K5, out=r(DT), in_=ldt, func=AF.Exp)
        V('tensor_tensor', K5, K5, out=r(MAG), in0=lre, in1=r(DT), op=OP.mult)
        A('activation', K5, K5, out=r(MAG), in_=r(MAG), func=AF.Exp)
        V('tensor_tensor', K5, K5, out=r(TH), in0=lim, in1=r(DT), op=OP.mult)
        V('tensor_scalar', K5, K5, out=smi[:], in0=r(TH), scalar1=1.0 / (2 * PI), scalar2=None, op0=OP.mult)
        V('tensor_copy', K5, K5, out=r(KF), in_=smi[:])
        V('scalar_tensor_tensor', K5, K5, out=r(R0), in0=r(KF), scalar=-2 * PI, in1=r(TH), op0=OP.mult, op1=OP.add)
        V('tensor_scalar', K5, K5, out=r(M1), in0=r(R0), scalar1=-PI, scalar2=1e30, op0=OP.add, op1=OP.mult)
        V('tensor_scalar', K5, K5, out=r(M1), in0=r(M1), scalar1=0.0, scalar2=1.0, op0=OP.max, op1=OP.min)
        V('scalar_tensor_tensor', K5, K5, out=r(R0), in0=r(M1), scalar=-2 * PI, in1=r(R0), op0=OP.mult, op1=OP.add)
        A('activation', K5, K5, out=r(SIN), in_=r(R0), func=AF.Sin)
        A('activation', K5, K5, out=r(X1), in_=r(R0), func=AF.Abs)
        V('tensor_scalar', K5, K5, out=r(X1), in0=r(X1), scalar1=-1.0, scalar2=PI / 2, op0=OP.mult, op1=OP.add)
        A('activation', K5, K5, out=r(COS), in_=r(X1), func=AF.Sin)
        V('tensor_tensor', K5, K5, out=r(ABR), in0=r(MAG), in1=r(COS), op=OP.mult)
        V('tensor_tensor', K5, K5, out=r(ABI), in0=r(MAG), in1=r(SIN), op=OP.mult)
        V('tensor_tensor', K5, K5, out=r(DEN), in0=lre, in1=lre, op=OP.mult)
        V('tensor_tensor', K5, K5, out=r(X1), in0=lim, in1=lim, op=OP.mult)
        V('tensor_tensor', K5, K5, out=r(DEN), in0=r(DEN), in1=r(X1), op=OP.add)
        V('reciprocal', K5, K5, out=r(DEN), in_=r(DEN))
        V('tensor_tensor', K5, K5, out=r(LR), in0=lre, in1=r(DEN), op=OP.mult)
        V('scalar_tensor_tensor', K5, K5, out=r(LI), in0=lim, scalar=-1.0, in1=r(DEN), op0=OP.mult, op1=OP.mult)
        V('tensor_scalar', K5, K5, out=r(X1), in0=r(ABR), scalar1=-1.0, scalar2=None, op0=OP.add)
        V('tensor_tensor', K5, K5, out=r(ZR), in0=r(X1), in1=r(LR), op=OP.mult)
        V('tensor_tensor', K5, K5, out=r(X2), in0=r(ABI), in1=r(LI), op=OP.mult)
        V('tensor_tensor', K5, K5, out=r(ZR), in0=r(ZR), in1=r(X2), op=OP.subtract)
        V('tensor_tensor', K5, K5, out=r(ZI), in0=r(X1), in1=r(LI), op=OP.mult)
        V('tensor_tensor', K5, K5, out=r(X2), in0=r(ABI), in1=r(LR), op=OP.mult)
        V('tensor_tensor', K5, K5, out=r(ZI), in0=r(ZI), in1=r(X2), op=OP.add)
        for st in range(8):
            zr = sm[:, ZR, st:st + 1]
            zi = sm[:, ZI, st:st + 1]
            V('tensor_scalar', K5, K5, out=tsm[:, 0:16], in0=s5b_sb[:, st, 1, :], scalar1=zi, scalar2=None, op0=OP.mult)
            V('scalar_tensor_tensor', K5, K5, out=bb[:, st, 0, :], in0=s5b_sb[:, st, 0, :], scalar=zr, in1=tsm[:, 0:16],
              op0=OP.mult, op1=OP.subtract)
            V('tensor_scalar', K5, K5, out=tsm[:, 0:16], in0=s5b_sb[:, st, 0, :], scalar1=zi, scalar2=None, op0=OP.mult)
            V('scalar_tensor_tensor', K5, K5, out=bb[:, st, 1, :], in0=s5b_sb[:, st, 1, :], scalar=zr, in1=tsm[:, 0:16],
              op0=OP.mult, op1=OP.add)
        for ri in range(2):
            V('memset', K5, K5, ap=BT[:], constant=0.0)
            for st in range(8):
                q = st % 4
                V('tensor_copy', K5, K5, out=BT[0:64, st, 32 * q:32 * q + 16], in_=bb[0:64, st, ri, :])
                V('tensor_copy', K5, K5, out=BT[64:128, st, 32 * q + 16:32 * q + 32], in_=bb[64:128, st, ri, :])
            for st in range(8):
                p_, pk = getps()
                T('transpose', K5 + ['ident_f'], [pk], out=p_[:, 0:128], in_=BT[:, st, :], identity=ident_f[:])
                A('copy', [pk], K5, out=Bm[:, st, ri, :], in_=p_[:, 0:128])
        V('memset', K5, K5, ap=Cm[:], constant=0.0)
        for ri in range(2):
            for st in range(8):
                q = st % 4
                sc = 1.0 if ri == 0 else -1.0
                V('tensor_scalar', K5, K5, out=Cm[0:64, st, ri, 32 * q:32 * q + 16], in0=s5c_sb[0:64, st, ri, :],
                  scalar1=sc, scalar2=None, op0=OP.mult)
                V('tensor_scalar', K5, K5, out=Cm[64:128, st, ri, 32 * q + 16:32 * q + 32], in0=s5c_sb[64:128, st, ri, :],
                  scalar1=sc, scalar2=None, op0=OP.mult)
        V('memset', K5, K5, ap=Ec[:, :, 0:1], constant=1.0)
        V('memset', K5, K5, ap=Es[:, :, 0:1], constant=0.0)
        V('tensor_copy', K5, K5, out=r(WC), in_=r(COS))
        V('tensor_copy', K5, K5, out=r(WS), in_=r(SIN))
        s = 1
        while s < 128:
            for st in range(8):
                wc = sm[:, WC, st:st + 1]
                ws = sm[:, WS, st:st + 1]
                V('tensor_scalar', K5, K5, out=tsm[:, 0:s], in0=Es[:, st, 0:s], scalar1=ws, scalar2=None, op0=OP.mult)
                V('scalar_tensor_tensor', K5, K5, out=Ec[:, st, s:2 * s], in0=Ec[:, st, 0:s], scalar=wc, in1=tsm[:, 0:s],
                  op0=OP.mult, op1=OP.subtract)
                V('tensor_scalar', K5, K5, out=tsm[:, 0:s], in0=Es[:, st, 0:s], scalar1=wc, scalar2=None, op0=OP.mult)
                V('scalar_tensor_tensor', K5, K5, out=Es[:, st, s:2 * s], in0=Ec[:, st, 0:s], scalar=ws, in1=tsm[:, 0:s],
                  op0=OP.mult, op1=OP.add)
            V('tensor_tensor', K5, K5, out=r(X1), in0=r(WC), in1=r(WC), op=OP.mult)
            V('tensor_tensor', K5, K5, out=r(X2), in0=r(WS), in1=r(WS), op=OP.mult)
            V('tensor_tensor', K5, K5, out=r(X2), in0=r(X1), in1=r(X2), op=OP.subtract)
            V('scalar_tensor_tensor', K5, K5, out=r(WS), in0=r(WC), scalar=2.0, in1=r(WS), op0=OP.mult, op1=OP.mult)
            V('tensor_copy', K5, K5, out=r(WC), in_=r(X2))
            s *= 2
        for st in range(8):
            V('tensor_copy', K5, K5, out=rtab[:, st, :], in_=sm[:, MAG, st:st + 1].to_broadcast([128, 128]))
            V('tensor_copy', K5, K5, out=rtabS[:, st, :], in_=sm[:, MAG, st:st + 1].to_broadcast([128, 64]))
        V('memset', K5, K5, ap=rtab[:, :, 0:1], constant=0.0)
        for q in range(NSEQ):
            V('memset', K5, K5, ap=rtabS[:, :, 16 * q:16 * q + 1], constant=0.0)
            V('tensor_copy', K5, K5, out=EcS[:, :, 16 * q:16 * q + 16], in_=Ec[:, :, 0:16])
            V('tensor_copy', K5, K5, out=EsS[:, :, 16 * q:16 * q + 16], in_=Es[:, :, 0:16])
            V('tensor_copy', K5, K5, out=abx[:, 0, :, q], in_=r(ABR))
            V('tensor_copy', K5, K5, out=abx[:, 1, :, q], in_=r(ABI))
        V('memset', K5, ['hcar'], ap=hcar[:], constant=0.0)

    def s5_chunk(l, c0, cw, nseg, sample):
        seglen = cw // nseg
        ec = (EcS if sample else Ec)[:, :, 0:cw]
        esn = (EsS if sample else Es)[:, :, 0:cw]
        rt = (rtabS if sample else rtab)[:, :, 0:cw]
        pks = []
        for ri in range(2):
            for hb in range(2):
                p_, pk = getps()
                pks.append((p_, pk, ri, hb))
                for sti in range(4):
                    st = hb * 4 + sti
                    T('matmul', K5 + ['uT_bf'], [pk], signal=(sti == 3), out=p_[:, sti * cw:(sti + 1) * cw], lhsT=Bm[:, st, ri, :],
                      rhs=uT_bf[:, st // 4, c0:c0 + cw], start=True, stop=True)
        for (p_, pk, ri, hb) in pks:
            dst = (bre if ri == 0 else bim)
            A('copy', [pk], ['bre' if ri == 0 else 'bim'], out=dst[:, hb * 4:hb * 4 + 4, 0:cw],
              in_=p_[:, 0:4 * cw].rearrange("p (a b) -> p a b", a=4))
        hp_r = hcar[:, 0, :, 0:nseg]
        hp_i = hcar[:, 1, :, 0:nseg]
        ar = abx[:, 0, :, 0:nseg]
        ai = abx[:, 1, :, 0:nseg]
        f = lambda i: ftmp[:, i, :, 0:nseg]
        sv = lambda t: t[:, :, 0:cw:seglen]
        ev_ = lambda t: t[:, :, seglen - 1:cw:seglen]
        V('tensor_tensor', ['hcar'] + K5, ['ftmp'], out=f(0), in0=ar, in1=hp_r, op=OP.mult)
        V('tensor_tensor', ['hcar'] + K5, ['ftmp'], out=f(1), in0=ai, in1=hp_i, op=OP.mult)
        V('tensor_tensor', ['ftmp'], ['ftmp'], out=f(0), in0=f(0), in1=f(1), op=OP.subtract)
        V('tensor_tensor', ['ftmp', 'bre'], ['bre'], out=sv(bre), in0=sv(bre), in1=f(0), op=OP.add)
        V('tensor_tensor', ['hcar'] + K5, ['ftmp'], out=f(2), in0=ar, in1=hp_i, op=OP.mult)
        V('tensor_tensor', ['hcar'] + K5, ['ftmp'], out=f(3), in0=ai, in1=hp_r, op=OP.mult)
        V('tensor_tensor', ['ftmp'], ['ftmp'], out=f(2), in0=f(2), in1=f(3), op=OP.add)
        V('tensor_tensor', ['ftmp', 'bim'], ['bim'], out=sv(bim), in0=sv(bim), in1=f(2), op=OP.add)
        W_ = lambda t: t[:, :, 0:cw]
        V('tensor_tensor', ['bre'] + K5, ['t1'], out=W_(t1), in0=W_(bre), in1=ec, op=OP.mult)
        V('tensor_tensor', ['bim'] + K5, ['t2'], out=W_(t2), in0=W_(bim), in1=esn, op=OP.mult)
        G('tensor_tensor', ['bim'] + K5, ['t3'], out=W_(t3), in0=W_(bim), in1=ec, op=OP.mult)
        G('tensor_tensor', ['bre'] + K5, ['t4'], out=W_(t4), in0=W_(bre), in1=esn, op=OP.mult)
        V('tensor_tensor', ['t1', 't2'], ['t1'], out=W_(t1), in0=W_(t1), in1=W_(t2), op=OP.add)
        G('tensor_tensor', ['t3', 't4'], ['t3'], out=W_(t3), in0=W_(t3), in1=W_(t4), op=OP.subtract)
        fl = lambda ap: ap.rearrange("p a b -> p (a b)")
        if cw == 128:
            V('tensor_tensor_scan', ['t1'] + K5, ['gre'], out=fl(gre[:]), data0=fl(rtab[:]), data1=fl(t1[:]), initial=0.0,
              op0=OP.mult, op1=OP.add)
            V('tensor_tensor_scan', ['t3'] + K5, ['gim'], out=fl(gim[:]), data0=fl(rtab[:]), data1=fl(t3[:]), initial=0.0,
              op0=OP.mult, op1=OP.add)
        else:
            for st in range(8):
                V('tensor_tensor_scan', ['t1'] + K5, ['gre'], out=gre[:, st, 0:cw], data0=rt[:, st, :], data1=t1[:, st, 0:cw],
                  initial=0.0, op0=OP.mult, op1=OP.add)
                V('tensor_tensor_scan', ['t3'] + K5, ['gim'], out=gim[:, st, 0:cw], data0=rt[:, st, :], data1=t3[:, st, 0:cw],
                  initial=0.0, op0=OP.mult, op1=OP.add)
        V('tensor_tensor', ['gre'] + K5, ['t1'], out=W_(t1), in0=W_(gre), in1=ec, op=OP.mult)
        V('tensor_tensor', ['gim'] + K5, ['t2'], out=W_(t2), in0=W_(gim), in1=esn, op=OP.mult)
        G('tensor_tensor', ['gre'] + K5, ['t3'], out=W_(t3), in0=W_(gre), in1=esn, op=OP.mult)
        G('tensor_tensor', ['gim'] + K5, ['t4'], out=W_(t4), in0=W_(gim), in1=ec, op=OP.mult)
        V('tensor_tensor', ['t1', 't2'], ['hre_bf'], out=W_(hre_bf), in0=W_(t1), in1=W_(t2), op=OP.subtract)
        G('tensor_tensor', ['t3', 't4'], ['him_bf'], out=W_(him_bf), in0=W_(t3), in1=W_(t4), op=OP.add)
        V('tensor_tensor', ['t1', 't2'], ['hcar'], out=hcar[:, 0, :, 0:nseg], in0=ev_(t1), in1=ev_(t2), op=OP.subtract)
        V('tensor_tensor', ['t3', 't4'], ['hcar'], out=hcar[:, 1, :, 0:nseg], in0=ev_(t3), in1=ev_(t4), op=OP.add)
        for m in range(2):
            p_, pk = getps()
            n = 0
            for sti in range(4):
                st = m * 4 + sti
                for ri in range(2):
                    T('matmul', K5 + ['hre_bf', 'him_bf'], [pk], signal=(n == 7), out=p_[:, 0:cw], lhsT=Cm[:, st, ri, :],
                      rhs=(hre_bf if ri == 0 else him_bf)[:, st, 0:cw], start=(n == 0), stop=(n == 7))
                    n += 1
            V('scalar_tensor_tensor', [pk, 'projT'] + K5, ['yT'], out=yT[:, m, c0:c0 + cw], in0=projT[:, m, c0:c0 + cw],
              scalar=s5d_sb[:, m:m + 1], in1=p_[:, 0:cw], op0=OP.mult, op1=OP.add)

    def fm_rstd(src, ntile, n, ntok, srckeys):
        p_, pk = getps()
        for t in range(ntile):
            A('activation', srckeys, ['sqb'], out=sqb[:, 0:ntok], in_=src[:, t, 0:ntok], func=AF.Square)
            T('matmul', ['sqb', 'ones_bf'], [pk], out=p_[:, 0:ntok], lhsT=ones_bf[:], rhs=sqb[:, 0:ntok],
              start=(t == 0), stop=(t == ntile - 1))
        rms_rstd(p_[:, 0:ntok], rbc[:, 0:ntok], n, [pk], ['rbc'])

    def hg_setup(l):
        pg.dma('sp', lbl_sb[:], lbl[:, :, :], writes=['lbl_sb'])
        e0, e1, tot, p0, p1, cum = [lbw[:, i, :] for i in range(6)]
        A('activation', ['lbl_sb'], ['lbw'], out=e0, in_=lbl_sb[:, 0, :], func=AF.Exp)
        A('activation', ['lbl_sb'], ['lbw'], out=e1, in_=lbl_sb[:, 1, :], func=AF.Exp)
        V('tensor_tensor', ['lbw'], ['lbw'], out=tot, in0=e0, in1=e1, op=OP.add)
        V('reciprocal', ['lbw'], ['lbw'], out=tot, in_=tot)
        V('tensor_tensor', ['lbw'], ['lbw'], out=p0, in0=e0, in1=tot, op=OP.mult)
        V('tensor_tensor', ['lbw'], ['lbw'], out=p1, in0=e1, in1=tot, op=OP.mult)
        V('tensor_copy', ['lbw'], ['lbw'], out=cum, in_=p0)
        if l == 1:
            V('tensor_tensor', ['lbw'], ['lbw'], out=cum, in0=cum, in1=p1, op=OP.add)
        V('tensor_tensor', ['lbw'], ['lb_sb'], out=lb_sb[:], in0=cum, in1=p0, op=OP.subtract)
        V('tensor_scalar', ['lb_sb'], ['oml_sb'], out=oml_sb[:], in0=lb_sb[:], scalar1=-1.0, scalar2=1.0, op0=OP.mult, op1=OP.add)
        V('memset', [], ['Sbd'], ap=Sbd[:], constant=0.0)
        V('memset', [], ['Sst'], ap=Sst[:], constant=0.0)

    def hg_block(l, ntok, C, sample, t0):
        ncnk = ntok // C
        mid = C // 2 - 1
        W = lambda t: t[:].rearrange("p a b -> p (a b)").rearrange("p (a b) -> p a b", a=2)[:, :, 0:ntok]
        fsg, logf, kf, qf, bcs, dd, ee = W(bre), W(bim), W(t1), W(t2), W(t3), W(t4), W(gre)
        gb = gim[:].rearrange("p a b -> p (a b)").bitcast(BF)
        qtil = gb[:, 0:1024].rearrange("p (a b) -> p a b", a=2)[:, :, 0:ntok]
        ktil = gb[:, 1024:2048].rearrange("p (a b) -> p a b", a=2)[:, :, 0:ntok]
        qdec = hre_bf[:].rearrange("p a b -> p (a b)").rearrange("p (a b) -> p a b", a=2)[:, :, 0:ntok]
        kdec = him_bf[:].rearrange("p a b -> p (a b)").rearrange("p (a b) -> p a b", a=2)[:, :, 0:ntok]
        qT_ = projT[:, 2:4, 0:ntok]
        fT_ = projT[:, 4:6, 0:ntok]
        cm = (cm16 if sample else cm64)[:, 0:ntok]
        A('activation', ['projT'], ['bre'], out=fsg, in_=fT_, func=AF.Sigmoid)
        for tl in range(2):
            V('tensor_scalar', ['bre', 'oml_sb', 'lb_sb'], ['bre'], out=fsg[:, tl, :], in0=fsg[:, tl, :], scalar1=oml_sb[:, tl:tl + 1],
              scalar2=lb_sb[:, tl:tl + 1], op0=OP.mult, op1=OP.add)
        A('activation', ['bre'], ['bim'], out=logf, in_=fsg, func=AF.Ln)
        V('tensor_scalar', ['bre'], ['t1'], out=kf, in0=fsg, scalar1=-1.0, scalar2=1.0, op0=OP.mult, op1=OP.add)
        A('activation', ['projT'], ['t2'], out=qf, in_=qT_, func=AF.Silu)
        for tl in range(2):
            V('tensor_tensor_scan', ['bim', 'cm64', 'cm16'], ['t3'], out=bcs[:, tl, :], data0=cm, data1=logf[:, tl, :], initial=0.0,
              op0=OP.mult, op1=OP.add)
        c3 = lambda ap: ap.rearrange("p (a b) -> p a b", b=C)
        for tl in range(2):
            bv = c3(bcs[:, tl, :])
            V('tensor_tensor', ['t3'], ['t4'], out=c3(dd[:, tl, :]), in0=bv, in1=bv[:, :, mid:mid + 1].to_broadcast([128, ncnk, C]),
              op=OP.subtract)
        A('activation', ['t4'], ['gre'], out=ee, in_=dd, func=AF.Exp)
        V('tensor_tensor', ['t2', 'gre'], ['gim'], out=qtil, in0=qf, in1=ee, op=OP.mult)
        A('activation', ['t4'], ['gre'], out=ee, in_=dd, func=AF.Exp, scale=-1.0)
        V('tensor_tensor', ['t1', 'gre'], ['gim'], out=ktil, in0=kf, in1=ee, op=OP.mult)
        for tl in range(2):
            bv = c3(bcs[:, tl, :])
            V('tensor_tensor', ['t3'], ['t4'], out=c3(dd[:, tl, :]), in0=bv[:, :, C - 1:C].to_broadcast([128, ncnk, C]), in1=bv,
              op=OP.subtract)
        A('activation', ['t4'], ['gre'], out=ee, in_=dd, func=AF.Exp)
        V('tensor_tensor', ['t1', 'gre'], ['him_bf'], out=kdec, in0=kf, in1=ee, op=OP.mult)
        A('activation', ['t3'], ['gre'], out=ee, in_=bcs, func=AF.Exp)
        V('tensor_tensor', ['t2', 'gre'], ['hre_bf'], out=qdec, in0=qf, in1=ee, op=OP.mult)
        for ci in range(ncnk):
            if 'c' in SKIP:
                break
            c0 = ci * C
            if sample:
                j, rb = ci, 0
                pg.dma('sp', Sst[:], st_hg[l, ci], writes=['Sst'])
                for tl in range(2):
                    A('copy', ['Sst'], ['Sbd'], out=Sbd[0:64, tl, 0:64], in_=Sst[0:64, tl, :])
                    A('copy', ['Sst'], ['Sbd'], out=Sbd[64:128, tl, 64:128], in_=Sst[64:128, tl, :])
                msk = cmaskS[0:C]
            else:
                j, rb = c0 // 128, c0 % 128
                msk = cmask4[rb:rb + C]
            for par in range(2):
                pa, pak = getps()
                pb = par * 64
                for tl in range(2):
                    T('matmul', ['gim'], [pak], signal=(tl == 1), out=pa[rb:rb + C, tl * C:(tl + 1) * C], lhsT=ktil[pb:pb + 64, tl, c0:c0 + C],
                      rhs=qtil[pb:pb + 64, tl, c0:c0 + C], start=True, stop=True)
                V('tensor_tensor', [pak, 'cmask4', 'cmaskS'], ['attm'], out=attm[rb:rb + C, par:4:2, 0:C],
                  in0=pa[rb:rb + C, 0:2 * C].rearrange("p (a b) -> p a b", a=2), in1=msk[:, 0:2, :], op=OP.mult)
            if 'p' in SKIP:
                continue
            pbk, pbkk = getpsb()
            for tl in range(2):
                T('transpose', ['him_bf', 'ident_b'], [pbkk], signal=(tl == 1), out=pbk[rb:rb + C, tl * 128:(tl + 1) * 128],
                  in_=kdec[:, tl, c0:c0 + C], identity=ident_b[:])
            A('copy', [pbkk], ['kdt'], out=kdt[rb:rb + C, :], in_=pbk[rb:rb + C, 0:256])
            if 'u' in SKIP:
                continue
            pu, puk = getps()
            for h in range(4):
                tl, pb = h // 2, (h % 2) * 64
                T('matmul', ['kdt', 'vtm_bf'], [puk], signal=(h == 3), out=pu[pb:pb + 64, tl * 64:(tl + 1) * 64],
                  lhsT=kdt[rb:rb + C, h * 64:(h + 1) * 64], rhs=vtm_bf[rb:rb + C, j, h * 64:(h + 1) * 64], start=True, stop=True)
            if 'o' in SKIP:
                continue
            po1, po1k = getps()
            for tl in range(2):
                T('matmul', ['Sbd', 'hre_bf'], [po1k], signal=(tl == 1), out=po1[:, tl * C:(tl + 1) * C], lhsT=Sbd[:, tl, :],
                  rhs=qdec[:, tl, c0:c0 + C], start=True, stop=True)
            if 'x' not in SKIP:
                po2, po2k = getps()
                for h in range(4):
                    tl, pb = h // 2, (h % 2) * 64
                    T('matmul', ['vtm_bf', 'attm'], [po2k], signal=(h == 3), out=po2[pb:pb + 64, tl * C:(tl + 1) * C],
                      lhsT=vtm_bf[rb:rb + C, j, h * 64:(h + 1) * 64], rhs=attm[rb:rb + C, h, 0:C], start=True, stop=True)
                A('copy', [po1k], ['tmpo'], out=tmpo[:, :, 0:C], in_=po1[:, 0:2 * C].rearrange("p (a b) -> p a b", a=2))
                V('tensor_tensor', ['tmpo', po2k], ['hgo'], out=hgo[:, :, c0:c0 + C], in0=tmpo[:, :, 0:C],
                  in1=po2[:, 0:2 * C].rearrange("p (a b) -> p a b", a=2), op=OP.add)
            if 'y' not in SKIP:
                for tl in range(2):
                    V('scalar_tensor_tensor', ['Sst', 'gre', puk], ['Sst'], out=Sst[:, tl, :], in0=Sst[:, tl, :],
                      scalar=ee[:, tl, c0 + C - 1:c0 + C], in1=pu[:, tl * 64:(tl + 1) * 64], op0=OP.mult, op1=OP.add)
                    A('copy', ['Sst'], ['Sbd'], out=Sbd[0:64, tl, 0:64], in_=Sst[0:64, tl, :])
                    A('copy', ['Sst'], ['Sbd'], out=Sbd[64:128, tl, 64:128], in_=Sst[64:128, tl, :])
            if sample:
                pg.dma('pool', o_hgs[l, ci], Sst[:], reads=['Sst'])
        if (not sample) and (t0 + ntok == SEQ or (nblk_limit is not None and t0 + ntok == nblk_limit * 512)):
            pg.dma('pool', o_hgp[l], Sst[:], reads=['Sst'])


    region()
    HS = SEQ // 2
    b_kT = sb("b_kT", [128, SEQ], BF)
    b_peT = [sb("b_peT0", [128, HS], BF), sb("b_peT1", [128, HS], BF)]
    b_v = sb("b_v", [128, SEQ // 128, 128], BF)
    b_wout = sb("b_wout", [128, 8, D], BF)
    b_wukT = sb("b_wukT", [128, 8, 128], BF)
    b_wuv = sb("b_wuv", [128, 8, 64], BF)
    b_dmask = sb("b_dmask", [128, 4, 512], BF)
    b_qT = sb("b_qT", [128, 8, 512], BF)
    b_qlat = sb("b_qlat", [128, 8, 512], BF)
    b_pT = [sb("b_pT0", [128, 512], BF), sb("b_pT1", [128, 512], BF)]
    b_recip = sb("b_recip", [128, 512], F32)
    b_acc = sb("b_acc", [128, 512], F32)
    b_acc1 = sb("b_acc1", [128, 512], F32)
    b_olat = sb("b_olat", [128, 512], BF)
    b_mlao = sb("b_mlao", [128, 4, 512], F32)
    b_mix = sb("b_mix", [128, 8, 512], BF)
    b_x = sb("b_x", [128, 4, D], F32)
    b_wst = sb("b_wst", [128, 2048], F32)
    b_ckv = b_wst.rearrange("p (a b) -> p a b", a=16)
    b_vS = sb("b_vS", [128, 16, 128], BF)
    b_kTS = sb("b_kTS", [128, PAST], BF)
    b_cpe = sb("b_cpe", [128, 16, 32], F32)
    b_cpeb = sb("b_cpeb", [128, 16, 32], BF)
    b_peTS = sb("b_peTS", [128, PAST], BF)
    b_kTn = sb("b_kTn", [128, 16], BF)
    b_peTn = sb("b_peTn", [128, 16], BF)
    b_vn = sb("b_vn", [16, 128], BF)
    b_qS = sb("b_qS", [128, 8, 16], BF)
    b_qlS = sb("b_qlS", [128, 8, 16], BF)
    b_pTS = [sb("b_pTS0", [128, 128], BF), sb("b_pTS1", [128, 128], BF)]
    SCALE = float(96.0 ** -0.5)
    NPROMPT = SEQ if nblk_limit is None else nblk_limit * 512

    def passB_setup(l):
        pg.dma('sp', b_kT[:, 0:NPROMPT], kT_d[:, 0:NPROMPT], reads=['kT_d'], writes=['b_kT'])
        V('memset', [], ['b_peT0'], ap=b_peT[0][:], constant=0.0)
        V('memset', [], ['b_peT1'], ap=b_peT[1][:], constant=0.0)
        V('memset', [], ['b_peTS'], ap=b_peTS[:], constant=0.0)
        V('memset', [], ['b_peTn'], ap=b_peTn[:], constant=0.0)
        g0 = min(NPROMPT, HS)
        pg.dma('sp', b_peT[0][0:32, 0:g0], peT_d[:, 0:g0], reads=['peT_d', 'b_peT0'], writes=['b_peT0'])
        if NPROMPT > HS:
            pg.dma('sp', b_peT[1][0:32, 0:NPROMPT - HS], peT_d[:, HS:NPROMPT], reads=['peT_d', 'b_peT1'], writes=['b_peT1'])
        pg.dma('sp', b_v[:, 0:NPROMPT // 128, :], vtok_d[0:NPROMPT, :].rearrange("(b p) c -> p b c", p=128), reads=['vtok_d'],
               writes=['b_v'])
        load_weight_bf(b_wout, 'b_wout', lambda kt, c0, c1, l=l: w_out[l, kt * 128:(kt + 1) * 128, c0:c1], 8, D, stage=b_wst, stagekey='b_wst')
        pg.dma('sp', b_wst[:, 0:1024], w_ukT[l].rearrange("p h c -> p (h c)"), reads=['b_wst0', 'b_wst1'], writes=['b_wst0', 'b_wst1'])
        G('tensor_copy', ['b_wst0'], ['b_wukT'], out=b_wukT[:].rearrange("p h c -> p (h c)"), in_=b_wst[:, 0:1024])
        pg.dma('sp', b_wst[:, 1024:1536], w_uv[l].rearrange("p h c -> p (h c)"), reads=['b_wst0', 'b_wst1'], writes=['b_wst0', 'b_wst1'])
        G('tensor_copy', ['b_wst1'], ['b_wuv'], out=b_wuv[:].rearrange("p h c -> p (h c)"), in_=b_wst[:, 1024:1536])
        G('memset', [], ['b_dmask'], ap=b_dmask[:], constant=1.0)
        for j in range(4):
            for kh in range(2):
                G('affine_select', ['b_dmask'], ['b_dmask'], out=b_dmask[kh * 64:(kh + 1) * 64, j, :].rearrange("p (a b) -> p a b", a=8),
                  in_=b_dmask[kh * 64:(kh + 1) * 64, j, :].rearrange("p (a b) -> p a b", a=8), pattern=[[1, 8], [0, 64]],
                  compare_op=OP.is_ge, fill=0.0, base=-(2 * j + kh), channel_multiplier=0)

    def passB_tail(l, t0, tp, nt, xsrc, xkey):
        ntok = tp * nt
        fm_rstd(b_mlao, 4, 512, ntok, ['b_mlao'])
        for t in range(4):
            V('scalar_tensor_tensor', ['b_mlao', 'ong_sb', 'rbc'], ['b_mix'], out=b_mix[:, 4 + t, 0:ntok], in0=b_mlao[:, t, 0:ntok],
              scalar=ong_sb[:, 4 + t:5 + t], in1=rbc[:, 0:ntok], op0=OP.mult, op1=OP.mult)
        for j in range(nt):
            for hf in range(2):
                p_, pk = getps()
                for kt in range(8):
                    T('matmul', ['b_mix', 'b_wout'], [pk], signal=(kt == 7), out=p_[0:tp, :], lhsT=b_mix[:, kt, j * tp:(j + 1) * tp],
                      rhs=b_wout[:, kt, hf * 512:(hf + 1) * 512], start=(kt == 0), stop=(kt == 7))
                V('tensor_tensor', ['b_x', pk], ['b_x'], out=b_x[0:tp, j, hf * 512:(hf + 1) * 512], in0=b_x[0:tp, j, hf * 512:(hf + 1) * 512],
                  in1=p_[0:tp, :], op=OP.add)
        pg.dma('pool', xmid_d[t0:t0 + ntok, :].rearrange("(j p) c -> p j c", p=tp), b_x[0:tp, 0:nt, :], reads=['b_x'], writes=['xmid_d'])

    def passB_prompt_block(l, t0, xsrc, xkey):
        pg.dma('sp', b_qT[:], qT_d[:, :, t0:t0 + 512].rearrange("h p t -> p h t"), reads=['qT_d'], writes=['b_qT'])
        pg.dma('sp', b_mix[:, 0:4, :], mixT_d[:, :, t0:t0 + 512].rearrange("a p t -> p a t"), reads=['mixT_d'], writes=['b_mix'])
        pg.dma('sp', b_x[:], xsrc[t0:t0 + 512, :].rearrange("(j p) c -> p j c", p=128), reads=[xkey], writes=['b_x'])
        psallow[0] = [4, 5]
        for h in range(8):
            p_, pk = getps()
            T('matmul', ['b_wukT', 'b_qT'], [pk], out=p_[:, :], lhsT=b_wukT[64:128, h, :], rhs=b_qT[64:128, h, :], start=True, stop=True)
            A('copy', [pk], ['b_qlat'], out=b_qlat[:, h, :], in_=p_[:, :])
        nkb = (t0 + 512) // 128
        kd0 = t0 // 128
        po, pok = psf[2], ('psf', 2)
        pd, pdk = psf[3], ('psf', 3)

        def S(h, kb):
            ps_s, sk = psf[kb % 2], ('psf', kb % 2)
            g, kl = kb // 64, kb % 64
            T('matmul', ['b_kT', 'b_qlat'], [sk], signal=False, out=ps_s[:, :], lhsT=b_kT[:, kb * 128:(kb + 1) * 128], rhs=b_qlat[:, h, :],
              start=True, stop=False)
            T('matmul', ['b_peT%d' % g, 'b_qT'], [sk], out=ps_s[:, :], lhsT=b_peT[g][:, kl * 128:(kl + 1) * 128], rhs=b_qT[:, h, :],
              start=False, stop=True)
            pT, pTk = b_pT[kb % 2], 'b_pT%d' % (kb % 2)
            A('activation', [sk], [pTk], out=pT[:], in_=ps_s[:, :], func=AF.Exp, scale=SCALE)
            if kb >= kd0:
                G('tensor_tensor', [pTk, 'b_dmask'], [pTk], out=pT[:], in0=pT[:], in1=b_dmask[:, kb - kd0, :], op=OP.mult)

        def PV(h, kb):
            pT, pTk = b_pT[kb % 2], 'b_pT%d' % (kb % 2)
            T('matmul', ['b_v', pTk], [pok], out=po[:, :], lhsT=b_v[:, kb, :], rhs=pT[:], start=(kb == 0), stop=(kb == nkb - 1))
            if kb == 0:
                V('tensor_copy', [pTk], ['b_acc'], out=b_acc[:], in_=pT[:])
            elif kb == 1:
                G('tensor_copy', [pTk], ['b_acc1'], out=b_acc1[:], in_=pT[:])
            elif kb % 2 == 0:
                V('tensor_tensor', [pTk, 'b_acc'], ['b_acc'], out=b_acc[:], in0=b_acc[:], in1=pT[:], op=OP.add)
            else:
                G('tensor_tensor', [pTk, 'b_acc1'], ['b_acc1'], out=b_acc1[:], in0=b_acc1[:], in1=pT[:], op=OP.add)

        for h in range(8):
            S(h, 0)
            for kb in range(nkb):
                if kb + 1 < nkb:
                    S(h, kb + 1)
                PV(h, kb)
            T('matmul', ['ones_f', 'b_acc'], [pdk], signal=False, out=pd[:, :], lhsT=ones_f[:], rhs=b_acc[:], start=True, stop=False)
            T('matmul', ['ones_f', 'b_acc1'], [pdk], out=pd[:, :], lhsT=ones_f[:], rhs=b_acc1[:], start=False, stop=True)
            V('reciprocal', [pdk], ['b_recip'], out=b_recip[:], in_=pd[:, :])
            V('tensor_tensor', [pok, 'b_recip'], ['b_olat'], out=b_olat[:], in0=po[:, :], in1=b_recip[:], op=OP.mult)
            p_, pk = getps()
            pbh = (h % 2) * 64
            T('matmul', ['b_wuv', 'b_olat'], [pk], out=p_[pbh:pbh + 64, :], lhsT=b_wuv[:, h, :], rhs=b_olat[:], start=True, stop=True)
            A('copy', [pk], ['b_mlao'], out=b_mlao[pbh:pbh + 64, h // 2, :], in_=p_[pbh:pbh + 64, :])
        passB_tail(l, t0, 128, 4, xsrc, xkey)
        psallow[0] = list(range(NPS))

    def passB_sample(l, xsrc, xkey):
        t0 = SEQ
        pg.dma('sp', b_mix[:, 0:4, 0:64], mixT_d[:, :, t0:t0 + 64].rearrange("a p t -> p a t"), reads=['mixT_d'], writes=['b_mix'])
        pg.dma('sp', b_x[0:64, 0, :], xsrc[t0:t0 + 64, :], reads=[xkey], writes=['b_x'])
        psallow[0] = [4, 5]
        po, pok = psf[2], ('psf', 2)
        pd, pdk = psf[3], ('psf', 3)
        for q in range(NSEQ):
            c0 = t0 + 16 * q
            pg.dma('sp', b_ckv[:], cache_kv[l, q].rearrange("(b p) c -> p b c", p=128), writes=['b_wst0', 'b_wst1'])
            pg.dma('sp', b_cpe[:], cache_pe[l, q].rearrange("(b p) c -> p b c", p=128), writes=['b_cpe'])
            pg.dma('sp', b_kTn[:], kT_d[:, c0:c0 + 16], reads=['kT_d'], writes=['b_kTn'])
            pg.dma('sp', b_peTn[0:32, :], peT_d[:, c0:c0 + 16], reads=['peT_d', 'b_peTn'], writes=['b_peTn'])
            pg.dma('sp', b_vn[:], vtok_d[c0:c0 + 16, :], reads=['vtok_d'], writes=['b_vn'])
            pg.dma('sp', b_qS[:], qT_d[:, :, c0:c0 + 16].rearrange("h p t -> p h t"), reads=['qT_d'], writes=['b_qS'])
            V('tensor_copy', ['b_wst0', 'b_wst1'], ['b_vS'], out=b_vS[:], in_=b_ckv[:])
            V('tensor_copy', ['b_cpe'], ['b_cpeb'], out=b_cpeb[:], in_=b_cpe[:])
            for grp in range(2):
                pb_, pbk_ = getpsb()
                for i in range(8):
                    T('transpose', ['b_vS', 'ident_b'], [pbk_], signal=(i == 7), out=pb_[:, i * 128:(i + 1) * 128], in_=b_vS[:, grp * 8 + i, :],
                      identity=ident_b[:])
                A('copy', [pbk_], ['b_kTS'], out=b_kTS[:, grp * 1024:(grp + 1) * 1024], in_=pb_[:, 0:1024])
            for grp in range(2):
                pb_, pbk_ = getpsb()
                for i in range(8):
                    T('transpose', ['b_cpeb', 'ident_b'], [pbk_], signal=(i == 7), out=pb_[0:32, i * 128:(i + 1) * 128],
                      in_=b_cpeb[:, grp * 8 + i, :], identity=ident_b[:])
                A('copy', [pbk_, 'b_peTS'], ['b_peTS'], out=b_peTS[0:32, grp * 1024:(grp + 1) * 1024], in_=pb_[0:32, 0:1024])
            p_, pk = getps()
            for h in range(8):
                T('matmul', ['b_wukT', 'b_qS'], [pk], signal=(h == 7), out=p_[:, h * 16:(h + 1) * 16], lhsT=b_wukT[64:128, h, :],
                  rhs=b_qS[64:128, h, :], start=True, stop=True)
            A('copy', [pk], ['b_qlS'], out=b_qlS[:].rearrange("p h t -> p (h t)"), in_=p_[:, 0:128])
            qlf = b_qlS[:].rearrange("p h t -> p (h t)")
            qsf = b_qS[:].rearrange("p h t -> p (h t)")
            for blk in range(17):
                nk = 128 if blk < 16 else 16
                ps_s, sk = psf[blk % 2], ('psf', blk % 2)
                kT_l = b_kTS[:, blk * 128:(blk + 1) * 128] if blk < 16 else b_kTn[:, :]
                pe_l = b_peTS[:, blk * 128:(blk + 1) * 128] if blk < 16 else b_peTn[:, :]
                v_l = b_vS[:, blk, :] if blk < 16 else b_vn[:, :]
                T('matmul', ['b_kTS', 'b_kTn', 'b_qlS'], [sk], signal=False, out=ps_s[0:nk, 0:128], lhsT=kT_l, rhs=qlf, start=True, stop=False)
                T('matmul', ['b_peTS', 'b_peTn', 'b_qS'], [sk], out=ps_s[0:nk, 0:128], lhsT=pe_l, rhs=qsf, start=False, stop=True)
                pT, pTk = b_pTS[blk % 2], 'b_pTS%d' % (blk % 2)
                A('activation', [sk], [pTk], out=pT[0:nk, :], in_=ps_s[0:nk, 0:128], func=AF.Exp, scale=SCALE)
                T('matmul', ['b_vS', 'b_vn', pTk], [pok], signal=False, out=po[:, 0:128], lhsT=v_l, rhs=pT[0:nk, :], start=(blk == 0), stop=(blk == 16))
                T('matmul', ['ones_bf', pTk], [pdk], out=pd[:, 0:128], lhsT=ones_bf[0:nk, :], rhs=pT[0:nk, :], start=(blk == 0), stop=(blk == 16))
            V('reciprocal', [pdk], ['b_recip'], out=b_recip[:, 0:128], in_=pd[:, 0:128])
            V('tensor_tensor', [pok, 'b_recip'], ['b_olat'], out=b_olat[:, 0:128], in0=po[:, 0:128], in1=b_recip[:, 0:128], op=OP.mult)
            for h in range(8):
                p_, pk = getps()
                pbh = (h % 2) * 64
                T('matmul', ['b_wuv', 'b_olat'], [pk], out=p_[pbh:pbh + 64, 0:16], lhsT=b_wuv[:, h, :], rhs=b_olat[:, h * 16:(h + 1) * 16],
                  start=True, stop=True)
                A('copy', [pk], ['b_mlao'], out=b_mlao[pbh:pbh + 64, h // 2, 16 * q:16 * q + 16], in_=p_[pbh:pbh + 64, 0:16])
        passB_tail(l, t0, 64, 1, xsrc, xkey)
        psallow[0] = list(range(NPS))

    region()
    c_wup = sb("c_wup", [128, 8, 4 * D], BF)
    c_wdn = sb("c_wdn", [128, 32, D], BF)
    c_x = sb("c_x", [128, 2, D], F32)
    c_h = sb("c_h", [128, 2, D], BF)
    c_hT = sb("c_hT", [128, 8, 256], BF)
    c_a = sb("c_a", [128, 32, 256], BF)
    c_junk = sb("c_junk", [128, D], F32)
    c_r = [sb("c_r0", [128, 256], F32), sb("c_r1", [128, 256], F32)]
    c_wst = sb("c_wst", [128, 2048], F32)
    c_g2 = sb("c_g2", [128, D], F32)
    c_gf = sb("c_gf", [128, D], F32)
    c_y = sb("c_y", [128, 2, D], F32)

    def passC_setup(l):
        pg.dma('sp', c_g2[:], g2B[l], writes=['c_g2'])
        pg.dma('sp', c_gf[:], gfB[:, :], writes=['c_gf'])
        load_weight_bf(c_wup, 'c_wup', lambda kt, c0, c1, l=l: w_up[l, kt * 128:(kt + 1) * 128, c0:c1], 8, 4 * D, stage=c_wst, stagekey='c_wst')
        load_weight_bf(c_wdn, 'c_wdn', lambda kt, c0, c1, l=l: w_down[l, kt * 128:(kt + 1) * 128, c0:c1], 32, D, stage=c_wst, stagekey='c_wst')

    def passC_block(l, t0, tp, nt):
        ntok = tp * nt
        pg.dma('sp', c_x[0:tp, 0:nt, :], xmid_d[t0:t0 + ntok, :].rearrange("(j p) c -> p j c", p=tp), reads=['xmid_d'], writes=['c_x'])
        for j in range(nt):
            A('activation', ['c_x'], ['c_junk'], out=c_junk[0:tp, :], in_=c_x[0:tp, j, :], func=AF.Square)
            V('reduce_sum', ['c_junk'], ['ss'], out=ss[0:tp, j:j + 1], in_=c_junk[0:tp, :], axis=AX.X)
        rms_rstd(ss[0:tp, 0:nt], rstd[0:tp, 0:nt], D, ['ss'], ['rstd'])
        for j in range(nt):
            V('scalar_tensor_tensor', ['c_x', 'rstd', 'c_g2'], ['c_h'], out=c_h[0:tp, j, :], in0=c_x[0:tp, j, :], scalar=rstd[0:tp, j:j + 1],
              in1=c_g2[0:tp, :], op0=OP.mult, op1=OP.mult)
        for kt in range(8):
            pb_, pbk_ = getpsb()
            for j in range(nt):
                T('transpose', ['c_h', 'ident_b'], [pbk_], signal=(j == nt - 1), out=pb_[:, j * tp:(j + 1) * tp],
                  in_=c_h[0:tp, j, kt * 128:(kt + 1) * 128], identity=ident_b[0:tp, 0:tp])
            A('copy', [pbk_], ['c_hT'], out=c_hT[:, kt, 0:ntok], in_=pb_[:, 0:ntok])
        for f in range(32):
            p_, pk = getps()
            for kt in range(8):
                T('matmul', ['c_hT', 'c_wup'], [pk], signal=(kt == 7), out=p_[:, 0:ntok], lhsT=c_wup[:, kt, f * 128:(f + 1) * 128],
                  rhs=c_hT[:, kt, 0:ntok], start=(kt == 0), stop=(kt == 7))
            rr_, rk = c_r[f % 2], 'c_r%d' % (f % 2)
            A('activation', [pk], [rk], out=rr_[:, 0:ntok], in_=p_[:, 0:ntok], func=AF.Relu)
            V('tensor_tensor', [rk], ['c_a'], out=c_a[:, f, 0:ntok], in0=rr_[:, 0:ntok], in1=rr_[:, 0:ntok], op=OP.mult)
        for j in range(nt):
            for hf in range(2):
                p_, pk = getps()
                for f in range(32):
                    T('matmul', ['c_a', 'c_wdn'], [pk], signal=(f == 31), out=p_[0:tp, :], lhsT=c_a[:, f, j * tp:(j + 1) * tp],
                      rhs=c_wdn[:, f, hf * 512:(hf + 1) * 512], start=(f == 0), stop=(f == 31))
                V('tensor_tensor', ['c_x', pk], ['c_x'], out=c_x[0:tp, j, hf * 512:(hf + 1) * 512], in0=c_x[0:tp, j, hf * 512:(hf + 1) * 512],
                  in1=p_[0:tp, :], op=OP.add)
        if l < DEPTH - 1:
            pg.dma('pool', xout_d[t0:t0 + ntok, :].rearrange("(j p) c -> p j c", p=tp), c_x[0:tp, 0:nt, :], reads=['c_x'], writes=['xout_d'])
        else:
            for j in range(nt):
                A('activation', ['c_x'], ['c_junk'], out=c_junk[0:tp, :], in_=c_x[0:tp, j, :], func=AF.Square)
                V('reduce_sum', ['c_junk'], ['ss'], out=ss[0:tp, 4 + j:5 + j], in_=c_junk[0:tp, :], axis=AX.X)
            rms_rstd(ss[0:tp, 4:4 + nt], rstd[0:tp, 4:4 + nt], D, ['ss'], ['rstd'])
            for j in range(nt):
                V('scalar_tensor_tensor', ['c_x', 'rstd', 'c_gf'], ['c_y'], out=c_y[0:tp, j, :], in0=c_x[0:tp, j, :],
                  scalar=rstd[0:tp, 4 + j:5 + j], in1=c_gf[0:tp, :], op0=OP.mult, op1=OP.mult)
            pg.dma('pool', o_y[t0:t0 + ntok, :].rearrange("(j p) c -> p j c", p=tp), c_y[0:tp, 0:nt, :], reads=['c_y'])

    for l in range(DEPTH):
        pg.barrier()
        passA_consts()
        pg.dma('sp', b_glu_sb[:], b_glu[l], writes=['b_glu_sb'])
        pg.dma('sp', qng_sb[:], qng[l], writes=['qng_sb'])
        pg.dma('sp', ong_sb[:], ong[l], writes=['ong_sb'])
        load_weight_bf(w_glu_bf, 'w_glu_bf', lambda kt, c0, c1, l=l: w_glu[l, kt * 128:(kt + 1) * 128, c0:c1], 2, 256)
        load_weight_bf(w_uq_bf, 'w_uq_bf', lambda kt, c0, c1, l=l: w_uq[l, kt * 128:(kt + 1) * 128, c0:c1], 2, 1024)
        hg_setup(l)
        pg.dma('sp', g1sb[:], g1B[l], writes=['g1sb'])
        pg.dma('sp', kvg[:], kvgB[l], writes=['kvg'])
        load_weight_bf(w_in_bf, 'w_in_bf', lambda kt, c0, c1, l=l: w_in[l, kt * 128:(kt + 1) * 128, c0:c1], 8, INW)
        s5_setup(l)
        xsrc, xkey = (xin, 'xin') if l == 0 else (xout_d, 'xout_d')
        for (t0, tp, nt) in blocks:
            ntok = tp * nt
            sample = (tp == 64)
            pg.dma('sp', xblk[0:tp, 0:nt, :], xsrc[t0:t0 + ntok, :].rearrange("(j p) c -> p j c", p=tp), reads=[xkey], writes=['xblk'])
            pg.dma('sp', rc[0:tp, 0:nt, :], ropec[t0:t0 + ntok, :].rearrange("(j p) c -> p j c", p=tp), writes=['rc'])
            pg.dma('sp', rs[0:tp, 0:nt, :], ropes[t0:t0 + ntok, :].rearrange("(j p) c -> p j c", p=tp), writes=['rs'])
            for j in range(nt):
                A('activation', ['xblk'], ['junkf'], out=junkf[0:tp, :], in_=xblk[0:tp, j, :], func=AF.Square)
                V('reduce_sum', ['junkf'], ['ss'], out=ss[0:tp, j:j + 1], in_=junkf[0:tp, :], axis=AX.X)
            rms_rstd(ss[0:tp, 0:nt], rstd[0:tp, 0:nt], D, ['ss'], ['rstd'])
            for j in range(nt):
                V('scalar_tensor_tensor', ['xblk', 'rstd', 'g1sb'], ['hbf'], out=hbf[0:tp, j, :], in0=xblk[0:tp, j, :],
                  scalar=rstd[0:tp, j:j + 1], in1=g1sb[0:tp, :], op0=OP.mult, op1=OP.mult)
            for kt in range(8):
                pb, pk = getpsb()
                for j in range(nt):
                    T('transpose', ['hbf', 'ident_b'], [pk], signal=(j == nt - 1), out=pb[:, j * tp:(j + 1) * tp],
                      in_=hbf[0:tp, j, kt * 128:(kt + 1) * 128], identity=ident_b[0:tp, 0:tp])
                A('copy', [pk], ['hT'], out=hT[:, kt, 0:ntok], in_=pb[:, 0:ntok])
            for j in range(nt):
                p_, pk = getps()
                for kt in range(8):
                    T('matmul', ['hT', 'w_in_bf'], [pk], signal=(kt == 7), out=p_[0:tp, 0:160], lhsT=hT[:, kt, j * tp:(j + 1) * tp],
                      rhs=w_in_bf[:, kt, 1536:1696], start=(kt == 0), stop=(kt == 7))
                A('copy', [pk], ['kvtm'], out=kvtm[0:tp, j, :], in_=p_[0:tp, 0:160])
            for j in range(nt):
                A('activation', ['kvtm'], ['junkf'], out=junkf[0:tp, 0:128], in_=kvtm[0:tp, j, 0:128], func=AF.Square)
                V('reduce_sum', ['junkf'], ['ss'], out=ss[0:tp, 4 + j:5 + j], in_=junkf[0:tp, 0:128], axis=AX.X)
            rms_rstd(ss[0:tp, 4:4 + nt], rstd[0:tp, 4:4 + nt], 128, ['ss'], ['rstd'])
            for j in range(nt):
                V('scalar_tensor_tensor', ['kvtm', 'rstd', 'kvg'], ['kvn'], out=kvn[0:tp, j, :], in0=kvtm[0:tp, j, 0:128],
                  scalar=rstd[0:tp, 4 + j:5 + j], in1=kvg[0:tp, :], op0=OP.mult, op1=OP.mult)
            x1 = kvtm[0:tp, 0:nt, 128:144]
            x2 = kvtm[0:tp, 0:nt, 144:160]
            c_ = rc[0:tp, 0:nt, :]
            s_ = rs[0:tp, 0:nt, :]
            ta = tmp16[0:tp, 0:nt, :]
            tb = tmp16b[0:tp, 0:nt, :]
            V('tensor_tensor', ['kvtm', 'rc'], ['tmp16'], out=ta, in0=x1, in1=c_, op=OP.mult)
            V('tensor_tensor', ['kvtm', 'rs'], ['tmp16b'], out=tb, in0=x2, in1=s_, op=OP.mult)
            V('tensor_tensor', ['tmp16', 'tmp16b'], ['pen'], out=pen[0:tp, 0:nt, 0:16], in0=ta, in1=tb, op=OP.subtract)
            V('tensor_tensor', ['kvtm', 'rc'], ['tmp16'], out=ta, in0=x2, in1=c_, op=OP.mult)
            V('tensor_tensor', ['kvtm', 'rs'], ['tmp16b'], out=tb, in0=x1, in1=s_, op=OP.mult)
            V('tensor_tensor', ['tmp16', 'tmp16b'], ['pen'], out=pen[0:tp, 0:nt, 16:32], in0=ta, in1=tb, op=OP.add)
            pg.dma('pool', o_kv[l, t0:t0 + ntok, :].rearrange("(j p) c -> p j c", p=tp), kvn[0:tp, 0:nt, :], reads=['kvn'])
            pg.dma('pool', o_pe[l, t0:t0 + ntok, :].rearrange("(j p) c -> p j c", p=tp), pen[0:tp, 0:nt, :], reads=['pen'])
            for slot, ct in enumerate([0, 1, 2, 3, 4, 5, 8, 9, 10, 11]):
                p_, pk = getps()
                for kt in range(8):
                    T('matmul', ['hT', 'w_in_bf'], [pk], signal=(kt == 7), out=p_[:, 0:ntok], lhsT=w_in_bf[:, kt, ct * 128:(ct + 1) * 128],
                      rhs=hT[:, kt, 0:ntok], start=(kt == 0), stop=(kt == 7))
                A('copy', [pk], ['projT'], out=projT[:, slot, 0:ntok], in_=p_[:, 0:ntok])
            V('tensor_copy', ['projT'], ['uT_bf'], out=uT_bf[:, :, 0:ntok], in_=projT[:, 0:2, 0:ntok])
            if sample:
                pg.dma('sp', hcarS[:], st_s5[l], writes=['hcarS'])
                V('tensor_copy', ['hcarS'], ['hcar'], out=hcar[:, 0, :, :], in_=hcarS[:, :, 0, :])
                V('tensor_copy', ['hcarS'], ['hcar'], out=hcar[:, 1, :, :], in_=hcarS[:, :, 1, :])
                s5_chunk(l, 0, 64, NSEQ, True)
                V('tensor_copy', ['hcar'], ['hcarS'], out=hcarS[:, :, 0, :], in_=hcar[:, 0, :, :])
                V('tensor_copy', ['hcar'], ['hcarS'], out=hcarS[:, :, 1, :], in_=hcar[:, 1, :, :])
                pg.dma('pool', o_s5s[l], hcarS[:], reads=['hcarS'])
            else:
                for ci in range(4):
                    s5_chunk(l, ci * 128, 128, 1, False)
                if t0 + ntok == SEQ or (nblk_limit is not None and t0 + ntok == nblk_limit * 512):
                    V('tensor_copy', ['hcar'], ['hcarS'], out=hcarS[:, :, 0, 0:1], in_=hcar[:, 0, :, 0:1])
                    V('tensor_copy', ['hcar'], ['hcarS'], out=hcarS[:, :, 1, 0:1], in_=hcar[:, 1, :, 0:1])
                    V('tensor_copy', ['hcarS'], ['hcarP'], out=hcarP[:], in_=hcarS[:, :, :, 0:1])
                    pg.dma('pool', o_s5p[l], hcarP[:], reads=['hcarP'])
            yv = yT[:, :, 0:ntok]
            zt = t1[:].rearrange("p a b -> p (a b)").rearrange("p (a b) -> p a b", a=2)[:, :, 0:ntok]
            zz = t2[:].rearrange("p a b -> p (a b)").rearrange("p (a b) -> p a b", a=2)[:, :, 0:ntok]
            s5o = t3[:].rearrange("p a b -> p (a b)").rearrange("p (a b) -> p a b", a=2)[:, :, 0:ntok]
            z_bf = uT_bf[:, :, 0:ntok]
            A('activation', ['yT'], ['t1'], out=zt, in_=yv, func=AF.Square)
            V('tensor_scalar', ['t1'], ['t1'], out=zt, in0=zt, scalar1=0.044715, scalar2=1.0, op0=OP.mult, op1=OP.add)
            V('tensor_tensor', ['t1', 'yT'], ['t1'], out=zt, in0=zt, in1=yv, op=OP.mult)
            A('activation', ['t1'], ['t1'], out=zt, in_=zt, func=AF.Sigmoid, scale=1.5957691216057308)
            V('tensor_tensor', ['t1', 'yT'], ['t2'], out=zz, in0=zt, in1=yv, op=OP.mult)
            V('tensor_copy', ['t2'], ['uT_bf'], out=z_bf, in_=zz)
            for m in range(2):
                p_, pk = getps()
                for kt in range(2):
                    T('matmul', ['uT_bf', 'w_glu_bf'], [pk], signal=(kt == 1), out=p_[:, 0:ntok], lhsT=w_glu_bf[:, kt, m * 128:(m + 1) * 128],
                      rhs=z_bf[:, kt, :], start=(kt == 0), stop=(kt == 1))
                A('activation', [pk, 'b_glu_sb'], ['t1'], out=zt[:, m, :], in_=p_[:, 0:ntok], func=AF.Sigmoid, bias=b_glu_sb[:, m:m + 1])
                V('tensor_tensor', ['t1', 't2'], ['t3'], out=s5o[:, m, :], in0=zz[:, m, :], in1=zt[:, m, :], op=OP.mult)
            fm_rstd(s5o, 2, 256, ntok, ['t3'])
            for m in range(2):
                V('scalar_tensor_tensor', ['t3', 'ong_sb', 'rbc'], ['mixA'], out=mixA[:, m, 0:ntok], in0=s5o[:, m, :],
                  scalar=ong_sb[:, m:m + 1], in1=rbc[:, 0:ntok], op0=OP.mult, op1=OP.mult)
            if sample:
                for q in range(NSEQ):
                    p_, pk = getps()
                    for kt in range(8):
                        T('matmul', ['hT', 'w_in_bf'], [pk], signal=(kt == 7), out=p_[0:16, 0:256], lhsT=hT[:, kt, 16 * q:16 * q + 16],
                          rhs=w_in_bf[:, kt, 768:1024], start=(kt == 0), stop=(kt == 7))
                    A('copy', [pk], ['vtm_bf'], out=vtm_bf[0:16, q, :], in_=p_[0:16, 0:256])
            else:
                for j in range(nt):
                    p_, pk = getps()
                    for kt in range(8):
                        T('matmul', ['hT', 'w_in_bf'], [pk], signal=(kt == 7), out=p_[0:tp, 0:256], lhsT=hT[:, kt, j * tp:(j + 1) * tp],
                          rhs=w_in_bf[:, kt, 768:1024], start=(kt == 0), stop=(kt == 7))
                    A('copy', [pk], ['vtm_bf'], out=vtm_bf[0:tp, j, :], in_=p_[0:tp, 0:256])
            if 'H' not in SKIP:
                hg_block(l, ntok, 16 if sample else 64, sample, t0)
            fm_rstd(hgo, 2, 256, ntok, ['hgo'])
            gsl = t1[:].rearrange("p a b -> p (a b)").rearrange("p (a b) -> p a b", a=2)[:, :, 0:ntok]
            A('activation', ['projT'], ['t1'], out=gsl, in_=projT[:, 6:8, 0:ntok], func=AF.Silu)
            for m in range(2):
                V('scalar_tensor_tensor', ['hgo', 'ong_sb', 'rbc'], ['t2'], out=zz[:, m, :], in0=hgo[:, m, 0:ntok],
                  scalar=ong_sb[:, 2 + m:3 + m], in1=rbc[:, 0:ntok], op0=OP.mult, op1=OP.mult)
                V('tensor_tensor', ['t1', 't2'], ['mixA'], out=mixA[:, 2 + m, 0:ntok], in0=zz[:, m, :], in1=gsl[:, m, :], op=OP.mult)
            pg.dma('pool', mixT_d[:, :, t0:t0 + ntok].rearrange("a p t -> p a t"), mixA[:, :, 0:ntok], reads=['mixA'], writes=['mixT_d'])
            if 'M' in SKIP:
                continue
            cqT = projT[:, 8:10, 0:ntok]
            fm_rstd(cqT, 2, 256, ntok, ['projT'])
            cqn = uT_bf[:, :, 0:ntok]
            for m in range(2):
                V('scalar_tensor_tensor', ['projT', 'qng_sb', 'rbc'], ['uT_bf'], out=cqn[:, m, :], in0=cqT[:, m, :],
                  scalar=qng_sb[:, m:m + 1], in1=rbc[:, 0:ntok], op0=OP.mult, op1=OP.mult)
            for j in range(nt):
                for hf in range(2):
                    p_, pk = getps()
                    for kt in range(2):
                        T('matmul', ['uT_bf', 'w_uq_bf'], [pk], signal=(kt == 1), out=p_[0:tp, :], lhsT=cqn[:, kt, j * tp:(j + 1) * tp],
                          rhs=w_uq_bf[:, kt, hf * 512:(hf + 1) * 512], start=(kt == 0), stop=(kt == 1))
                    A('copy', [pk], ['xblk'], out=xblk[0:tp, j, hf * 512:(hf + 1) * 512], in_=p_[0:tp, :])
            for j in range(nt):
                qv = xblk[0:tp, j, :].rearrange("p (h c) -> p h c", c=128)[:, :, 0:64].rearrange("p h (k e) -> p h k e", e=32)
                x1 = qv[:, :, :, 0:16]
                x2 = qv[:, :, :, 16:32]
                cb = rc[0:tp, j, :].unsqueeze(1).unsqueeze(1).to_broadcast([tp, 8, 2, 16])
                sbb = rs[0:tp, j, :].unsqueeze(1).unsqueeze(1).to_broadcast([tp, 8, 2, 16])
                ra = junkf[0:tp, 0:256].rearrange("p (a k b) -> p a k b", a=8, k=2)
                rb_ = junkf[0:tp, 256:512].rearrange("p (a k b) -> p a k b", a=8, k=2)
                rc_ = junkf[0:tp, 512:768].rearrange("p (a k b) -> p a k b", a=8, k=2)
                rd_ = junkf[0:tp, 768:1024].rearrange("p (a k b) -> p a k b", a=8, k=2)
                V('tensor_tensor', ['xblk', 'rc'], ['junkf'], out=ra, in0=x1, in1=cb, op=OP.mult)
                V('tensor_tensor', ['xblk', 'rs'], ['junkf'], out=rb_, in0=x2, in1=sbb, op=OP.mult)
                V('tensor_tensor', ['xblk', 'rc'], ['junkf'], out=rc_, in0=x2, in1=cb, op=OP.mult)
                V('tensor_tensor', ['xblk', 'rs'], ['junkf'], out=rd_, in0=x1, in1=sbb, op=OP.mult)
                V('tensor_tensor', ['junkf'], ['xblk'], out=x1, in0=ra, in1=rb_, op=OP.subtract)
                V('tensor_tensor', ['junkf'], ['xblk'], out=x2, in0=rc_, in1=rd_, op=OP.add)
                V('tensor_copy', ['xblk'], ['hbf'], out=hbf[0:tp, j, :], in_=xblk[0:tp, j, :])
            for h in range(8):
                pb_, pbk_ = getpsb()
                for j in range(nt):
                    T('transpose', ['hbf', 'ident_b'], [pbk_], signal=(j == nt - 1), out=pb_[:, j * tp:(j + 1) * tp],
                      in_=hbf[0:tp, j, h * 128:(h + 1) * 128], identity=ident_b[0:tp, 0:tp])
                A('copy', [pbk_], ['hT'], out=hT[:, h, 0:ntok], in_=pb_[:, 0:ntok])
            pg.dma('pool', qT_d[:, :, t0:t0 + ntok].rearrange("a p t -> p a t"), hT[:, :, 0:ntok], reads=['hT'], writes=['qT_d'])
            V('tensor_copy', ['kvn'], ['kvn_bf'], out=kvn_bf[0:tp, 0:nt, :], in_=kvn[0:tp, 0:nt, :])
            V('tensor_copy', ['pen'], ['pen_bf'], out=pen_bf[0:tp, 0:nt, :], in_=pen[0:tp, 0:nt, :])
            pg.dma('pool', vtok_d[t0:t0 + ntok, :].rearrange("(j p) c -> p j c", p=tp), kvn_bf[0:tp, 0:nt, :], reads=['kvn_bf'], writes=['vtok_d'])
            pb_, pbk_ = getpsb()
            for j in range(nt):
                T('transpose', ['kvn_bf', 'ident_b'], [pbk_], signal=(j == nt - 1), out=pb_[:, j * tp:(j + 1) * tp],
                  in_=kvn_bf[0:tp, j, :], identity=ident_b[0:tp, 0:tp])
            A('copy', [pbk_], ['kT_blk'], out=kT_blk[:, 0:ntok], in_=pb_[:, 0:ntok])
            pg.dma('pool', kT_d[:, t0:t0 + ntok], kT_blk[:, 0:ntok], reads=['kT_blk'], writes=['kT_d'])
            pb_, pbk_ = getpsb()
            for j in range(nt):
                T('transpose', ['pen_bf', 'ident_b'], [pbk_], signal=(j == nt - 1), out=pb_[0:32, j * tp:(j + 1) * tp],
                  in_=pen_bf[0:tp, j, :], identity=ident_b[0:tp, 0:tp])
            A('copy', [pbk_], ['peT_blk'], out=peT_blk[:, 0:ntok], in_=pb_[0:32, 0:ntok])
            pg.dma('pool', peT_d[:, t0:t0 + ntok], peT_blk[:, 0:ntok], reads=['peT_blk'], writes=['peT_d'])
        if stop_after == 'A0':
            break
        pg.barrier()
        passB_setup(l)
        for (t0, tp, nt) in blocks:
            if tp == 64:
                passB_sample(l, xsrc, xkey)
            else:
                passB_prompt_block(l, t0, xsrc, xkey)
        if stop_after == 'B0':
            break
        pg.barrier()
        passC_setup(l)
        for (t0, tp, nt) in blocks:
            if tp == 64:
                passC_block(l, t0, 64, 1)
            else:
                passC_block(l, t0, 128, 2)
                passC_block(l, t0 + 256, 128, 2)

    pg.finish()
    pg.replay()
    return nc, pg


def rope_tables():
    half = 16
    inv = (10000.0 ** (-np.arange(half, dtype=np.float32) / half)).astype(np.float32)
    pos = np.concatenate([np.arange(SEQ, dtype=np.float32)] + [PAST + np.arange(DSEQ, dtype=np.float32)] * NSEQ)
    ang = pos.astype(np.float32)[:, None] * inv[None, :]
    return np.cos(ang).astype(np.float32), np.sin(ang).astype(np.float32)


_CACHE = {}


def kernel(**inp):
    f = lambda k: np.ascontiguousarray(np.asarray(inp[k], dtype=np.float32))
    nblk = os.environ.get("K_NBLK")
    nblk = int(nblk) if nblk else None
    stop = os.environ.get("K_STOP")
    key = (nblk, stop)
    if key not in _CACHE:
        _CACHE[key] = build(nblk, stop)
    nc, pg = _CACHE[key]
    xp = f('x_prompt')[0]
    xs = f('x_sample')
    cosT, sinT = rope_tables()
    rep = lambda a: np.ascontiguousarray(np.broadcast_to(a[:, None, :], (a.shape[0], 128, a.shape[1])))
    def sm_(a):
        return a.reshape(DEPTH, 8, 128).transpose(0, 2, 1)
    lam_re, lam_im = f('s5_lambda_re'), f('s5_lambda_im')
    ldt = np.repeat(f('s5_log_dt')[:, :, None], 64, axis=2)
    s5p = np.ascontiguousarray(np.stack([sm_(lam_re), sm_(lam_im), sm_(ldt)], axis=-1))
    def bm_(a):
        return a.reshape(DEPTH, 8, 128, 16).transpose(0, 2, 1, 3)
    s5b = np.ascontiguousarray(np.stack([bm_(f('s5_b_re')), bm_(f('s5_b_im'))], axis=3))
    cT = lambda a: a.transpose(0, 1, 3, 2)
    s5c = np.ascontiguousarray(np.stack([bm_(cT(f('s5_c_re'))), bm_(cT(f('s5_c_im')))], axis=3))
    s5d = np.ascontiguousarray(f('s5_d').reshape(DEPTH, 2, 128).transpose(0, 2, 1))
    sre, sim = f('state_s5_re'), f('state_s5_im')
    common = {
        "ropec": cosT, "ropes": sinT,
        "g1B": rep(f('norm1_g')), "w_in": f('w_in'), "kvgB": rep(f('mla_kv_norm_g')),
        "s5p": s5p, "s5b": s5b, "s5c": s5c, "s5d": s5d,
        "w_glu": f('s5_w_glu'),
        "b_glu": np.ascontiguousarray(f('s5_b_glu').reshape(DEPTH, 2, 128).transpose(0, 2, 1)),
        "lbl": np.ascontiguousarray(f('hgrn_lb_logits').reshape(DEPTH, 2, 128).transpose(2, 0, 1)),
        "qng": np.ascontiguousarray(f('mla_q_norm_g').reshape(DEPTH, 2, 128).transpose(0, 2, 1)),
        "ong": np.ascontiguousarray(f('out_norm_g').reshape(DEPTH, 8, 128).transpose(0, 2, 1)),
    }
    wq = f('mla_w_uq').reshape(DEPTH, 256, 8, 96)
    common["w_uq"] = np.ascontiguousarray(np.concatenate([wq[..., 64:96], wq[..., 64:96], wq[..., 0:64]], axis=-1).reshape(DEPTH, 256, 1024))
    shg = f('state_hgrn')
    wuk = f('mla_w_uk')
    wukT = np.zeros((DEPTH, 128, 8, 128), np.float32)
    wukT[:, 64:128] = wuk.transpose(0, 3, 2, 1)
    common.update({
        "w_out": f('w_out'), "w_ukT": wukT, "w_uv": f('mla_w_uv'),
        "g2B": rep(f('norm2_g')), "gfB": np.ascontiguousarray(np.broadcast_to(f('final_norm_g')[None, :], (128, D))),
        "w_up": f('w_up'), "w_down": f('w_down'),
    })
    ckv_all, cpe_all = f('cache_mla_kv'), f('cache_mla_pe')
    in_maps = []
    for c in range(NCORES):
        m = dict(common)
        m["xin"] = np.ascontiguousarray(np.concatenate([xp, xs[c * NSEQ:(c + 1) * NSEQ].reshape(NSEQ * DSEQ, D)], axis=0))
        def st_(a):
            return a.reshape(DEPTH, NSEQ, 8, 128).transpose(0, 3, 2, 1)
        sl = slice(c * NSEQ, (c + 1) * NSEQ)
        m["cache_kv"] = np.ascontiguousarray(ckv_all[:, sl])
        m["cache_pe"] = np.ascontiguousarray(cpe_all[:, sl])
        m["st_hg"] = np.ascontiguousarray(shg[:, sl].reshape(DEPTH, NSEQ, 2, 2, 64, 64).transpose(0, 1, 3, 4, 2, 5).reshape(DEPTH, NSEQ, 128, 2, 64))
        m["st_s5"] = np.ascontiguousarray(np.stack([st_(sre[:, sl]), st_(sim[:, sl])], axis=3))
        in_maps.append(m)
    res = run_bass_kernel_spmd(nc, in_maps, core_ids=list(range(NCORES)))
    R = res.results
    y_p = R[0]["o_y"][:SEQ][None]
    y_s = np.concatenate([R[c]["o_y"][SEQ:].reshape(NSEQ, DSEQ, D) for c in range(NCORES)], axis=0)
    kv_p = R[0]["o_kv"][:, :SEQ][:, None]
    pe_p = R[0]["o_pe"][:, :SEQ][:, None]
    kv_s = np.concatenate([R[c]["o_kv"][:, SEQ:].reshape(DEPTH, NSEQ, DSEQ, 128) for c in range(NCORES)], axis=1)
    pe_s = np.concatenate([R[c]["o_pe"][:, SEQ:].reshape(DEPTH, NSEQ, DSEQ, 32) for c in range(NCORES)], axis=1)
    z = lambda *s: np.zeros(s, np.float32)
    def us_(a):
        return a.transpose(0, 2, 1).reshape(DEPTH, 16, 64)
    s5re_p = us_(R[0]["o_s5p"][:, :, :, 0, 0])[:, None]
    s5im_p = us_(R[0]["o_s5p"][:, :, :, 1, 0])[:, None]
    s5re_s = np.stack([us_(R[c]["o_s5s"][:, :, :, 0, q]) for c in range(NCORES) for q in range(NSEQ)], axis=1)
    s5im_s = np.stack([us_(R[c]["o_s5s"][:, :, :, 1, q]) for c in range(NCORES) for q in range(NSEQ)], axis=1)
    def uh_(a):
        sh = a.shape[:-3]
        return a.reshape(sh + (2, 64, 2, 64)).transpose(tuple(range(len(sh))) + (len(sh) + 2, len(sh), len(sh) + 1, len(sh) + 3)).reshape(sh + (4, 64, 64))
    hg_p = uh_(R[0]["o_hgp"])[:, None]
    hg_s = np.concatenate([uh_(R[c]["o_hgs"]) for c in range(NCORES)], axis=1)
    return (y_p, y_s, kv_p, pe_p, hg_p, s5re_p, s5im_p,
            kv_s, pe_s, hg_s, s5re_s, s5im_s)
```

```python
import os
from contextlib import ExitStack
import numpy as np
import concourse.bass as bass
import concourse.mybir as mybir
from concourse.bass_utils import run_bass_kernel_spmd

F32 = mybir.dt.float32
BF = mybir.dt.bfloat16
AF = mybir.ActivationFunctionType
OP = mybir.AluOpType
AX = mybir.AxisListType

NCORES = 8
D = 1024
SEQ = 16384
DEPTH = 2
NSEQ = 4
DSEQ = 16
PAST = 2048
INW = 1696
EPS = 1e-5
NDMASEM = 12


class Pg:
    def __init__(self, nc, es):
        self.nc = nc
        self.names = ['pe', 'act', 'dve', 'pool', 'sp']
        self.q = {k: [] for k in self.names}
        self.sem = {k: es.enter_context(nc.semaphore("sem_" + k)) for k in self.names}
        self.cnt = {k: 0 for k in self.names}
        self.dsem = [es.enter_context(nc.semaphore("dsem%d" % i)) for i in range(NDMASEM)]
        self.dcnt = [0] * NDMASEM
        self.dnext = 0
        self.known = {k: {} for k in self.names}
        self.lastw = {}
        self.readers = {}
        self.ninstr = 0

    def _semh(self, key):
        return self.sem[key] if isinstance(key, str) else self.dsem[key[1]]

    def _need(self, e, waits, ev):
        if ev is None:
            return
        k, v = ev
        if k == e and e == 'pe':
            return
        if self.known[e].get(k, 0) >= v:
            return
        if waits.get(k, 0) < v:
            waits[k] = v

    def _emit_waits(self, e, waits):
        for k, v in waits.items():
            h = self._semh(k)
            self.q[e].append(lambda eng, h=h, v=v: eng.wait_ge(h, v))
            self.known[e][k] = v
            self.ninstr += 1

    def _deps(self, e, reads, writes):
        waits = {}
        for k in reads:
            self._need(e, waits, self.lastw.get(k))
        for k in writes:
            self._need(e, waits, self.lastw.get(k))
            for ev in self.readers.get(k, {}).values():
                self._need(e, waits, ev)
        self._emit_waits(e, waits)

    def _record(self, e, ev, reads, writes):
        for k in reads:
            self.readers.setdefault(k, {})[e] = ev
        for k in writes:
            self.lastw[k] = ev
            self.readers[k] = {}

    def op(self, e, meth, reads=(), writes=(), signal=True, **kw):
        self._deps(e, reads, writes)
        if signal:
            self.cnt[e] += 1
            ev = (e, self.cnt[e])
            h = self.sem[e]
            self.q[e].append(lambda eng, meth=meth, kw=kw, h=h: getattr(eng, meth)(**kw).then_inc(h, 1))
        else:
            ev = (e, self.cnt[e] + 1)
            self.q[e].append(lambda eng, meth=meth, kw=kw: getattr(eng, meth)(**kw))
        self.ninstr += 1
        self._record(e, ev, reads, writes)

    def dma(self, e, out, in_, reads=(), writes=()):
        i = self.dnext
        self.dnext = (self.dnext + 1) % NDMASEM
        waits = {}
        if self.dcnt[i] > 0:
            self._need(e, waits, (('d', i), self.dcnt[i]))
        self._emit_waits(e, waits)
        self._deps(e, reads, writes)
        self.dcnt[i] += 16
        ev = (('d', i), self.dcnt[i])
        h = self.dsem[i]
        self.q[e].append(lambda eng, out=out, in_=in_, h=h: eng.dma_start(out=out, in_=in_).then_inc(h, 16))
        self.ninstr += 1
        self._record(e, ev, reads, writes)

    def barrier(self):
        for e in self.names:
            waits = {}
            for o in self.names:
                if o != e and self.cnt[o] > 0:
                    self._need(e, waits, (o, self.cnt[o]))
            for i in range(NDMASEM):
                if self.dcnt[i] > 0:
                    self._need(e, waits, (('d', i), self.dcnt[i]))
            self._emit_waits(e, waits)

    def finish(self):
        waits = {}
        for i in range(NDMASEM):
            if self.dcnt[i] > 0:
                self._need('sp', waits, (('d', i), self.dcnt[i]))
        self._emit_waits('sp', waits)

    def replay(self):
        nc = self.nc
        with nc.Block() as block:
            @block.tensor
            def _(eng):
                for c in self.q['pe']:
                    c(eng)

            @block.scalar
            def _(eng):
                for c in self.q['act']:
                    c(eng)

            @block.vector
            def _(eng):
                for c in self.q['dve']:
                    c(eng)

            @block.gpsimd
            def _(eng):
                for c in self.q['pool']:
                    c(eng)

            @block.sync
            def _(eng):
                for c in self.q['sp']:
                    c(eng)


def build(nblk_limit=None, stop_after=None):
    SKIP = os.environ.get('K_SKIP', '')
    nc = bass.Bass("TRN2", target_bir_lowering=False)
    es = ExitStack()
    pg = Pg(nc, es)

    def din(name, shape, dt=F32):
        return nc.dram_tensor(name, list(shape), dt, kind="ExternalInput")

    def dout(name, shape, dt=F32):
        return nc.dram_tensor(name, list(shape), dt, kind="ExternalOutput")

    PERSIST = {"ident_f", "ident_b", "ones_bf", "ones_f", "sqb", "rbc", "ong_sb", "ss", "rstd"}
    ARW = 51800
    arenaP = es.enter_context(nc.sbuf_tensor("arenaP", [128, 1300], F32))
    arenaW = es.enter_context(nc.sbuf_tensor("arenaW", [128, ARW], F32))
    cur = {'P': 0, 'W': 0}

    def sb(name, shape, dt=F32):
        which = 'P' if name in PERSIST else 'W'
        ar = arenaP if which == 'P' else arenaW
        esz = 2 if dt == BF else 4
        nel = int(np.prod(shape[1:]))
        nw = (nel * esz + 3) // 4
        off = cur[which]
        cur[which] = off + nw
        assert cur[which] <= (1300 if which == 'P' else ARW), (name, cur)
        ap = ar[0:shape[0], off:off + nw]
        if dt != F32:
            ap = ap.bitcast(dt)
            if esz == 2 and nel % 2 == 1:
                ap = ap[:, 0:nel]
        if len(shape) == 3:
            ap = ap.rearrange("p (a b) -> p a b", a=shape[1])
        elif len(shape) == 4:
            ap = ap.rearrange("p (a b c) -> p a b c", a=shape[1], b=shape[2])
        elif len(shape) == 5:
            ap = ap.rearrange("p (a b c d) -> p a b c d", a=shape[1], b=shape[2], c=shape[3])
        return ap

    def region():
        cur['W'] = 0

    def ps(name, shape, dt=F32):
        return es.enter_context(nc.psum_tensor(name, list(shape), dt))

    def V(meth, R, W, **kw):
        pg.op('dve', meth, R, W, **kw)

    def A(meth, R, W, **kw):
        pg.op('act', meth, R, W, **kw)

    def G(meth, R, W, **kw):
        pg.op('pool', meth, R, W, **kw)

    def T(meth, R, W, signal=True, **kw):
        pg.op('pe', meth, R, W, signal=signal, **kw)

    TALL = SEQ + NSEQ * DSEQ
    NS1 = 1 + NSEQ
    xin = din("xin", [TALL, D])
    ropec = din("ropec", [TALL, 16])
    ropes = din("ropes", [TALL, 16])
    g1B = din("g1B", [DEPTH, 128, D])
    w_in = din("w_in", [DEPTH, D, INW])
    kvgB = din("kvgB", [DEPTH, 128, 128])
    s5p = din("s5p", [DEPTH, 128, 8, 3])
    s5b = din("s5b", [DEPTH, 128, 8, 2, 16])
    s5c = din("s5c", [DEPTH, 128, 8, 2, 16])
    s5d = din("s5d", [DEPTH, 128, 2])
    st_s5 = din("st_s5", [DEPTH, 128, 8, 2, NSEQ])
    w_glu = din("w_glu", [DEPTH, 256, 256])
    b_glu = din("b_glu", [DEPTH, 128, 2])
    lbl = din("lbl", [128, DEPTH, 2])
    st_hg = din("st_hg", [DEPTH, NSEQ, 128, 2, 64])
    qng = din("qng", [DEPTH, 128, 2])
    w_uq = din("w_uq", [DEPTH, 256, 1024])
    ong = din("ong", [DEPTH, 128, 8])
    w_out = din("w_out", [DEPTH, D, D])
    w_ukT = din("w_ukT", [DEPTH, 128, 8, 128])
    w_uv = din("w_uv", [DEPTH, 128, 8, 64])
    cache_kv = din("cache_kv", [DEPTH, NSEQ, PAST, 128])
    cache_pe = din("cache_pe", [DEPTH, NSEQ, PAST, 32])
    g2B = din("g2B", [DEPTH, 128, D])
    gfB = din("gfB", [128, D])
    w_up = din("w_up", [DEPTH, D, 4 * D])
    w_down = din("w_down", [DEPTH, 4 * D, D])
    o_hgp = dout("o_hgp", [DEPTH, 128, 2, 64])
    o_hgs = dout("o_hgs", [DEPTH, NSEQ, 128, 2, 64])
    qT_d = nc.dram_tensor("qT_d", [8, 128, TALL], BF)
    kT_d = nc.dram_tensor("kT_d", [128, TALL], BF)
    peT_d = nc.dram_tensor("peT_d", [32, TALL], BF)
    vtok_d = nc.dram_tensor("vtok_d", [TALL, 128], BF)
    mixT_d = nc.dram_tensor("mixT_d", [4, 128, TALL], BF)
    xmid_d = nc.dram_tensor("xmid_d", [TALL, D], F32)
    xout_d = nc.dram_tensor("xout_d", [TALL, D], F32)
    o_y = dout("o_y", [TALL, D])
    o_kv = dout("o_kv", [DEPTH, TALL, 128])
    o_pe = dout("o_pe", [DEPTH, TALL, 32])
    o_s5p = dout("o_s5p", [DEPTH, 128, 8, 2, 1])
    o_s5s = dout("o_s5s", [DEPTH, 128, 8, 2, NSEQ])

    ident_f = sb("ident_f", [128, 128], F32)
    ident_b = sb("ident_b", [128, 128], BF)
    G('memset', [], ['ident_f'], ap=ident_f[:], constant=0.0)
    G('affine_select', ['ident_f'], ['ident_f'], out=ident_f[:], in_=ident_f[:], pattern=[[-1, 128]],
      compare_op=OP.not_equal, fill=1.0, base=0, channel_multiplier=1)
    V('tensor_copy', ['ident_f'], ['ident_b'], out=ident_b[:], in_=ident_f[:])

    NPS = 6
    psf = [ps("psf%d" % i, [128, 512], F32) for i in range(NPS)]
    psb = [ps("psb%d" % i, [128, 1024], BF) for i in range(2)]
    rr = {'f': 0, 'b': 0}

    psallow = [list(range(NPS))]

    def getps():
        al = psallow[0]
        rr['f'] = (rr['f'] + 1) % len(al)
        i = al[rr['f']]
        return psf[i], ('psf', i)

    def getpsb():
        i = rr['b']
        rr['b'] = (i + 1) % 2
        return psb[i], ('psb', i)

    blocks = [(b * 512, 128, 4) for b in range(SEQ // 512)] + [(SEQ, 64, 1)]
    if nblk_limit is not None:
        blocks = blocks[:nblk_limit] + blocks[-1:]

    xblk = sb("xblk", [128, 4, D], F32)
    hbf = sb("hbf", [128, 4, D], BF)
    hT = sb("hT", [128, 8, 512], BF)
    junkf = sb("junkf", [128, D], F32)
    ss = sb("ss", [128, 8], F32)
    rstd = sb("rstd", [128, 8], F32)
    g1sb = sb("g1sb", [128, D], F32)
    kvg = sb("kvg", [128, 128], F32)
    w_in_bf = sb("w_in_bf", [128, 8, INW], BF)
    wstage = sb("wstage", [128, 2048], F32)
    kvtm = sb("kvtm", [128, 4, 160], F32)
    kvn = sb("kvn", [128, 4, 128], F32)
    pen = sb("pen", [128, 4, 32], F32)
    rc = sb("rc", [128, 4, 16], F32)
    rs = sb("rs", [128, 4, 16], F32)
    tmp16 = sb("tmp16", [128, 4, 16], F32)
    tmp16b = sb("tmp16b", [128, 4, 16], F32)
    projT = sb("projT", [128, 10, 512], F32)
    uT_bf = sb("uT_bf", [128, 2, 512], BF)
    s5p_sb = sb("s5p_sb", [128, 8, 3], F32)
    s5b_sb = sb("s5b_sb", [128, 8, 2, 16], F32)
    s5c_sb = sb("s5c_sb", [128, 8, 2, 16], F32)
    s5d_sb = sb("s5d_sb", [128, 2], F32)
    sm = sb("sm", [128, 24, 8], F32)
    smi = sb("smi", [128, 8], mybir.dt.int32)
    bb = sb("bb", [128, 8, 2, 16], F32)
    Bm = sb("Bm", [128, 8, 2, 128], BF)
    Cm = sb("Cm", [128, 8, 2, 128], BF)
    Ec = sb("Ec", [128, 8, 128], F32)
    Es = sb("Es", [128, 8, 128], F32)
    rtab = sb("rtab", [128, 8, 128], F32)
    EcS = sb("EcS", [128, 8, 64], F32)
    EsS = sb("EsS", [128, 8, 64], F32)
    rtabS = sb("rtabS", [128, 8, 64], F32)
    abx = sb("abx", [128, 2, 8, NSEQ], F32)
    hcar = sb("hcar", [128, 2, 8, NSEQ], F32)
    hcarS = sb("hcarS", [128, 8, 2, NSEQ], F32)
    hcarP = sb("hcarP", [128, 8, 2, 1], F32)
    ftmp = sb("ftmp", [128, 6, 8, NSEQ], F32)
    bre = sb("bre", [128, 8, 128], F32)
    bim = sb("bim", [128, 8, 128], F32)
    t1 = sb("t1", [128, 8, 128], F32)
    t2 = sb("t2", [128, 8, 128], F32)
    t3 = sb("t3", [128, 8, 128], F32)
    t4 = sb("t4", [128, 8, 128], F32)
    BT = t4
    gre = sb("gre", [128, 8, 128], F32)
    gim = sb("gim", [128, 8, 128], F32)
    hre_bf = sb("hre_bf", [128, 8, 128], BF)
    him_bf = sb("him_bf", [128, 8, 128], BF)
    yT = sb("yT", [128, 2, 512], F32)
    tsm = sb("tsm", [128, 128], F32)

    ones_bf = sb("ones_bf", [128, 128], BF)
    V('memset', [], ['ones_bf'], ap=ones_bf[:], constant=1.0)
    ones_f = sb("ones_f", [128, 128], F32)
    V('memset', [], ['ones_f'], ap=ones_f[:], constant=1.0)
    cm64 = sb("cm64", [128, 512], F32)
    cm16 = sb("cm16", [128, 64], F32)
    cmask4 = sb("cmask4", [128, 4, 64], BF)
    cmaskS = sb("cmaskS", [16, 4, 16], BF)
    w_glu_bf = sb("w_glu_bf", [128, 2, 256], BF)
    b_glu_sb = sb("b_glu_sb", [128, 2], F32)
    lbl_sb = sb("lbl_sb", [128, DEPTH, 2], F32)
    lbw = sb("lbw", [128, 8, 2], F32)
    lb_sb = sb("lb_sb", [128, 2], F32)
    oml_sb = sb("oml_sb", [128, 2], F32)
    qng_sb = sb("qng_sb", [128, 2], F32)
    ong_sb = sb("ong_sb", [128, 8], F32)
    w_uq_bf = sb("w_uq_bf", [128, 2, 1024], BF)
    hgo = sb("hgo", [128, 2, 512], F32)
    vtm_bf = sb("vtm_bf", [128, 4, 256], BF)
    attm = sb("attm", [128, 4, 64], BF)
    kdt = sb("kdt", [128, 256], BF)
    Sst = sb("Sst", [128, 2, 64], F32)
    Sbd = sb("Sbd", [128, 2, 128], BF)
    rbc = sb("rbc", [128, 512], F32)
    sqb = sb("sqb", [128, 512], BF)
    mixA = sb("mixA", [128, 4, 512], BF)
    kvn_bf = sb("kvn_bf", [128, 4, 128], BF)
    pen_bf = sb("pen_bf", [128, 4, 32], BF)
    kT_blk = sb("kT_blk", [128, 512], BF)
    peT_blk = sb("peT_blk", [32, 512], BF)
    tmpo = sb("tmpo", [128, 2, 64], F32)

    def passA_consts():
        V('memset', [], ['cm64'], ap=cm64[:], constant=1.0)
        V('memset', ['cm64'], ['cm64'], ap=cm64[:, 0:512:64], constant=0.0)
        V('memset', [], ['cm16'], ap=cm16[:], constant=1.0)
        V('memset', ['cm16'], ['cm16'], ap=cm16[:, 0:64:16], constant=0.0)
        G('memset', [], ['cmask4'], ap=cmask4[:], constant=1.0)
        for hf in range(2):
            G('affine_select', ['cmask4'], ['cmask4'], out=cmask4[hf * 64:(hf + 1) * 64], in_=cmask4[hf * 64:(hf + 1) * 64],
              pattern=[[0, 4], [1, 64]], compare_op=OP.is_ge, fill=0.0, base=0, channel_multiplier=-1)
        G('memset', [], ['cmaskS'], ap=cmaskS[:], constant=1.0)
        G('affine_select', ['cmaskS'], ['cmaskS'], out=cmaskS[:], in_=cmaskS[:], pattern=[[0, 4], [1, 16]],
          compare_op=OP.is_ge, fill=0.0, base=0, channel_multiplier=-1)

    def load_weight_bf(dst, dstkey, src_ap_fn, ktiles, ncols, stage=None, stagekey='wstage'):
        stage = wstage if stage is None else stage
        n = 0
        for kt in range(ktiles):
            for c0 in range(0, ncols, 1024):
                c1 = min(ncols, c0 + 1024)
                so = (n % 2) * 1024
                n += 1
                pg.dma('sp', stage[:, so:so + c1 - c0], src_ap_fn(kt, c0, c1), writes=[stagekey + str(n % 2)])
                G('tensor_copy', [stagekey + str(n % 2)], [dstkey], out=dst[:, kt, c0:c1], in_=stage[:, so:so + c1 - c0])

    def rms_rstd(ss_ap, rstd_ap, n, keys_r, keys_w):
        V('tensor_scalar', keys_r, keys_w, out=rstd_ap, in0=ss_ap, scalar1=1.0 / n, scalar2=EPS, op0=OP.mult, op1=OP.add)
        A('activation', keys_w, keys_w, out=rstd_ap, in_=rstd_ap, func=AF.Sqrt)
        V('reciprocal', keys_w, keys_w, out=rstd_ap, in_=rstd_ap)

    PI = float(np.pi)
    K5 = ['s5set']

    def s5_setup(l):
        pg.dma('sp', s5p_sb[:], s5p[l], writes=K5)
        pg.dma('sp', s5b_sb[:], s5b[l], writes=K5)
        pg.dma('sp', s5c_sb[:], s5c[l], writes=K5)
        pg.dma('sp', s5d_sb[:], s5d[l], writes=K5)
        lre = s5p_sb[:, :, 0]
        lim = s5p_sb[:, :, 1]
        ldt = s5p_sb[:, :, 2]
        r = lambda i: sm[:, i, :]
        DT, MAG, TH, KF, R0, M1, SIN, COS, ABR, ABI, DEN, LR, LI, ZR, ZI, WC, WS, X1, X2 = range(19)
        A('activation', K5, K5, out=r(DT), in_=ldt, func=AF.Exp)
        V('tensor_tensor', K5, K5, out=r(MAG), in0=lre, in1=r(DT), op=OP.mult)
        A('activation', K5, K5, out=r(MAG), in_=r(MAG), func=AF.Exp)
        V('tensor_tensor', K5, K5, out=r(TH), in0=lim, in1=r(DT), op=OP.mult)
        V('tensor_scalar', K5, K5, out=smi[:], in0=r(TH), scalar1=1.0 / (2 * PI), scalar2=None, op0=OP.mult)
        V('tensor_copy', K5, K5, out=r(KF), in_=smi[:])
        V('scalar_tensor_tensor', K5, K5, out=r(R0), in0=r(KF), scalar=-2 * PI, in1=r(TH), op0=OP.mult, op1=OP.add)
        V('tensor_scalar', K5, K5, out=r(M1), in0=r(R0), scalar1=-PI, scalar2=1e30, op0=OP.add, op1=OP.mult)
        V('tensor_scalar', K5, K5, out=r(M1), in0=r(M1), scalar1=0.0, scalar2=1.0, op0=OP.max, op1=OP.min)
        V('scalar_tensor_tensor', K5, K5, out=r(R0), in0=r(M1), scalar=-2 * PI, in1=r(R0), op0=OP.mult, op1=OP.add)
        A('activation', K5, K5, out=r(SIN), in_=r(R0), func=AF.Sin)
        A('activation', K5, K5, out=r(X1), in_=r(R0), func=AF.Abs)
        V('tensor_scalar', K5, K5, out=r(X1), in0=r(X1), scalar1=-1.0, scalar2=PI / 2, op0=OP.mult, op1=OP.add)
        A('activation', K5, K5, out=r(COS), in_=r(X1), func=AF.Sin)
        V('tensor_tensor', K5, K5, out=r(ABR), in0=r(MAG), in1=r(COS), op=OP.mult)
        V('tensor_tensor', K5, K5, out=r(ABI), in0=r(MAG), in1=r(SIN), op=OP.mult)
        V('tensor_tensor', K5, K5, out=r(DEN), in0=lre, in1=lre, op=OP.mult)
        V('tensor_tensor', K5, K5, out=r(X1), in0=lim, in1=lim, op=OP.mult)
        V('tensor_tensor', K5, K5, out=r(DEN), in0=r(DEN), in1=r(X1), op=OP.add)
        V('reciprocal', K5, K5, out=r(DEN), in_=r(DEN))
        V('tensor_tensor', K5, K5, out=r(LR), in0=lre, in1=r(DEN), op=OP.mult)
        V('scalar_tensor_tensor', K5, K5, out=r(LI), in0=lim, scalar=-1.0, in1=r(DEN), op0=OP.mult, op1=OP.mult)
        V('tensor_scalar', K5, K5, out=r(X1), in0=r(ABR), scalar1=-1.0, scalar2=None, op0=OP.add)
        V('tensor_tensor', K5, K5, out=r(ZR), in0=r(X1), in1=r(LR), op=OP.mult)
        V('tensor_tensor', K5, K5, out=r(X2), in0=r(ABI), in1=r(LI), op=OP.mult)
        V('tensor_tensor', K5, K5, out=r(ZR), in0=r(ZR), in1=r(X2), op=OP.subtract)
        V('tensor_tensor', K5, K5, out=r(ZI), in0=r(X1), in1=r(LI), op=OP.mult)
        V('tensor_tensor', K5, K5, out=r(X2), in0=r(ABI), in1=r(LR), op=OP.mult)
        V('tensor_tensor', K5, K5, out=r(ZI), in0=r(ZI), in1=r(X2), op=OP.add)
        for st in range(8):
            zr = sm[:, ZR, st:st + 1]
            zi = sm[:, ZI, st:st + 1]
            V('tensor_scalar', K5, K5, out=tsm[:, 0:16], in0=s5b_sb[:, st, 1, :], scalar1=zi, scalar2=None, op0=OP.mult)
            V('scalar_tensor_tensor', K5, K5, out=bb[:, st, 0, :], in0=s5b_sb[:, st, 0, :], scalar=zr, in1=tsm[:, 0:16],
              op0=OP.mult, op1=OP.subtract)
            V('tensor_scalar', K5, K5, out=tsm[:, 0:16], in0=s5b_sb[:, st, 0, :], scalar1=zi, scalar2=None, op0=OP.mult)
            V('scalar_tensor_tensor', K5, K5, out=bb[:, st, 1, :], in0=s5b_sb[:, st, 1, :], scalar=zr, in1=tsm[:, 0:16],
              op0=OP.mult, op1=OP.add)
        for ri in range(2):
            V('memset', K5, K5, ap=BT[:], constant=0.0)
            for st in range(8):
                q = st % 4
                V('tensor_copy', K5, K5, out=BT[0:64, st, 32 * q:32 * q + 16], in_=bb[0:64, st, ri, :])
                V('tensor_copy', K5, K5, out=BT[64:128, st, 32 * q + 16:32 * q + 32], in_=bb[64:128, st, ri, :])
            for st in range(8):
                p_, pk = getps()
                T('transpose', K5 + ['ident_f'], [pk], out=p_[:, 0:128], in_=BT[:, st, :], identity=ident_f[:])
                A('copy', [pk], K5, out=Bm[:, st, ri, :], in_=p_[:, 0:128])
        V('memset', K5, K5, ap=Cm[:], constant=0.0)
        for ri in range(2):
            for st in range(8):
                q = st % 4
                sc = 1.0 if ri == 0 else -1.0
                V('tensor_scalar', K5, K5, out=Cm[0:64, st, ri, 32 * q:32 * q + 16], in0=s5c_sb[0:64, st, ri, :],
                  scalar1=sc, scalar2=None, op0=OP.mult)
                V('tensor_scalar', K5, K5, out=Cm[64:128, st, ri, 32 * q + 16:32 * q + 32], in0=s5c_sb[64:128, st, ri, :],
                  scalar1=sc, scalar2=None, op0=OP.mult)
        V('memset', K5, K5, ap=Ec[:, :, 0:1], constant=1.0)
        V('memset', K5, K5, ap=Es[:, :, 0:1], constant=0.0)
        V('tensor_copy', K5, K5, out=r(WC), in_=r(COS))
        V('tensor_copy', K5, K5, out=r(WS), in_=r(SIN))
        s = 1
        while s < 128:
            for st in range(8):
                wc = sm[:, WC, st:st + 1]
                ws = sm[:, WS, st:st + 1]
                V('tensor_scalar', K5, K5, out=tsm[:, 0:s], in0=Es[:, st, 0:s], scalar1=ws, scalar2=None, op0=OP.mult)
                V('scalar_tensor_tensor', K5, K5, out=Ec[:, st, s:2 * s], in0=Ec[:, st, 0:s], scalar=wc, in1=tsm[:, 0:s],
                  op0=OP.mult, op1=OP.subtract)
                V('tensor_scalar', K5, K5, out=tsm[:, 0:s], in0=Es[:, st, 0:s], scalar1=wc, scalar2=None, op0=OP.mult)
                V('scalar_tensor_tensor', K5, K5, out=Es[:, st, s:2 * s], in0=Ec[:, st, 0:s], scalar=ws, in1=tsm[:, 0:s],
                  op0=OP.mult, op1=OP.add)
            V('tensor_tensor', K5, K5, out=r(X1), in0=r(WC), in1=r(WC), op=OP.mult)
            V('tensor_tensor', K5, K5, out=r(X2), in0=r(WS), in1=r(WS), op=OP.mult)
            V('tensor_tensor', K5, K5, out=r(X2), in0=r(X1), in1=r(X2), op=OP.subtract)
            V('scalar_tensor_tensor', K5, K5, out=r(WS), in0=r(WC), scalar=2.0, in1=r(WS), op0=OP.mult, op1=OP.mult)
            V('tensor_copy', K5, K5, out=r(WC), in_=r(X2))
            s *= 2
        for st in range(8):
            V('tensor_copy', K5, K5, out=rtab[:, st, :], in_=sm[:, MAG, st:st + 1].to_broadcast([128, 128]))
            V('tensor_copy', K5, K5, out=rtabS[:, st, :], in_=sm[:, MAG, st:st + 1].to_broadcast([128, 64]))
        V('memset', K5, K5, ap=rtab[:, :, 0:1], constant=0.0)
        for q in range(NSEQ):
            V('memset', K5, K5, ap=rtabS[:, :, 16 * q:16 * q + 1], constant=0.0)
            V('tensor_copy', K5, K5, out=EcS[:, :, 16 * q:16 * q + 16], in_=Ec[:, :, 0:16])
            V('tensor_copy', K5, K5, out=EsS[:, :, 16 * q:16 * q + 16], in_=Es[:, :, 0:16])
            V('tensor_copy', K5, K5, out=abx[:, 0, :, q], in_=r(ABR))
            V('tensor_copy', K5, K5, out=abx[:, 1, :, q], in_=r(ABI))
        V('memset', K5, ['hcar'], ap=hcar[:], constant=0.0)

    def s5_chunk(l, c0, cw, nseg, sample):
        seglen = cw // nseg
        ec = (EcS if sample else Ec)[:, :, 0:cw]
        esn = (EsS if sample else Es)[:, :, 0:cw]
        rt = (rtabS if sample else rtab)[:, :, 0:cw]
        pks = []
        for ri in range(2):
            for hb in range(2):
                p_, pk = getps()
                pks.append((p_, pk, ri, hb))
                for sti in range(4):
                    st = hb * 4 + sti
                    T('matmul', K5 + ['uT_bf'], [pk], signal=(sti == 3), out=p_[:, sti * cw:(sti + 1) * cw], lhsT=Bm[:, st, ri, :],
                      rhs=uT_bf[:, st // 4, c0:c0 + cw], start=True, stop=True)
        for (p_, pk, ri, hb) in pks:
            dst = (bre if ri == 0 else bim)
            A('copy', [pk], ['bre' if ri == 0 else 'bim'], out=dst[:, hb * 4:hb * 4 + 4, 0:cw],
              in_=p_[:, 0:4 * cw].rearrange("p (a b) -> p a b", a=4))
        hp_r = hcar[:, 0, :, 0:nseg]
        hp_i = hcar[:, 1, :, 0:nseg]
        ar = abx[:, 0, :, 0:nseg]
        ai = abx[:, 1, :, 0:nseg]
        f = lambda i: ftmp[:, i, :, 0:nseg]
        sv = lambda t: t[:, :, 0:cw:seglen]
        ev_ = lambda t: t[:, :, seglen - 1:cw:seglen]
        V('tensor_tensor', ['hcar'] + K5, ['ftmp'], out=f(0), in0=ar, in1=hp_r, op=OP.mult)
        V('tensor_tensor', ['hcar'] + K5, ['ftmp'], out=f(1), in0=ai, in1=hp_i, op=OP.mult)
        V('tensor_tensor', ['ftmp'], ['ftmp'], out=f(0), in0=f(0), in1=f(1), op=OP.subtract)
        V('tensor_tensor', ['ftmp', 'bre'], ['bre'], out=sv(bre), in0=sv(bre), in1=f(0), op=OP.add)
        V('tensor_tensor', ['hcar'] + K5, ['ftmp'], out=f(2), in0=ar, in1=hp_i, op=OP.mult)
        V('tensor_tensor', ['hcar'] + K5, ['ftmp'], out=f(3), in0=ai, in1=hp_r, op=OP.mult)
        V('tensor_tensor', ['ftmp'], ['ftmp'], out=f(2), in0=f(2), in1=f(3), op=OP.add)
        V('tensor_tensor', ['ftmp', 'bim'], ['bim'], out=sv(bim), in0=sv(bim), in1=f(2), op=OP.add)
        W_ = lambda t: t[:, :, 0:cw]
        V('tensor_tensor', ['bre'] + K5, ['t1'], out=W_(t1), in0=W_(bre), in1=ec, op=OP.mult)
        V('tensor_tensor', ['bim'] + K5, ['t2'], out=W_(t2), in0=W_(bim), in1=esn, op=OP.mult)
        G('tensor_tensor', ['bim'] + K5, ['t3'], out=W_(t3), in0=W_(bim), in1=ec, op=OP.mult)
        G('tensor_tensor', ['bre'] + K5, ['t4'], out=W_(t4), in0=W_(bre), in1=esn, op=OP.mult)
        V('tensor_tensor', ['t1', 't2'], ['t1'], out=W_(t1), in0=W_(t1), in1=W_(t2), op=OP.add)
        G('tensor_tensor', ['t3', 't4'], ['t3'], out=W_(t3), in0=W_(t3), in1=W_(t4), op=OP.subtract)
        fl = lambda ap: ap.rearrange("p a b -> p (a b)")
        if cw == 128:
            V('tensor_tensor_scan', ['t1'] + K5, ['gre'], out=fl(gre[:]), data0=fl(rtab[:]), data1=fl(t1[:]), initial=0.0,
              op0=OP.mult, op1=OP.add)
            V('tensor_tensor_scan', ['t3'] + K5, ['gim'], out=fl(gim[:]), data0=fl(rtab[:]), data1=fl(t3[:]), initial=0.0,
              op0=OP.mult, op1=OP.add)
        else:
            for st in range(8):
                V('tensor_tensor_scan', ['t1'] + K5, ['gre'], out=gre[:, st, 0:cw], data0=rt[:, st, :], data1=t1[:, st, 0:cw],
                  initial=0.0, op0=OP.mult, op1=OP.add)
                V('tensor_tensor_scan', ['t3'] + K5, ['gim'], out=gim[:, st, 0:cw], data0=rt[:, st, :], data1=t3[:, st, 0:cw],
                  initial=0.0, op0=OP.mult, op1=OP.add)
        V('tensor_tensor', ['gre'] + K5, ['t1'], out=W_(t1), in0=W_(gre), in1=ec, op=OP.mult)
        V('tensor_tensor', ['gim'] + K5, ['t2'], out=W_(t2), in0=W_(gim), in1=esn, op=OP.mult)
        G('tensor_tensor', ['gre'] + K5, ['t3'], out=W_(t3), in0=W_(gre), in1=esn, op=OP.mult)
        G('tensor_tensor', ['gim'] + K5, ['t4'], out=W_(t4), in0=W_(gim), in1=ec, op=OP.mult)
        V('tensor_tensor', ['t1', 't2'], ['hre_bf'], out=W_(hre_bf), in0=W_(t1), in1=W_(t2), op=OP.subtract)
        G('tensor_tensor', ['t3', 't4'], ['him_bf'], out=W_(him_bf), in0=W_(t3), in1=W_(t4), op=OP.add)
        V('tensor_tensor', ['t1', 't2'], ['hcar'], out=hcar[:, 0, :, 0:nseg], in0=ev_(t1), in1=ev_(t2), op=OP.subtract)
        V('tensor_tensor', ['t3', 't4'], ['hcar'], out=hcar[:, 1, :, 0:nseg], in0=ev_(t3), in1=ev_(t4), op=OP.add)
        for m in range(2):
            p_, pk = getps()
            n = 0
            for sti in range(4):
                st = m * 4 + sti
                for ri in range(2):
                    T('matmul', K5 + ['hre_bf', 'him_bf'], [pk], signal=(n == 7), out=p_[:, 0:cw], lhsT=Cm[:, st, ri, :],
                      rhs=(hre_bf if ri == 0 else him_bf)[:, st, 0:cw], start=(n == 0), stop=(n == 7))
                    n += 1
            V('scalar_tensor_tensor', [pk, 'projT'] + K5, ['yT'], out=yT[:, m, c0:c0 + cw], in0=projT[:, m, c0:c0 + cw],
              scalar=s5d_sb[:, m:m + 1], in1=p_[:, 0:cw], op0=OP.mult, op1=OP.add)

    def fm_rstd(src, ntile, n, ntok, srckeys):
        p_, pk = getps()
        for t in range(ntile):
            A('activation', srckeys, ['sqb'], out=sqb[:, 0:ntok], in_=src[:, t, 0:ntok], func=AF.Square)
            T('matmul', ['sqb', 'ones_bf'], [pk], out=p_[:, 0:ntok], lhsT=ones_bf[:], rhs=sqb[:, 0:ntok],
              start=(t == 0), stop=(t == ntile - 1))
        rms_rstd(p_[:, 0:ntok], rbc[:, 0:ntok], n, [pk], ['rbc'])

    def hg_setup(l):
        pg.dma('sp', lbl_sb[:], lbl[:, :, :], writes=['lbl_sb'])
        e0, e1, tot, p0, p1, cum = [lbw[:, i, :] for i in range(6)]
        A('activation', ['lbl_sb'], ['lbw'], out=e0, in_=lbl_sb[:, 0, :], func=AF.Exp)
        A('activation', ['lbl_sb'], ['lbw'], out=e1, in_=lbl_sb[:, 1, :], func=AF.Exp)
        V('tensor_tensor', ['lbw'], ['lbw'], out=tot, in0=e0, in1=e1, op=OP.add)
        V('reciprocal', ['lbw'], ['lbw'], out=tot, in_=tot)
        V('tensor_tensor', ['lbw'], ['lbw'], out=p0, in0=e0, in1=tot, op=OP.mult)
        V('tensor_tensor', ['lbw'], ['lbw'], out=p1, in0=e1, in1=tot, op=OP.mult)
        V('tensor_copy', ['lbw'], ['lbw'], out=cum, in_=p0)
        if l == 1:
            V('tensor_tensor', ['lbw'], ['lbw'], out=cum, in0=cum, in1=p1, op=OP.add)
        V('tensor_tensor', ['lbw'], ['lb_sb'], out=lb_sb[:], in0=cum, in1=p0, op=OP.subtract)
        V('tensor_scalar', ['lb_sb'], ['oml_sb'], out=oml_sb[:], in0=lb_sb[:], scalar1=-1.0, scalar2=1.0, op0=OP.mult, op1=OP.add)
        V('memset', [], ['Sbd'], ap=Sbd[:], constant=0.0)
        V('memset', [], ['Sst'], ap=Sst[:], constant=0.0)

    def hg_block(l, ntok, C, sample, t0):
        ncnk = ntok // C
        mid = C // 2 - 1
        W = lambda t: t[:].rearrange("p a b -> p (a b)").rearrange("p (a b) -> p a b", a=2)[:, :, 0:ntok]
        fsg, logf, kf, qf, bcs, dd, ee = W(bre), W(bim), W(t1), W(t2), W(t3), W(t4), W(gre)
        gb = gim[:].rearrange("p a b -> p (a b)").bitcast(BF)
        qtil = gb[:, 0:1024].rearrange("p (a b) -> p a b", a=2)[:, :, 0:ntok]
        ktil = gb[:, 1024:2048].rearrange("p (a b) -> p a b", a=2)[:, :, 0:ntok]
        qdec = hre_bf[:].rearrange("p a b -> p (a b)").rearrange("p (a b) -> p a b", a=2)[:, :, 0:ntok]
        kdec = him_bf[:].rearrange("p a b -> p (a b)").rearrange("p (a b) -> p a b", a=2)[:, :, 0:ntok]
        qT_ = projT[:, 2:4, 0:ntok]
        fT_ = projT[:, 4:6, 0:ntok]
        cm = (cm16 if sample else cm64)[:, 0:ntok]
        A('activation', ['projT'], ['bre'], out=fsg, in_=fT_, func=AF.Sigmoid)
        for tl in range(2):
            V('tensor_scalar', ['bre', 'oml_sb', 'lb_sb'], ['bre'], out=fsg[:, tl, :], in0=fsg[:, tl, :], scalar1=oml_sb[:, tl:tl + 1],
              scalar2=lb_sb[:, tl:tl + 1], op0=OP.mult, op1=OP.add)
        A('activation', ['bre'], ['bim'], out=logf, in_=fsg, func=AF.Ln)
        V('tensor_scalar', ['bre'], ['t1'], out=kf, in0=fsg, scalar1=-1.0, scalar2=1.0, op0=OP.mult, op1=OP.add)
        A('activation', ['projT'], ['t2'], out=qf, in_=qT_, func=AF.Silu)
        for tl in range(2):
            V('tensor_tensor_scan', ['bim', 'cm64', 'cm16'], ['t3'], out=bcs[:, tl, :], data0=cm, data1=logf[:, tl, :], initial=0.0,
              op0=OP.mult, op1=OP.add)
        c3 = lambda ap: ap.rearrange("p (a b) -> p a b", b=C)
        for tl in range(2):
            bv = c3(bcs[:, tl, :])
            V('tensor_tensor', ['t3'], ['t4'], out=c3(dd[:, tl, :]), in0=bv, in1=bv[:, :, mid:mid + 1].to_broadcast([128, ncnk, C]),
              op=OP.subtract)
        A('activation', ['t4'], ['gre'], out=ee, in_=dd, func=AF.Exp)
        V('tensor_tensor', ['t2', 'gre'], ['gim'], out=qtil, in0=qf, in1=ee, op=OP.mult)
        A('activation', ['t4'], ['gre'], out=ee, in_=dd, func=AF.Exp, scale=-1.0)
        V('tensor_tensor', ['t1', 'gre'], ['gim'], out=ktil, in0=kf, in1=ee, op=OP.mult)
        for tl in range(2):
            bv = c3(bcs[:, tl, :])
            V('tensor_tensor', ['t3'], ['t4'], out=c3(dd[:, tl, :]), in0=bv[:, :, C - 1:C].to_broadcast([128, ncnk, C]), in1=bv,
              op=OP.subtract)
        A('activation', ['t4'], ['gre'], out=ee, in_=dd, func=AF.Exp)
        V('tensor_tensor', ['t1', 'gre'], ['him_bf'], out=kdec, in0=kf, in1=ee, op=OP.mult)
        A('activation', ['t3'], ['gre'], out=ee, in_=bcs, func=AF.Exp)
        V('tensor_tensor', ['t2', 'gre'], ['hre_bf'], out=qdec, in0=qf, in1=ee, op=OP.mult)
        for ci in range(ncnk):
            if 'c' in SKIP:
                break
            c0 = ci * C
            if sample:
                j, rb = ci, 0
                pg.dma('sp', Sst[:], st_hg[l, ci], writes=['Sst'])
                for tl in range(2):
                    A('copy', ['Sst'], ['Sbd'], out=Sbd[0:64, tl, 0:64], in_=Sst[0:64, tl, :])
                    A('copy', ['Sst'], ['Sbd'], out=Sbd[64:128, tl, 64:128], in_=Sst[64:128, tl, :])
                msk = cmaskS[0:C]
            else:
                j, rb = c0 // 128, c0 % 128
                msk = cmask4[rb:rb + C]
            for par in range(2):
                pa, pak = getps()
                pb = par * 64
                for tl in range(2):
                    T('matmul', ['gim'], [pak], signal=(tl == 1), out=pa[rb:rb + C, tl * C:(tl + 1) * C], lhsT=ktil[pb:pb + 64, tl, c0:c0 + C],
                      rhs=qtil[pb:pb + 64, tl, c0:c0 + C], start=True, stop=True)
                V('tensor_tensor', [pak, 'cmask4', 'cmaskS'], ['attm'], out=attm[rb:rb + C, par:4:2, 0:C],
                  in0=pa[rb:rb + C, 0:2 * C].rearrange("p (a b) -> p a b", a=2), in1=msk[:, 0:2, :], op=OP.mult)
            if 'p' in SKIP:
                continue
            pbk, pbkk = getpsb()
            for tl in range(2):
                T('transpose', ['him_bf', 'ident_b'], [pbkk], signal=(tl == 1), out=pbk[rb:rb + C, tl * 128:(tl + 1) * 128],
                  in_=kdec[:, tl, c0:c0 + C], identity=ident_b[:])
            A('copy', [pbkk], ['kdt'], out=kdt[rb:rb + C, :], in_=pbk[rb:rb + C, 0:256])
            if 'u' in SKIP:
                continue
            pu, puk = getps()
            for h in range(4):
                tl, pb = h // 2, (h % 2) * 64
                T('matmul', ['kdt', 'vtm_bf'], [puk], signal=(h == 3), out=pu[pb:pb + 64, tl * 64:(tl + 1) * 64],
                  lhsT=kdt[rb:rb + C, h * 64:(h + 1) * 64], rhs=vtm_bf[rb:rb + C, j, h * 64:(h + 1) * 64], start=True, stop=True)
            if 'o' in SKIP:
                continue
            po1, po1k = getps()
            for tl in range(2):
                T('matmul', ['Sbd', 'hre_bf'], [po1k], signal=(tl == 1), out=po1[:, tl * C:(tl + 1) * C], lhsT=Sbd[:, tl, :],
                  rhs=qdec[:, tl, c0:c0 + C], start=True, stop=True)
            if 'x' not in SKIP:
                po2, po2k = getps()
                for h in range(4):
                    tl, pb = h // 2, (h % 2) * 64
                    T('matmul', ['vtm_bf', 'attm'], [po2k], signal=(h == 3), out=po2[pb:pb + 64, tl * C:(tl + 1) * C],
                      lhsT=vtm_bf[rb:rb + C, j, h * 64:(h + 1) * 64], rhs=attm[rb:rb + C, h, 0:C], start=True, stop=True)
                A('copy', [po1k], ['tmpo'], out=tmpo[:, :, 0:C], in_=po1[:, 0:2 * C].rearrange("p (a b) -> p a b", a=2))
                V('tensor_tensor', ['tmpo', po2k], ['hgo'], out=hgo[:, :, c0:c0 + C], in0=tmpo[:, :, 0:C],
                  in1=po2[:, 0:2 * C].rearrange("p (a b) -> p a b", a=2), op=OP.add)
            if 'y' not in SKIP:
                for tl in range(2):
                    V('scalar_tensor_tensor', ['Sst', 'gre', puk], ['Sst'], out=Sst[:, tl, :], in0=Sst[:, tl, :],
                      scalar=ee[:, tl, c0 + C - 1:c0 + C], in1=pu[:, tl * 64:(tl + 1) * 64], op0=OP.mult, op1=OP.add)
                    A('copy', ['Sst'], ['Sbd'], out=Sbd[0:64, tl, 0:64], in_=Sst[0:64, tl, :])
                    A('copy', ['Sst'], ['Sbd'], out=Sbd[64:128, tl, 64:128], in_=Sst[64:128, tl, :])
            if sample:
                pg.dma('pool', o_hgs[l, ci], Sst[:], reads=['Sst'])
        if (not sample) and (t0 + ntok == SEQ or (nblk_limit is not None and t0 + ntok == nblk_limit * 512)):
            pg.dma('pool', o_hgp[l], Sst[:], reads=['Sst'])


    region()
    HS = SEQ // 2
    b_kT = sb("b_kT", [128, SEQ], BF)
    b_peT = [sb("b_peT0", [128, HS], BF), sb("b_peT1", [128, HS], BF)]
    b_v = sb("b_v", [128, SEQ // 128, 128], BF)
    b_wout = sb("b_wout", [128, 8, D], BF)
    b_wukT = sb("b_wukT", [128, 8, 128], BF)
    b_wuv = sb("b_wuv", [128, 8, 64], BF)
    b_dmask = sb("b_dmask", [128, 4, 512], BF)
    b_qT = sb("b_qT", [128, 8, 512], BF)
    b_qlat = sb("b_qlat", [128, 8, 512], BF)
    b_pT = [sb("b_pT0", [128, 512], BF), sb("b_pT1", [128, 512], BF)]
    b_recip = sb("b_recip", [128, 512], F32)
    b_acc = sb("b_acc", [128, 512], F32)
    b_acc1 = sb("b_acc1", [128, 512], F32)
    b_olat = sb("b_olat", [128, 512], BF)
    b_mlao = sb("b_mlao", [128, 4, 512], F32)
    b_mix = sb("b_mix", [128, 8, 512], BF)
    b_x = sb("b_x", [128, 4, D], F32)
    b_wst = sb("b_wst", [128, 2048], F32)
    b_ckv = b_wst.rearrange("p (a b) -> p a b", a=16)
    b_vS = sb("b_vS", [128, 16, 128], BF)
    b_kTS = sb("b_kTS", [128, PAST], BF)
    b_cpe = sb("b_cpe", [128, 16, 32], F32)
    b_cpeb = sb("b_cpeb", [128, 16, 32], BF)
    b_peTS = sb("b_peTS", [128, PAST], BF)
    b_kTn = sb("b_kTn", [128, 16], BF)
    b_peTn = sb("b_peTn", [128, 16], BF)
    b_vn = sb("b_vn", [16, 128], BF)
    b_qS = sb("b_qS", [128, 8, 16], BF)
    b_qlS = sb("b_qlS", [128, 8, 16], BF)
    b_pTS = [sb("b_pTS0", [128, 128], BF), sb("b_pTS1", [128, 128], BF)]
    SCALE = float(96.0 ** -0.5)
    NPROMPT = SEQ if nblk_limit is None else nblk_limit * 512

    def passB_setup(l):
        pg.dma('sp', b_kT[:, 0:NPROMPT], kT_d[:, 0:NPROMPT], reads=['kT_d'], writes=['b_kT'])
        V('memset', [], ['b_peT0'], ap=b_peT[0][:], constant=0.0)
        V('memset', [], ['b_peT1'], ap=b_peT[1][:], constant=0.0)
        V('memset', [], ['b_peTS'], ap=b_peTS[:], constant=0.0)
        V('memset', [], ['b_peTn'], ap=b_peTn[:], constant=0.0)
        g0 = min(NPROMPT, HS)
        pg.dma('sp', b_peT[0][0:32, 0:g0], peT_d[:, 0:g0], reads=['peT_d', 'b_peT0'], writes=['b_peT0'])
        if NPROMPT > HS:
            pg.dma('sp', b_peT[1][0:32, 0:NPROMPT - HS], peT_d[:, HS:NPROMPT], reads=['peT_d', 'b_peT1'], writes=['b_peT1'])
        pg.dma('sp', b_v[:, 0:NPROMPT // 128, :], vtok_d[0:NPROMPT, :].rearrange("(b p) c -> p b c", p=128), reads=['vtok_d'],
               writes=['b_v'])
        load_weight_bf(b_wout, 'b_wout', lambda kt, c0, c1, l=l: w_out[l, kt * 128:(kt + 1) * 128, c0:c1], 8, D, stage=b_wst, stagekey='b_wst')
        pg.dma('sp', b_wst[:, 0:1024], w_ukT[l].rearrange("p h c -> p (h c)"), reads=['b_wst0', 'b_wst1'], writes=['b_wst0', 'b_wst1'])
        G('tensor_copy', ['b_wst0'], ['b_wukT'], out=b_wukT[:].rearrange("p h c -> p (h c)"), in_=b_wst[:, 0:1024])
        pg.dma('sp', b_wst[:, 1024:1536], w_uv[l].rearrange("p h c -> p (h c)"), reads=['b_wst0', 'b_wst1'], writes=['b_wst0', 'b_wst1'])
        G('tensor_copy', ['b_wst1'], ['b_wuv'], out=b_wuv[:].rearrange("p h c -> p (h c)"), in_=b_wst[:, 1024:1536])
        G('memset', [], ['b_dmask'], ap=b_dmask[:], constant=1.0)
        for j in range(4):
            for kh in range(2):
                G('affine_select', ['b_dmask'], ['b_dmask'], out=b_dmask[kh * 64:(kh + 1) * 64, j, :].rearrange("p (a b) -> p a b", a=8),
                  in_=b_dmask[kh * 64:(kh + 1) * 64, j, :].rearrange("p (a b) -> p a b", a=8), pattern=[[1, 8], [0, 64]],
                  compare_op=OP.is_ge, fill=0.0, base=-(2 * j + kh), channel_multiplier=0)

    def passB_tail(l, t0, tp, nt, xsrc, xkey):
        ntok = tp * nt
        fm_rstd(b_mlao, 4, 512, ntok, ['b_mlao'])
        for t in range(4):
            V('scalar_tensor_tensor', ['b_mlao', 'ong_sb', 'rbc'], ['b_mix'], out=b_mix[:, 4 + t, 0:ntok], in0=b_mlao[:, t, 0:ntok],
              scalar=ong_sb[:, 4 + t:5 + t], in1=rbc[:, 0:ntok], op0=OP.mult, op1=OP.mult)
        for j in range(nt):
            for hf in range(2):
                p_, pk = getps()
                for kt in range(8):
                    T('matmul', ['b_mix', 'b_wout'], [pk], signal=(kt == 7), out=p_[0:tp, :], lhsT=b_mix[:, kt, j * tp:(j + 1) * tp],
                      rhs=b_wout[:, kt, hf * 512:(hf + 1) * 512], start=(kt == 0), stop=(kt == 7))
                V('tensor_tensor', ['b_x', pk], ['b_x'], out=b_x[0:tp, j, hf * 512:(hf + 1) * 512], in0=b_x[0:tp, j, hf * 512:(hf + 1) * 512],
                  in1=p_[0:tp, :], op=OP.add)
        pg.dma('pool', xmid_d[t0:t0 + ntok, :].rearrange("(j p) c -> p j c", p=tp), b_x[0:tp, 0:nt, :], reads=['b_x'], writes=['xmid_d'])

    def passB_prompt_block(l, t0, xsrc, xkey):
        pg.dma('sp', b_qT[:], qT_d[:, :, t0:t0 + 512].rearrange("h p t -> p h t"), reads=['qT_d'], writes=['b_qT'])
        pg.dma('sp', b_mix[:, 0:4, :], mixT_d[:, :, t0:t0 + 512].rearrange("a p t -> p a t"), reads=['mixT_d'], writes=['b_mix'])
        pg.dma('sp', b_x[:], xsrc[t0:t0 + 512, :].rearrange("(j p) c -> p j c", p=128), reads=[xkey], writes=['b_x'])
        psallow[0] = [4, 5]
        for h in range(8):
            p_, pk = getps()
            T('matmul', ['b_wukT', 'b_qT'], [pk], out=p_[:, :], lhsT=b_wukT[64:128, h, :], rhs=b_qT[64:128, h, :], start=True, stop=True)
            A('copy', [pk], ['b_qlat'], out=b_qlat[:, h, :], in_=p_[:, :])
        nkb = (t0 + 512) // 128
        kd0 = t0 // 128
        po, pok = psf[2], ('psf', 2)
        pd, pdk = psf[3], ('psf', 3)

        def S(h, kb):
            ps_s, sk = psf[kb % 2], ('psf', kb % 2)
            g, kl = kb // 64, kb % 64
            T('matmul', ['b_kT', 'b_qlat'], [sk], signal=False, out=ps_s[:, :], lhsT=b_kT[:, kb * 128:(kb + 1) * 128], rhs=b_qlat[:, h, :],
              start=True, stop=False)
            T('matmul', ['b_peT%d' % g, 'b_qT'], [sk], out=ps_s[:, :], lhsT=b_peT[g][:, kl * 128:(kl + 1) * 128], rhs=b_qT[:, h, :],
              start=False, stop=True)
            pT, pTk = b_pT[kb % 2], 'b_pT%d' % (kb % 2)
            A('activation', [sk], [pTk], out=pT[:], in_=ps_s[:, :], func=AF.Exp, scale=SCALE)
            if kb >= kd0:
                G('tensor_tensor', [pTk, 'b_dmask'], [pTk], out=pT[:], in0=pT[:], in1=b_dmask[:, kb - kd0, :], op=OP.mult)

        def PV(h, kb):
            pT, pTk = b_pT[kb % 2], 'b_pT%d' % (kb % 2)
            T('matmul', ['b_v', pTk], [pok], out=po[:, :], lhsT=b_v[:, kb, :], rhs=pT[:], start=(kb == 0), stop=(kb == nkb - 1))
            if kb == 0:
                V('tensor_copy', [pTk], ['b_acc'], out=b_acc[:], in_=pT[:])
            else:
                V('tensor_tensor', [pTk, 'b_acc'], ['b_acc'], out=b_acc[:], in0=b_acc[:], in1=pT[:], op=OP.add)

        for h in range(8):
            S(h, 0)
            for kb in range(nkb):
                if kb + 1 < nkb:
                    S(h, kb + 1)
                PV(h, kb)
            T('matmul', ['ones_f', 'b_acc'], [pdk], out=pd[:, :], lhsT=ones_f[:], rhs=b_acc[:], start=True, stop=True)
            V('reciprocal', [pdk], ['b_recip'], out=b_recip[:], in_=pd[:, :])
            V('tensor_tensor', [pok, 'b_recip'], ['b_olat'], out=b_olat[:], in0=po[:, :], in1=b_recip[:], op=OP.mult)
            p_, pk = getps()
            pbh = (h % 2) * 64
            T('matmul', ['b_wuv', 'b_olat'], [pk], out=p_[pbh:pbh + 64, :], lhsT=b_wuv[:, h, :], rhs=b_olat[:], start=True, stop=True)
            A('copy', [pk], ['b_mlao'], out=b_mlao[pbh:pbh + 64, h // 2, :], in_=p_[pbh:pbh + 64, :])
        passB_tail(l, t0, 128, 4, xsrc, xkey)
        psallow[0] = list(range(NPS))

    def passB_sample(l, xsrc, xkey):
        t0 = SEQ
        pg.dma('sp', b_mix[:, 0:4, 0:64], mixT_d[:, :, t0:t0 + 64].rearrange("a p t -> p a t"), reads=['mixT_d'], writes=['b_mix'])
        pg.dma('sp', b_x[0:64, 0, :], xsrc[t0:t0 + 64, :], reads=[xkey], writes=['b_x'])
        psallow[0] = [4, 5]
        po, pok = psf[2], ('psf', 2)
        pd, pdk = psf[3], ('psf', 3)
        for q in range(NSEQ):
            c0 = t0 + 16 * q
            pg.dma('sp', b_ckv[:], cache_kv[l, q].rearrange("(b p) c -> p b c", p=128), writes=['b_wst0', 'b_wst1'])
            pg.dma('sp', b_cpe[:], cache_pe[l, q].rearrange("(b p) c -> p b c", p=128), writes=['b_cpe'])
            pg.dma('sp', b_kTn[:], kT_d[:, c0:c0 + 16], reads=['kT_d'], writes=['b_kTn'])
            pg.dma('sp', b_peTn[0:32, :], peT_d[:, c0:c0 + 16], reads=['peT_d', 'b_peTn'], writes=['b_peTn'])
            pg.dma('sp', b_vn[:], vtok_d[c0:c0 + 16, :], reads=['vtok_d'], writes=['b_vn'])
            pg.dma('sp', b_qS[:], qT_d[:, :, c0:c0 + 16].rearrange("h p t -> p h t"), reads=['qT_d'], writes=['b_qS'])
            V('tensor_copy', ['b_wst0', 'b_wst1'], ['b_vS'], out=b_vS[:], in_=b_ckv[:])
            V('tensor_copy', ['b_cpe'], ['b_cpeb'], out=b_cpeb[:], in_=b_cpe[:])
            for grp in range(2):
                pb_, pbk_ = getpsb()
                for i in range(8):
                    T('transpose', ['b_vS', 'ident_b'], [pbk_], signal=(i == 7), out=pb_[:, i * 128:(i + 1) * 128], in_=b_vS[:, grp * 8 + i, :],
                      identity=ident_b[:])
                A('copy', [pbk_], ['b_kTS'], out=b_kTS[:, grp * 1024:(grp + 1) * 1024], in_=pb_[:, 0:1024])
            for grp in range(2):
                pb_, pbk_ = getpsb()
                for i in range(8):
                    T('transpose', ['b_cpeb', 'ident_b'], [pbk_], signal=(i == 7), out=pb_[0:32, i * 128:(i + 1) * 128],
                      in_=b_cpeb[:, grp * 8 + i, :], identity=ident_b[:])
                A('copy', [pbk_, 'b_peTS'], ['b_peTS'], out=b_peTS[0:32, grp * 1024:(grp + 1) * 1024], in_=pb_[0:32, 0:1024])
            p_, pk = getps()
            for h in range(8):
                T('matmul', ['b_wukT', 'b_qS'], [pk], signal=(h == 7), out=p_[:, h * 16:(h + 1) * 16], lhsT=b_wukT[64:128, h, :],
                  rhs=b_qS[64:128, h, :], start=True, stop=True)
            A('copy', [pk], ['b_qlS'], out=b_qlS[:].rearrange("p h t -> p (h t)"), in_=p_[:, 0:128])
            qlf = b_qlS[:].rearrange("p h t -> p (h t)")
            qsf = b_qS[:].rearrange("p h t -> p (h t)")
            for blk in range(17):
                nk = 128 if blk < 16 else 16
                ps_s, sk = psf[blk % 2], ('psf', blk % 2)
                kT_l = b_kTS[:, blk * 128:(blk + 1) * 128] if blk < 16 else b_kTn[:, :]
                pe_l = b_peTS[:, blk * 128:(blk + 1) * 128] if blk < 16 else b_peTn[:, :]
                v_l = b_vS[:, blk, :] if blk < 16 else b_vn[:, :]
                T('matmul', ['b_kTS', 'b_kTn', 'b_qlS'], [sk], signal=False, out=ps_s[0:nk, 0:128], lhsT=kT_l, rhs=qlf, start=True, stop=False)
                T('matmul', ['b_peTS', 'b_peTn', 'b_qS'], [sk], out=ps_s[0:nk, 0:128], lhsT=pe_l, rhs=qsf, start=False, stop=True)
                pT, pTk = b_pTS[blk % 2], 'b_pTS%d' % (blk % 2)
                A('activation', [sk], [pTk], out=pT[0:nk, :], in_=ps_s[0:nk, 0:128], func=AF.Exp, scale=SCALE)
                T('matmul', ['b_vS', 'b_vn', pTk], [pok], signal=False, out=po[:, 0:128], lhsT=v_l, rhs=pT[0:nk, :], start=(blk == 0), stop=(blk == 16))
                T('matmul', ['ones_bf', pTk], [pdk], out=pd[:, 0:128], lhsT=ones_bf[0:nk, :], rhs=pT[0:nk, :], start=(blk == 0), stop=(blk == 16))
            V('reciprocal', [pdk], ['b_recip'], out=b_recip[:, 0:128], in_=pd[:, 0:128])
            V('tensor_tensor', [pok, 'b_recip'], ['b_olat'], out=b_olat[:, 0:128], in0=po[:, 0:128], in1=b_recip[:, 0:128], op=OP.mult)
            for h in range(8):
                p_, pk = getps()
                pbh = (h % 2) * 64
                T('matmul', ['b_wuv', 'b_olat'], [pk], out=p_[pbh:pbh + 64, 0:16], lhsT=b_wuv[:, h, :], rhs=b_olat[:, h * 16:(h + 1) * 16],
                  start=True, stop=True)
                A('copy', [pk], ['b_mlao'], out=b_mlao[pbh:pbh + 64, h // 2, 16 * q:16 * q + 16], in_=p_[pbh:pbh + 64, 0:16])
        passB_tail(l, t0, 64, 1, xsrc, xkey)
        psallow[0] = list(range(NPS))

    region()
    c_wup = sb("c_wup", [128, 8, 4 * D], BF)
    c_wdn = sb("c_wdn", [128, 32, D], BF)
    c_x = sb("c_x", [128, 2, D], F32)
    c_h = sb("c_h", [128, 2, D], BF)
    c_hT = sb("c_hT", [128, 8, 256], BF)
    c_a = sb("c_a", [128, 32, 256], BF)
    c_junk = sb("c_junk", [128, D], F32)
    c_r = [sb("c_r0", [128, 256], F32), sb("c_r1", [128, 256], F32)]
    c_wst = sb("c_wst", [128, 2048], F32)
    c_g2 = sb("c_g2", [128, D], F32)
    c_gf = sb("c_gf", [128, D], F32)
    c_y = sb("c_y", [128, 2, D], F32)

    def passC_setup(l):
        pg.dma('sp', c_g2[:], g2B[l], writes=['c_g2'])
        pg.dma('sp', c_gf[:], gfB[:, :], writes=['c_gf'])
        load_weight_bf(c_wup, 'c_wup', lambda kt, c0, c1, l=l: w_up[l, kt * 128:(kt + 1) * 128, c0:c1], 8, 4 * D, stage=c_wst, stagekey='c_wst')
        load_weight_bf(c_wdn, 'c_wdn', lambda kt, c0, c1, l=l: w_down[l, kt * 128:(kt + 1) * 128, c0:c1], 32, D, stage=c_wst, stagekey='c_wst')

    def passC_block(l, t0, tp, nt):
        ntok = tp * nt
        pg.dma('sp', c_x[0:tp, 0:nt, :], xmid_d[t0:t0 + ntok, :].rearrange("(j p) c -> p j c", p=tp), reads=['xmid_d'], writes=['c_x'])
        for j in range(nt):
            A('activation', ['c_x'], ['c_junk'], out=c_junk[0:tp, :], in_=c_x[0:tp, j, :], func=AF.Square)
            V('reduce_sum', ['c_junk'], ['ss'], out=ss[0:tp, j:j + 1], in_=c_junk[0:tp, :], axis=AX.X)
        rms_rstd(ss[0:tp, 0:nt], rstd[0:tp, 0:nt], D, ['ss'], ['rstd'])
        for j in range(nt):
            V('scalar_tensor_tensor', ['c_x', 'rstd', 'c_g2'], ['c_h'], out=c_h[0:tp, j, :], in0=c_x[0:tp, j, :], scalar=rstd[0:tp, j:j + 1],
              in1=c_g2[0:tp, :], op0=OP.mult, op1=OP.mult)
        for kt in range(8):
            pb_, pbk_ = getpsb()
            for j in range(nt):
                T('transpose', ['c_h', 'ident_b'], [pbk_], signal=(j == nt - 1), out=pb_[:, j * tp:(j + 1) * tp],
                  in_=c_h[0:tp, j, kt * 128:(kt + 1) * 128], identity=ident_b[0:tp, 0:tp])
            A('copy', [pbk_], ['c_hT'], out=c_hT[:, kt, 0:ntok], in_=pb_[:, 0:ntok])
        for f in range(32):
            p_, pk = getps()
            for kt in range(8):
                T('matmul', ['c_hT', 'c_wup'], [pk], signal=(kt == 7), out=p_[:, 0:ntok], lhsT=c_wup[:, kt, f * 128:(f + 1) * 128],
                  rhs=c_hT[:, kt, 0:ntok], start=(kt == 0), stop=(kt == 7))
            rr_, rk = c_r[f % 2], 'c_r%d' % (f % 2)
            A('activation', [pk], [rk], out=rr_[:, 0:ntok], in_=p_[:, 0:ntok], func=AF.Relu)
            V('tensor_tensor', [rk], ['c_a'], out=c_a[:, f, 0:ntok], in0=rr_[:, 0:ntok], in1=rr_[:, 0:ntok], op=OP.mult)
        for j in range(nt):
            for hf in range(2):
                p_, pk = getps()
                for f in range(32):
                    T('matmul', ['c_a', 'c_wdn'], [pk], signal=(f == 31), out=p_[0:tp, :], lhsT=c_a[:, f, j * tp:(j + 1) * tp],
                      rhs=c_wdn[:, f, hf * 512:(hf + 1) * 512], start=(f == 0), stop=(f == 31))
                V('tensor_tensor', ['c_x', pk], ['c_x'], out=c_x[0:tp, j, hf * 512:(hf + 1) * 512], in0=c_x[0:tp, j, hf * 512:(hf + 1) * 512],
                  in1=p_[0:tp, :], op=OP.add)
        if l < DEPTH - 1:
            pg.dma('pool', xout_d[t0:t0 + ntok, :].rearrange("(j p) c -> p j c", p=tp), c_x[0:tp, 0:nt, :], reads=['c_x'], writes=['xout_d'])
        else:
            for j in range(nt):
                A('activation', ['c_x'], ['c_junk'], out=c_junk[0:tp, :], in_=c_x[0:tp, j, :], func=AF.Square)
                V('reduce_sum', ['c_junk'], ['ss'], out=ss[0:tp, 4 + j:5 + j], in_=c_junk[0:tp, :], axis=AX.X)
            rms_rstd(ss[0:tp, 4:4 + nt], rstd[0:tp, 4:4 + nt], D, ['ss'], ['rstd'])
            for j in range(nt):
                V('scalar_tensor_tensor', ['c_x', 'rstd', 'c_gf'], ['c_y'], out=c_y[0:tp, j, :], in0=c_x[0:tp, j, :],
                  scalar=rstd[0:tp, 4 + j:5 + j], in1=c_gf[0:tp, :], op0=OP.mult, op1=OP.mult)
            pg.dma('pool', o_y[t0:t0 + ntok, :].rearrange("(j p) c -> p j c", p=tp), c_y[0:tp, 0:nt, :], reads=['c_y'])

    for l in range(DEPTH):
        pg.barrier()
        passA_consts()
        pg.dma('sp', b_glu_sb[:], b_glu[l], writes=['b_glu_sb'])
        pg.dma('sp', qng_sb[:], qng[l], writes=['qng_sb'])
        pg.dma('sp', ong_sb[:], ong[l], writes=['ong_sb'])
        load_weight_bf(w_glu_bf, 'w_glu_bf', lambda kt, c0, c1, l=l: w_glu[l, kt * 128:(kt + 1) * 128, c0:c1], 2, 256)
        load_weight_bf(w_uq_bf, 'w_uq_bf', lambda kt, c0, c1, l=l: w_uq[l, kt * 128:(kt + 1) * 128, c0:c1], 2, 1024)
        hg_setup(l)
        pg.dma('sp', g1sb[:], g1B[l], writes=['g1sb'])
        pg.dma('sp', kvg[:], kvgB[l], writes=['kvg'])
        load_weight_bf(w_in_bf, 'w_in_bf', lambda kt, c0, c1, l=l: w_in[l, kt * 128:(kt + 1) * 128, c0:c1], 8, INW)
        s5_setup(l)
        xsrc, xkey = (xin, 'xin') if l == 0 else (xout_d, 'xout_d')
        for (t0, tp, nt) in blocks:
            ntok = tp * nt
            sample = (tp == 64)
            pg.dma('sp', xblk[0:tp, 0:nt, :], xsrc[t0:t0 + ntok, :].rearrange("(j p) c -> p j c", p=tp), reads=[xkey], writes=['xblk'])
            pg.dma('sp', rc[0:tp, 0:nt, :], ropec[t0:t0 + ntok, :].rearrange("(j p) c -> p j c", p=tp), writes=['rc'])
            pg.dma('sp', rs[0:tp, 0:nt, :], ropes[t0:t0 + ntok, :].rearrange("(j p) c -> p j c", p=tp), writes=['rs'])
            for j in range(nt):
                A('activation', ['xblk'], ['junkf'], out=junkf[0:tp, :], in_=xblk[0:tp, j, :], func=AF.Square)
                V('reduce_sum', ['junkf'], ['ss'], out=ss[0:tp, j:j + 1], in_=junkf[0:tp, :], axis=AX.X)
            rms_rstd(ss[0:tp, 0:nt], rstd[0:tp, 0:nt], D, ['ss'], ['rstd'])
            for j in range(nt):
                V('scalar_tensor_tensor', ['xblk', 'rstd', 'g1sb'], ['hbf'], out=hbf[0:tp, j, :], in0=xblk[0:tp, j, :],
                  scalar=rstd[0:tp, j:j + 1], in1=g1sb[0:tp, :], op0=OP.mult, op1=OP.mult)
            for kt in range(8):
                pb, pk = getpsb()
                for j in range(nt):
                    T('transpose', ['hbf', 'ident_b'], [pk], signal=(j == nt - 1), out=pb[:, j * tp:(j + 1) * tp],
                      in_=hbf[0:tp, j, kt * 128:(kt + 1) * 128], identity=ident_b[0:tp, 0:tp])
                A('copy', [pk], ['hT'], out=hT[:, kt, 0:ntok], in_=pb[:, 0:ntok])
            for j in range(nt):
                p_, pk = getps()
                for kt in range(8):
                    T('matmul', ['hT', 'w_in_bf'], [pk], signal=(kt == 7), out=p_[0:tp, 0:160], lhsT=hT[:, kt, j * tp:(j + 1) * tp],
                      rhs=w_in_bf[:, kt, 1536:1696], start=(kt == 0), stop=(kt == 7))
                A('copy', [pk], ['kvtm'], out=kvtm[0:tp, j, :], in_=p_[0:tp, 0:160])
            for j in range(nt):
                A('activation', ['kvtm'], ['junkf'], out=junkf[0:tp, 0:128], in_=kvtm[0:tp, j, 0:128], func=AF.Square)
                V('reduce_sum', ['junkf'], ['ss'], out=ss[0:tp, 4 + j:5 + j], in_=junkf[0:tp, 0:128], axis=AX.X)
            rms_rstd(ss[0:tp, 4:4 + nt], rstd[0:tp, 4:4 + nt], 128, ['ss'], ['rstd'])
            for j in range(nt):
                V('scalar_tensor_tensor', ['kvtm', 'rstd', 'kvg'], ['kvn'], out=kvn[0:tp, j, :], in0=kvtm[0:tp, j, 0:128],
                  scalar=rstd[0:tp, 4 + j:5 + j], in1=kvg[0:tp, :], op0=OP.mult, op1=OP.mult)
            x1 = kvtm[0:tp, 0:nt, 128:144]
            x2 = kvtm[0:tp, 0:nt, 144:160]
            c_ = rc[0:tp, 0:nt, :]
            s_ = rs[0:tp, 0:nt, :]
            ta = tmp16[0:tp, 0:nt, :]
            tb = tmp16b[0:tp, 0:nt, :]
            V('tensor_tensor', ['kvtm', 'rc'], ['tmp16'], out=ta, in0=x1, in1=c_, op=OP.mult)
            V('tensor_tensor', ['kvtm', 'rs'], ['tmp16b'], out=tb, in0=x2, in1=s_, op=OP.mult)
            V('tensor_tensor', ['tmp16', 'tmp16b'], ['pen'], out=pen[0:tp, 0:nt, 0:16], in0=ta, in1=tb, op=OP.subtract)
            V('tensor_tensor', ['kvtm', 'rc'], ['tmp16'], out=ta, in0=x2, in1=c_, op=OP.mult)
            V('tensor_tensor', ['kvtm', 'rs'], ['tmp16b'], out=tb, in0=x1, in1=s_, op=OP.mult)
            V('tensor_tensor', ['tmp16', 'tmp16b'], ['pen'], out=pen[0:tp, 0:nt, 16:32], in0=ta, in1=tb, op=OP.add)
            pg.dma('pool', o_kv[l, t0:t0 + ntok, :].rearrange("(j p) c -> p j c", p=tp), kvn[0:tp, 0:nt, :], reads=['kvn'])
            pg.dma('pool', o_pe[l, t0:t0 + ntok, :].rearrange("(j p) c -> p j c", p=tp), pen[0:tp, 0:nt, :], reads=['pen'])
            for slot, ct in enumerate([0, 1, 2, 3, 4, 5, 8, 9, 10, 11]):
                p_, pk = getps()
                for kt in range(8):
                    T('matmul', ['hT', 'w_in_bf'], [pk], signal=(kt == 7), out=p_[:, 0:ntok], lhsT=w_in_bf[:, kt, ct * 128:(ct + 1) * 128],
                      rhs=hT[:, kt, 0:ntok], start=(kt == 0), stop=(kt == 7))
                A('copy', [pk], ['projT'], out=projT[:, slot, 0:ntok], in_=p_[:, 0:ntok])
            V('tensor_copy', ['projT'], ['uT_bf'], out=uT_bf[:, :, 0:ntok], in_=projT[:, 0:2, 0:ntok])
            if sample:
                pg.dma('sp', hcarS[:], st_s5[l], writes=['hcarS'])
                V('tensor_copy', ['hcarS'], ['hcar'], out=hcar[:, 0, :, :], in_=hcarS[:, :, 0, :])
                V('tensor_copy', ['hcarS'], ['hcar'], out=hcar[:, 1, :, :], in_=hcarS[:, :, 1, :])
                s5_chunk(l, 0, 64, NSEQ, True)
                V('tensor_copy', ['hcar'], ['hcarS'], out=hcarS[:, :, 0, :], in_=hcar[:, 0, :, :])
                V('tensor_copy', ['hcar'], ['hcarS'], out=hcarS[:, :, 1, :], in_=hcar[:, 1, :, :])
                pg.dma('pool', o_s5s[l], hcarS[:], reads=['hcarS'])
            else:
                for ci in range(4):
                    s5_chunk(l, ci * 128, 128, 1, False)
                if t0 + ntok == SEQ or (nblk_limit is not None and t0 + ntok == nblk_limit * 512):
                    V('tensor_copy', ['hcar'], ['hcarS'], out=hcarS[:, :, 0, 0:1], in_=hcar[:, 0, :, 0:1])
                    V('tensor_copy', ['hcar'], ['hcarS'], out=hcarS[:, :, 1, 0:1], in_=hcar[:, 1, :, 0:1])
                    V('tensor_copy', ['hcarS'], ['hcarP'], out=hcarP[:], in_=hcarS[:, :, :, 0:1])
                    pg.dma('pool', o_s5p[l], hcarP[:], reads=['hcarP'])
            yv = yT[:, :, 0:ntok]
            zt = t1[:].rearrange("p a b -> p (a b)").rearrange("p (a b) -> p a b", a=2)[:, :, 0:ntok]
            zz = t2[:].rearrange("p a b -> p (a b)").rearrange("p (a b) -> p a b", a=2)[:, :, 0:ntok]
            s5o = t3[:].rearrange("p a b -> p (a b)").rearrange("p (a b) -> p a b", a=2)[:, :, 0:ntok]
            z_bf = uT_bf[:, :, 0:ntok]
            A('activation', ['yT'], ['t1'], out=zt, in_=yv, func=AF.Square)
            V('tensor_scalar', ['t1'], ['t1'], out=zt, in0=zt, scalar1=0.044715, scalar2=1.0, op0=OP.mult, op1=OP.add)
            V('tensor_tensor', ['t1', 'yT'], ['t1'], out=zt, in0=zt, in1=yv, op=OP.mult)
            A('activation', ['t1'], ['t1'], out=zt, in_=zt, func=AF.Sigmoid, scale=1.5957691216057308)
            V('tensor_tensor', ['t1', 'yT'], ['t2'], out=zz, in0=zt, in1=yv, op=OP.mult)
            V('tensor_copy', ['t2'], ['uT_bf'], out=z_bf, in_=zz)
            for m in range(2):
                p_, pk = getps()
                for kt in range(2):
                    T('matmul', ['uT_bf', 'w_glu_bf'], [pk], signal=(kt == 1), out=p_[:, 0:ntok], lhsT=w_glu_bf[:, kt, m * 128:(m + 1) * 128],
                      rhs=z_bf[:, kt, :], start=(kt == 0), stop=(kt == 1))
                A('activation', [pk, 'b_glu_sb'], ['t1'], out=zt[:, m, :], in_=p_[:, 0:ntok], func=AF.Sigmoid, bias=b_glu_sb[:, m:m + 1])
                V('tensor_tensor', ['t1', 't2'], ['t3'], out=s5o[:, m, :], in0=zz[:, m, :], in1=zt[:, m, :], op=OP.mult)
            fm_rstd(s5o, 2, 256, ntok, ['t3'])
            for m in range(2):
                V('scalar_tensor_tensor', ['t3', 'ong_sb', 'rbc'], ['mixA'], out=mixA[:, m, 0:ntok], in0=s5o[:, m, :],
                  scalar=ong_sb[:, m:m + 1], in1=rbc[:, 0:ntok], op0=OP.mult, op1=OP.mult)
            if sample:
                for q in range(NSEQ):
                    p_, pk = getps()
                    for kt in range(8):
                        T('matmul', ['hT', 'w_in_bf'], [pk], signal=(kt == 7), out=p_[0:16, 0:256], lhsT=hT[:, kt, 16 * q:16 * q + 16],
                          rhs=w_in_bf[:, kt, 768:1024], start=(kt == 0), stop=(kt == 7))
                    A('copy', [pk], ['vtm_bf'], out=vtm_bf[0:16, q, :], in_=p_[0:16, 0:256])
            else:
                for j in range(nt):
                    p_, pk = getps()
                    for kt in range(8):
                        T('matmul', ['hT', 'w_in_bf'], [pk], signal=(kt == 7), out=p_[0:tp, 0:256], lhsT=hT[:, kt, j * tp:(j + 1) * tp],
                          rhs=w_in_bf[:, kt, 768:1024], start=(kt == 0), stop=(kt == 7))
                    A('copy', [pk], ['vtm_bf'], out=vtm_bf[0:tp, j, :], in_=p_[0:tp, 0:256])
            if 'H' not in SKIP:
                hg_block(l, ntok, 16 if sample else 64, sample, t0)
            fm_rstd(hgo, 2, 256, ntok, ['hgo'])
            gsl = t1[:].rearrange("p a b -> p (a b)").rearrange("p (a b) -> p a b", a=2)[:, :, 0:ntok]
            A('activation', ['projT'], ['t1'], out=gsl, in_=projT[:, 6:8, 0:ntok], func=AF.Silu)
            for m in range(2):
                V('scalar_tensor_tensor', ['hgo', 'ong_sb', 'rbc'], ['t2'], out=zz[:, m, :], in0=hgo[:, m, 0:ntok],
                  scalar=ong_sb[:, 2 + m:3 + m], in1=rbc[:, 0:ntok], op0=OP.mult, op1=OP.mult)
                V('tensor_tensor', ['t1', 't2'], ['mixA'], out=mixA[:, 2 + m, 0:ntok], in0=zz[:, m, :], in1=gsl[:, m, :], op=OP.mult)
            pg.dma('pool', mixT_d[:, :, t0:t0 + ntok].rearrange("a p t -> p a t"), mixA[:, :, 0:ntok], reads=['mixA'], writes=['mixT_d'])
            if 'M' in SKIP:
                continue
            cqT = projT[:, 8:10, 0:ntok]
            fm_rstd(cqT, 2, 256, ntok, ['projT'])
            cqn = uT_bf[:, :, 0:ntok]
            for m in range(2):
                V('scalar_tensor_tensor', ['projT', 'qng_sb', 'rbc'], ['uT_bf'], out=cqn[:, m, :], in0=cqT[:, m, :],
                  scalar=qng_sb[:, m:m + 1], in1=rbc[:, 0:ntok], op0=OP.mult, op1=OP.mult)
            for j in range(nt):
                for hf in range(2):
                    p_, pk = getps()
                    for kt in range(2):
                        T('matmul', ['uT_bf', 'w_uq_bf'], [pk], signal=(kt == 1), out=p_[0:tp, :], lhsT=cqn[:, kt, j * tp:(j + 1) * tp],
                          rhs=w_uq_bf[:, kt, hf * 512:(hf + 1) * 512], start=(kt == 0), stop=(kt == 1))
                    A('copy', [pk], ['xblk'], out=xblk[0:tp, j, hf * 512:(hf + 1) * 512], in_=p_[0:tp, :])
            for j in range(nt):
                qv = xblk[0:tp, j, :].rearrange("p (h c) -> p h c", c=128)[:, :, 0:64].rearrange("p h (k e) -> p h k e", e=32)
                x1 = qv[:, :, :, 0:16]
                x2 = qv[:, :, :, 16:32]
                cb = rc[0:tp, j, :].unsqueeze(1).unsqueeze(1).to_broadcast([tp, 8, 2, 16])
                sbb = rs[0:tp, j, :].unsqueeze(1).unsqueeze(1).to_broadcast([tp, 8, 2, 16])
                ra = junkf[0:tp, 0:256].rearrange("p (a k b) -> p a k b", a=8, k=2)
                rb_ = junkf[0:tp, 256:512].rearrange("p (a k b) -> p a k b", a=8, k=2)
                rc_ = junkf[0:tp, 512:768].rearrange("p (a k b) -> p a k b", a=8, k=2)
                rd_ = junkf[0:tp, 768:1024].rearrange("p (a k b) -> p a k b", a=8, k=2)
                V('tensor_tensor', ['xblk', 'rc'], ['junkf'], out=ra, in0=x1, in1=cb, op=OP.mult)
                V('tensor_tensor', ['xblk', 'rs'], ['junkf'], out=rb_, in0=x2, in1=sbb, op=OP.mult)
                V('tensor_tensor', ['xblk', 'rc'], ['junkf'], out=rc_, in0=x2, in1=cb, op=OP.mult)
                V('tensor_tensor', ['xblk', 'rs'], ['junkf'], out=rd_, in0=x1, in1=sbb, op=OP.mult)
                V('tensor_tensor', ['junkf'], ['xblk'], out=x1, in0=ra, in1=rb_, op=OP.subtract)
                V('tensor_tensor', ['junkf'], ['xblk'], out=x2, in0=rc_, in1=rd_, op=OP.add)
                V('tensor_copy', ['xblk'], ['hbf'], out=hbf[0:tp, j, :], in_=xblk[0:tp, j, :])
            for h in range(8):
                pb_, pbk_ = getpsb()
                for j in range(nt):
                    T('transpose', ['hbf', 'ident_b'], [pbk_], signal=(j == nt - 1), out=pb_[:, j * tp:(j + 1) * tp],
                      in_=hbf[0:tp, j, h * 128:(h + 1) * 128], identity=ident_b[0:tp, 0:tp])
                A('copy', [pbk_], ['hT'], out=hT[:, h, 0:ntok], in_=pb_[:, 0:ntok])
            pg.dma('pool', qT_d[:, :, t0:t0 + ntok].rearrange("a p t -> p a t"), hT[:, :, 0:ntok], reads=['hT'], writes=['qT_d'])
            V('tensor_copy', ['kvn'], ['kvn_bf'], out=kvn_bf[0:tp, 0:nt, :], in_=kvn[0:tp, 0:nt, :])
            V('tensor_copy', ['pen'], ['pen_bf'], out=pen_bf[0:tp, 0:nt, :], in_=pen[0:tp, 0:nt, :])
            pg.dma('pool', vtok_d[t0:t0 + ntok, :].rearrange("(j p) c -> p j c", p=tp), kvn_bf[0:tp, 0:nt, :], reads=['kvn_bf'], writes=['vtok_d'])
            pb_, pbk_ = getpsb()
            for j in range(nt):
                T('transpose', ['kvn_bf', 'ident_b'], [pbk_], signal=(j == nt - 1), out=pb_[:, j * tp:(j + 1) * tp],
                  in_=kvn_bf[0:tp, j, :], identity=ident_b[0:tp, 0:tp])
            A('copy', [pbk_], ['kT_blk'], out=kT_blk[:, 0:ntok], in_=pb_[:, 0:ntok])
            pg.dma('pool', kT_d[:, t0:t0 + ntok], kT_blk[:, 0:ntok], reads=['kT_blk'], writes=['kT_d'])
            pb_, pbk_ = getpsb()
            for j in range(nt):
                T('transpose', ['pen_bf', 'ident_b'], [pbk_], signal=(j == nt - 1), out=pb_[0:32, j * tp:(j + 1) * tp],
                  in_=pen_bf[0:tp, j, :], identity=ident_b[0:tp, 0:tp])
            A('copy', [pbk_], ['peT_blk'], out=peT_blk[:, 0:ntok], in_=pb_[0:32, 0:ntok])
            pg.dma('pool', peT_d[:, t0:t0 + ntok], peT_blk[:, 0:ntok], reads=['peT_blk'], writes=['peT_d'])
        if stop_after == 'A0':
            break
        pg.barrier()
        passB_setup(l)
        for (t0, tp, nt) in blocks:
            if tp == 64:
                passB_sample(l, xsrc, xkey)
            else:
                passB_prompt_block(l, t0, xsrc, xkey)
        if stop_after == 'B0':
            break
        pg.barrier()
        passC_setup(l)
        for (t0, tp, nt) in blocks:
            if tp == 64:
                passC_block(l, t0, 64, 1)
            else:
                passC_block(l, t0, 128, 2)
                passC_block(l, t0 + 256, 128, 2)

    pg.finish()
    pg.replay()
    return nc, pg


def rope_tables():
    half = 16
    inv = (10000.0 ** (-np.arange(half, dtype=np.float32) / half)).astype(np.float32)
    pos = np.concatenate([np.arange(SEQ, dtype=np.float32)] + [PAST + np.arange(DSEQ, dtype=np.float32)] * NSEQ)
    ang = pos.astype(np.float32)[:, None] * inv[None, :]
    return np.cos(ang).astype(np.float32), np.sin(ang).astype(np.float32)


_CACHE = {}


def kernel(**inp):
    f = lambda k: np.ascontiguousarray(np.asarray(inp[k], dtype=np.float32))
    nblk = os.environ.get("K_NBLK")
    nblk = int(nblk) if nblk else None
    stop = os.environ.get("K_STOP")
    key = (nblk, stop)
    if key not in _CACHE:
        _CACHE[key] = build(nblk, stop)
    nc, pg = _CACHE[key]
    xp = f('x_prompt')[0]
    xs = f('x_sample')
    cosT, sinT = rope_tables()
    rep = lambda a: np.ascontiguousarray(np.broadcast_to(a[:, None, :], (a.shape[0], 128, a.shape[1])))
    def sm_(a):
        return a.reshape(DEPTH, 8, 128).transpose(0, 2, 1)
    lam_re, lam_im = f('s5_lambda_re'), f('s5_lambda_im')
    ldt = np.repeat(f('s5_log_dt')[:, :, None], 64, axis=2)
    s5p = np.ascontiguousarray(np.stack([sm_(lam_re), sm_(lam_im), sm_(ldt)], axis=-1))
    def bm_(a):
        return a.reshape(DEPTH, 8, 128, 16).transpose(0, 2, 1, 3)
    s5b = np.ascontiguousarray(np.stack([bm_(f('s5_b_re')), bm_(f('s5_b_im'))], axis=3))
    cT = lambda a: a.transpose(0, 1, 3, 2)
    s5c = np.ascontiguousarray(np.stack([bm_(cT(f('s5_c_re'))), bm_(cT(f('s5_c_im')))], axis=3))
    s5d = np.ascontiguousarray(f('s5_d').reshape(DEPTH, 2, 128).transpose(0, 2, 1))
    sre, sim = f('state_s5_re'), f('state_s5_im')
    common = {
        "ropec": cosT, "ropes": sinT,
        "g1B": rep(f('norm1_g')), "w_in": f('w_in'), "kvgB": rep(f('mla_kv_norm_g')),
        "s5p": s5p, "s5b": s5b, "s5c": s5c, "s5d": s5d,
        "w_glu": f('s5_w_glu'),
        "b_glu": np.ascontiguousarray(f('s5_b_glu').reshape(DEPTH, 2, 128).transpose(0, 2, 1)),
        "lbl": np.ascontiguousarray(f('hgrn_lb_logits').reshape(DEPTH, 2, 128).transpose(2, 0, 1)),
        "qng": np.ascontiguousarray(f('mla_q_norm_g').reshape(DEPTH, 2, 128).transpose(0, 2, 1)),
        "ong": np.ascontiguousarray(f('out_norm_g').reshape(DEPTH, 8, 128).transpose(0, 2, 1)),
    }
    wq = f('mla_w_uq').reshape(DEPTH, 256, 8, 96)
    common["w_uq"] = np.ascontiguousarray(np.concatenate([wq[..., 64:96], wq[..., 64:96], wq[..., 0:64]], axis=-1).reshape(DEPTH, 256, 1024))
    shg = f('state_hgrn')
    wuk = f('mla_w_uk')
    wukT = np.zeros((DEPTH, 128, 8, 128), np.float32)
    wukT[:, 64:128] = wuk.transpose(0, 3, 2, 1)
    common.update({
        "w_out": f('w_out'), "w_ukT": wukT, "w_uv": f('mla_w_uv'),
        "g2B": rep(f('norm2_g')), "gfB": np.ascontiguousarray(np.broadcast_to(f('final_norm_g')[None, :], (128, D))),
        "w_up": f('w_up'), "w_down": f('w_down'),
    })
    ckv_all, cpe_all = f('cache_mla_kv'), f('cache_mla_pe')
    in_maps = []
    for c in range(NCORES):
        m = dict(common)
        m["xin"] = np.ascontiguousarray(np.concatenate([xp, xs[c * NSEQ:(c + 1) * NSEQ].reshape(NSEQ * DSEQ, D)], axis=0))
        def st_(a):
            return a.reshape(DEPTH, NSEQ, 8, 128).transpose(0, 3, 2, 1)
        sl = slice(c * NSEQ, (c + 1) * NSEQ)
        m["cache_kv"] = np.ascontiguousarray(ckv_all[:, sl])
        m["cache_pe"] = np.ascontiguousarray(cpe_all[:, sl])
        m["st_hg"] = np.ascontiguousarray(shg[:, sl].reshape(DEPTH, NSEQ, 2, 2, 64, 64).transpose(0, 1, 3, 4, 2, 5).reshape(DEPTH, NSEQ, 128, 2, 64))
        m["st_s5"] = np.ascontiguousarray(np.stack([st_(sre[:, sl]), st_(sim[:, sl])], axis=3))
        in_maps.append(m)
    res = run_bass_kernel_spmd(nc, in_maps, core_ids=list(range(NCORES)))
    R = res.results
    y_p = R[0]["o_y"][:SEQ][None]
    y_s = np.concatenate([R[c]["o_y"][SEQ:].reshape(NSEQ, DSEQ, D) for c in range(NCORES)], axis=0)
    kv_p = R[0]["o_kv"][:, :SEQ][:, None]
    pe_p = R[0]["o_pe"][:, :SEQ][:, None]
    kv_s = np.concatenate([R[c]["o_kv"][:, SEQ:].reshape(DEPTH, NSEQ, DSEQ, 128) for c in range(NCORES)], axis=1)
    pe_s = np.concatenate([R[c]["o_pe"][:, SEQ:].reshape(DEPTH, NSEQ, DSEQ, 32) for c in range(NCORES)], axis=1)
    z = lambda *s: np.zeros(s, np.float32)
    def us_(a):
        return a.transpose(0, 2, 1).reshape(DEPTH, 16, 64)
    s5re_p = us_(R[0]["o_s5p"][:, :, :, 0, 0])[:, None]
    s5im_p = us_(R[0]["o_s5p"][:, :, :, 1, 0])[:, None]
    s5re_s = np.stack([us_(R[c]["o_s5s"][:, :, :, 0, q]) for c in range(NCORES) for q in range(NSEQ)], axis=1)
    s5im_s = np.stack([us_(R[c]["o_s5s"][:, :, :, 1, q]) for c in range(NCORES) for q in range(NSEQ)], axis=1)
    def uh_(a):
        sh = a.shape[:-3]
        return a.reshape(sh + (2, 64, 2, 64)).transpose(tuple(range(len(sh))) + (len(sh) + 2, len(sh), len(sh) + 1, len(sh) + 3)).reshape(sh + (4, 64, 64))
    hg_p = uh_(R[0]["o_hgp"])[:, None]
    hg_s = np.concatenate([uh_(R[c]["o_hgs"]) for c in range(NCORES)], axis=1)
    return (y_p, y_s, kv_p, pe_p, hg_p, s5re_p, s5im_p,
            kv_s, pe_s, hg_s, s5re_s, s5im_s)
```
